# Optimizing a Trainium2 kernel written in Bass

```python
import math
import jax
import jax.numpy as jnp
from jax import lax
import numpy as np

D_MODEL = 1024
BATCH = 8
SEQ = 4096
DEPTH = 2

GRID_W = 64
CTX_LEN = 256
HEAD_DIM = 64
MIX_HEADS = D_MODEL // HEAD_DIM
A_Q_HEADS = MIX_HEADS // 2
A_KV_HEADS = A_Q_HEADS // 4
B_Q_HEADS = MIX_HEADS - A_Q_HEADS
B_KV_HEADS = B_Q_HEADS // 4
WINDOW = 128
BLOCK = 128
NA_ROWS = 8
NA_COLS = 16
NA_QCOLS = 16
NA_KCOLS = NA_QCOLS + NA_COLS
C_HEADS = D_MODEL // (2 * HEAD_DIM)
C_V_DIM = 2 * HEAD_DIM
D_FF = 4 * D_MODEL
ROPE_THETA = 10000.0
EPS = 1e-6

A_QW = A_Q_HEADS * HEAD_DIM
A_KVW = A_KV_HEADS * HEAD_DIM
B_QW = B_Q_HEADS * HEAD_DIM
B_KVW = B_KV_HEADS * HEAD_DIM
EVEN_WIDTHS = (A_QW, A_KVW, A_KVW, B_QW, B_KVW, B_KVW)
EVEN_IN = A_QW + 2 * A_KVW + B_QW + 2 * B_KVW
EVEN_OUT = A_QW + B_QW
C_QKW = C_HEADS * 2 * HEAD_DIM
C_VW = C_HEADS * C_V_DIM
ODD_IN = 2 * C_QKW + C_VW

kernel_name = 'hybrid_dit_window_natten_diffattn'


def _rms_norm(x, g):
    xf = x.astype(jnp.float32)
    y = xf * lax.rsqrt(jnp.mean(xf * xf, axis=-1, keepdims=True) + EPS)
    return (y * g.astype(jnp.float32)).astype(x.dtype)


def _modulate(x, g, shift, scale):
    return _rms_norm(x, g) * (1 + scale) + shift


def _rope_tables(n):
    t = jnp.arange(n, dtype=jnp.int32)
    row = (t // GRID_W).astype(jnp.float32)
    col = (t % GRID_W).astype(jnp.float32)
    quarter = HEAD_DIM // 4
    inv_freq = ROPE_THETA ** (-jnp.arange(quarter, dtype=jnp.float32) / quarter)
    ar = row[:, None] * inv_freq[None, :]
    ac = col[:, None] * inv_freq[None, :]
    ang = jnp.concatenate([ar, ar, ac, ac], axis=-1)
    return jnp.cos(ang), jnp.sin(ang)


def _apply_rope(x, cos, sin):
    xs = x.reshape(x.shape[:-1] + (2, 2, HEAD_DIM // 4))
    rot = jnp.stack([-xs[..., 1, :], xs[..., 0, :]], axis=-2).reshape(x.shape)
    return x * cos[:, None, :].astype(x.dtype) + rot * sin[:, None, :].astype(x.dtype)


def _ctx_attn(q, k, v, sink):
    b, n, hq, d = q.shape
    hkv = k.shape[2]
    g = hq // hkv
    qg = q.reshape(b, n, hkv, g, d)
    s = jnp.einsum('bqhgd,bkhd->bhgqk', qg, k).astype(jnp.float32) * d ** -0.5
    if sink is not None:
        sk = jnp.broadcast_to(sink.astype(jnp.float32).reshape(1, hkv, g, 1, 1), s.shape[:-1] + (1,))
        s = jnp.concatenate([s, sk], axis=-1)
    p = jax.nn.softmax(s, axis=-1)[..., :n].astype(v.dtype)
    return jnp.einsum('bhgqk,bkhd->bqhgd', p, v).reshape(b, n, hq * d)


def _window_attn(q, k, v, kc, vc, sink):
    b, n, hq, d = q.shape
    hkv = k.shape[2]
    g = hq // hkv
    nb = n // BLOCK
    qb = q.reshape(b, nb, BLOCK, hkv, g, d)
    pad = ((0, 0), (BLOCK, BLOCK), (0, 0), (0, 0))

    def band(t):
        tp = jnp.pad(t, pad).reshape(b, nb + 2, BLOCK, hkv, d)
        return jnp.concatenate([tp[:, :-2], tp[:, 1:-1], tp[:, 2:]], axis=2)

    kb, vb = band(k), band(v)
    scale = d ** -0.5
    s_lat = jnp.einsum('bnqhgd,bnkhd->bhgnqk', qb, kb).astype(jnp.float32) * scale
    blk = jnp.arange(nb)[:, None, None] * BLOCK
    qpos = blk + jnp.arange(BLOCK)[None, :, None]
    kpos = blk - BLOCK + jnp.arange(3 * BLOCK)[None, None, :]
    valid = (jnp.abs(qpos - kpos) <= WINDOW) & (kpos >= 0) & (kpos < n)
    s_lat = jnp.where(valid, s_lat, -jnp.inf)
    s_ctx = jnp.einsum('bnqhgd,bchd->bhgnqc', qb, kc).astype(jnp.float32) * scale
    s_sink = jnp.broadcast_to(sink.astype(jnp.float32).reshape(1, hkv, g, 1, 1, 1), s_ctx.shape[:-1] + (1,))
    p = jax.nn.softmax(jnp.concatenate([s_lat, s_ctx, s_sink], axis=-1), axis=-1).astype(v.dtype)
    nk, nc = 3 * BLOCK, kc.shape[1]
    out = (jnp.einsum('bhgnqk,bnkhd->bnqhgd', p[..., :nk], vb)
           + jnp.einsum('bhgnqc,bchd->bnqhgd', p[..., nk:nk + nc], vc))
    return out.reshape(b, n, hq * d)


def _neighbourhood_attn(q, k, v, kc, vc, rpb):
    b, n, hq, d = q.shape
    hkv = k.shape[2]
    g = hq // hkv
    rows = n // GRID_W
    kr = min(NA_ROWS, rows)
    ncb = GRID_W // NA_QCOLS
    nkey = kr * NA_KCOLS
    r = jnp.arange(rows)
    row_idx = jnp.clip(r - kr // 2, 0, rows - kr)[:, None] + jnp.arange(kr)[None, :]
    m = jnp.arange(ncb)
    col_idx = jnp.clip(m * NA_QCOLS - NA_COLS // 2, 0, GRID_W - NA_KCOLS)[:, None] + jnp.arange(NA_KCOLS)[None, :]
    qcol = m[:, None] * NA_QCOLS + jnp.arange(NA_QCOLS)[None, :]
    cstart = jnp.clip(qcol - NA_COLS // 2, 0, GRID_W - NA_COLS)
    kcol = col_idx[:, None, :]
    col_ok = (kcol >= cstart[..., None]) & (kcol < cstart[..., None] + NA_COLS)
    mask = jnp.broadcast_to(col_ok[:, :, None, :], (ncb, NA_QCOLS, kr, NA_KCOLS)).reshape(ncb, NA_QCOLS, nkey)

    def gather(t):
        tg = t.reshape(b, rows, GRID_W, hkv, d)
        tb = tg[:, row_idx[:, None, :, None], col_idx[None, :, None, :]]
        return tb.reshape(b, rows, ncb, nkey, hkv, d)

    kb, vb = gather(k), gather(v)
    roff = row_idx - r[:, None] + (NA_ROWS - 1)
    coff = jnp.clip(kcol - qcol[..., None], 1 - NA_COLS, NA_COLS - 1) + (NA_COLS - 1)
    bias = rpb[:, roff[:, :, None, None, None], coff[None, None]]
    bias = jnp.transpose(bias, (0, 1, 3, 4, 2, 5)).reshape(hkv, g, rows, ncb, NA_QCOLS, nkey).astype(jnp.float32)
    qb = q.reshape(b, rows, ncb, NA_QCOLS, hkv, g, d)
    scale = d ** -0.5
    s_nb = jnp.einsum('brmqhgd,brmkhd->bhgrmqk', qb, kb).astype(jnp.float32) * scale + bias
    s_nb = jnp.where(mask, s_nb, -jnp.inf)
    s_ctx = jnp.einsum('brmqhgd,bchd->bhgrmqc', qb, kc).astype(jnp.float32) * scale
    p = jax.nn.softmax(jnp.concatenate([s_nb, s_ctx], axis=-1), axis=-1).astype(v.dtype)
    out = (jnp.einsum('bhgrmqk,brmkhd->brmqhgd', p[..., :nkey], vb)
           + jnp.einsum('bhgrmqc,bchd->brmqhgd', p[..., nkey:], vc))
    return out.reshape(b, n, hq * d)


def _split_even(p):
    b, n, _ = p.shape
    idx = np.cumsum(EVEN_WIDTHS)[:-1].tolist()
    parts = jnp.split(p, idx, axis=-1)
    return [t.reshape(b, n, -1, HEAD_DIM) for t in parts]


def _even_mixer(hx, hc, w_in, w_out, sink, rpb, cos, sin, with_ctx):
    aq, ak, av, bq, bk, bv = _split_even(hx @ w_in)
    caq, cak, cav, cbq, cbk, cbv = _split_even(hc @ w_in)
    aq = _apply_rope(aq, cos, sin)
    ak = _apply_rope(ak, cos, sin)
    ya = _window_attn(aq, ak, av, cak, cav, sink)
    yb = _neighbourhood_attn(bq, bk, bv, cbk, cbv, rpb)
    y_x = jnp.concatenate([ya, yb], axis=-1) @ w_out
    if not with_ctx:
        return y_x, None
    y_c = jnp.concatenate([_ctx_attn(caq, cak, cav, sink), _ctx_attn(cbq, cbk, cbv, None)], axis=-1) @ w_out
    return y_x, y_c


def _split_odd(p):
    b, n, _ = p.shape
    q, k, v = jnp.split(p, [C_QKW, 2 * C_QKW], axis=-1)
    return (q.reshape(b, n, C_HEADS, 2, HEAD_DIM), k.reshape(b, n, C_HEADS, 2, HEAD_DIM),
            v.reshape(b, n, C_HEADS, C_V_DIM))


def _diff_core(q, k, v, lam, sub_g, lam_init):
    b, nq, h, _, d = q.shape
    s = jnp.einsum('bqhmd,bkhmd->bhmqk', q, k).astype(jnp.float32) * d ** -0.5
    p = jax.nn.softmax(s, axis=-1)
    a = (p[:, :, 0] - lam * p[:, :, 1]).astype(v.dtype)
    o = jnp.einsum('bhqk,bkhe->bqhe', a, v)
    o = _rms_norm(o, sub_g) * (1.0 - lam_init)
    return o.reshape(b, nq, h * v.shape[-1])


def _odd_mixer(hx, hc, w_in, w_out, lq1, lk1, lq2, lk2, sub_g, lam_init, cos, sin, with_ctx):
    q, k, v = _split_odd(hx @ w_in)
    cq, ck, cv = _split_odd(hc @ w_in)
    b, n = q.shape[:2]
    q = _apply_rope(q.reshape(b, n, 2 * C_HEADS, HEAD_DIM), cos, sin).reshape(q.shape)
    k = _apply_rope(k.reshape(b, n, 2 * C_HEADS, HEAD_DIM), cos, sin).reshape(k.shape)
    f32 = jnp.float32
    lam = (jnp.exp(jnp.sum(lq1.astype(f32) * lk1.astype(f32)))
           - jnp.exp(jnp.sum(lq2.astype(f32) * lk2.astype(f32))) + lam_init)
    kf = jnp.concatenate([k, ck], axis=1)
    vf = jnp.concatenate([v, cv], axis=1)
    nb = n // BLOCK
    qb = jnp.swapaxes(q.reshape(b, nb, BLOCK, C_HEADS, 2, HEAD_DIM), 0, 1)
    yx = lax.map(lambda qblk: _diff_core(qblk, kf, vf, lam, sub_g, lam_init), qb)
    yx = jnp.swapaxes(yx, 0, 1).reshape(b, n, C_VW) @ w_out
    if not with_ctx:
        return yx, None
    yc = _diff_core(cq, ck, cv, lam, sub_g, lam_init) @ w_out
    return yx, yc


def _mlp(h, w1, w2):
    a = jax.nn.relu(h @ w1)
    return (a * a) @ w2


def setup_inputs(seed: int = 0) -> dict:
    key = jax.random.key(seed)
    ks = jax.random.split(key, 22)
    n_even = (DEPTH + 1) // 2
    n_odd = DEPTH // 2

    def nrm(k, shape, s):
        return jax.random.normal(k, shape, jnp.float32) * s

    return {
        'x': nrm(ks[0], (BATCH, SEQ, D_MODEL), 1.0),
        'c': nrm(ks[1], (BATCH, D_MODEL), 1.0),
        'ctx': nrm(ks[2], (BATCH, CTX_LEN, D_MODEL), 1.0),
        'c_ctx': nrm(ks[3], (D_MODEL,), 1.0),
        'ada_w': nrm(ks[4], (DEPTH, D_MODEL, 6 * D_MODEL), 0.5 * D_MODEL ** -0.5),
        'ada_b': nrm(ks[5], (DEPTH, 6 * D_MODEL), 0.01),
        'norm1_g': 1.0 + nrm(ks[6], (DEPTH, D_MODEL), 0.05),
        'norm2_g': 1.0 + nrm(ks[7], (DEPTH, D_MODEL), 0.05),
        'even_w_in': nrm(ks[8], (n_even, D_MODEL, EVEN_IN), D_MODEL ** -0.5),
        'even_w_out': nrm(ks[9], (n_even, EVEN_OUT, D_MODEL), EVEN_OUT ** -0.5),
        'a_sink': nrm(ks[10], (n_even, A_Q_HEADS), 0.5),
        'b_rpb': nrm(ks[11], (n_even, B_Q_HEADS, 2 * NA_ROWS - 1, 2 * NA_COLS - 1), 0.1),
        'odd_w_in': nrm(ks[12], (n_odd, D_MODEL, ODD_IN), D_MODEL ** -0.5),
        'odd_w_out': nrm(ks[13], (n_odd, C_VW, D_MODEL), C_VW ** -0.5),
        'lam_q1': nrm(ks[14], (n_odd, HEAD_DIM), 0.1),
        'lam_k1': nrm(ks[15], (n_odd, HEAD_DIM), 0.1),
        'lam_q2': nrm(ks[16], (n_odd, HEAD_DIM), 0.1),
        'lam_k2': nrm(ks[17], (n_odd, HEAD_DIM), 0.1),
        'subln_g': 1.0 + nrm(ks[18], (n_odd, C_V_DIM), 0.05),
        'mlp_w1': nrm(ks[19], (DEPTH, D_MODEL, D_FF), D_MODEL ** -0.5),
        'mlp_w2': nrm(ks[20], (DEPTH, D_FF, D_MODEL), D_FF ** -0.5),
        'final_g': 1.0 + nrm(ks[21], (D_MODEL,), 0.05),
    }


def reference(x, c, ctx, c_ctx, ada_w, ada_b, norm1_g, norm2_g, even_w_in, even_w_out, a_sink, b_rpb,
              odd_w_in, odd_w_out, lam_q1, lam_k1, lam_q2, lam_k2, subln_g, mlp_w1, mlp_w2, final_g):
    n = x.shape[1]
    cos, sin = _rope_tables(n)
    hx, hc = x, ctx
    for i in range(DEPTH):
        with_ctx = i < DEPTH - 1
        j = i // 2
        mx = (jax.nn.silu(c) @ ada_w[i] + ada_b[i])[:, None, :]
        mc = jax.nn.silu(c_ctx) @ ada_w[i] + ada_b[i]
        sx1, cx1, gx1, sx2, cx2, gx2 = jnp.split(mx, 6, axis=-1)
        sc1, cc1, gc1, sc2, cc2, gc2 = jnp.split(mc, 6, axis=-1)
        ax = _modulate(hx, norm1_g[i], sx1, cx1)
        ac = _modulate(hc, norm1_g[i], sc1, cc1)
        if i % 2 == 0:
            yx, yc = _even_mixer(ax, ac, even_w_in[j], even_w_out[j], a_sink[j], b_rpb[j], cos, sin, with_ctx)
        else:
            lam_init = 0.8 - 0.6 * math.exp(-0.3 * i)
            yx, yc = _odd_mixer(ax, ac, odd_w_in[j], odd_w_out[j], lam_q1[j], lam_k1[j], lam_q2[j], lam_k2[j],
                                subln_g[j], lam_init, cos, sin, with_ctx)
        hx = hx + gx1 * yx
        hx = hx + gx2 * _mlp(_modulate(hx, norm2_g[i], sx2, cx2), mlp_w1[i], mlp_w2[i])
        if with_ctx:
            hc = hc + gc1 * yc
            hc = hc + gc2 * _mlp(_modulate(hc, norm2_g[i], sc2, cc2), mlp_w1[i], mlp_w2[i])
    return _rms_norm(hx, final_g)
```

```python
import math
from contextlib import ExitStack
import numpy as np
import concourse.bass as bass
import concourse.mybir as mybir
from concourse.bass_utils import run_bass_kernel_spmd

F32 = mybir.dt.float32
BF16 = mybir.dt.bfloat16
AF = mybir.ActivationFunctionType
ALU = mybir.AluOpType

D = 1024
NL = 4096
NCX = 256
NT = NL + NCX
KC = 8
EPS = 1e-6
NEG = -30000.0
LAM_INIT = 0.8 - 0.6 * math.exp(-0.3 * 1)
TILES = [(t * 512, 512, 0) for t in range(8)] + [(NL, NCX, 1)]
import os as _os
_SKIP = set(_os.environ.get("KSKIP", "").split(","))

ENGS = ("pe", "act", "dve", "pool", "sp")
SAME_ENGINE_SYNC = {"pe": False, "act": True, "dve": True, "pool": True, "sp": False}


class Prog:
    def __init__(self, nc, sems):
        self.nc = nc
        self.esem = {e: sems[i] for i, e in enumerate(("pe", "act", "dve", "pool"))}
        self.dsem_pool = list(sems[4:])
        self.dsem = {}
        self.count = {}
        self.semobj = {}
        for e, s in self.esem.items():
            self.count[id(s)] = 0
            self.semobj[id(s)] = s
        self.ops = {e: [] for e in ENGS}
        self.seen = {e: {} for e in ENGS}
        self.last_w = {}
        self.readers = {}
        self.n_ops = 0

    def dma_sem(self, name):
        if name not in self.dsem:
            s = self.dsem_pool.pop()
            self.dsem[name] = s
            self.count[id(s)] = 0
            self.semobj[id(s)] = s
        return self.dsem[name]

    def _need(self, eng, ev, waits):
        if ev is None:
            return
        sid, val = ev
        if eng in self.esem and sid == id(self.esem[eng]) and not SAME_ENGINE_SYNC[eng]:
            return
        if self.seen[eng].get(sid, 0) >= val:
            return
        if waits.get(sid, 0) < val:
            waits[sid] = val

    def _deps(self, eng, reads, writes):
        waits = {}
        for k in reads:
            self._need(eng, self.last_w.get(k), waits)
        for k in writes:
            self._need(eng, self.last_w.get(k), waits)
            for ev in self.readers.get(k, ()):
                self._need(eng, ev, waits)
        for sid, val in waits.items():
            self.seen[eng][sid] = val
        return [(self.semobj[sid], val) for sid, val in waits.items()]

    def _commit(self, ev, reads, writes):
        for k in writes:
            self.last_w[k] = ev
            self.readers[k] = []
        for k in reads:
            lst = self.readers.setdefault(k, [])
            for i, (sid, val) in enumerate(lst):
                if sid == ev[0]:
                    lst[i] = (sid, max(val, ev[1]))
                    break
            else:
                lst.append(ev)

    def op(self, eng, fn, reads=(), writes=()):
        waits = self._deps(eng, reads, writes)
        s = self.esem[eng]
        self.count[id(s)] += 1
        ev = (id(s), self.count[id(s)])
        self.ops[eng].append((waits, fn, (s, 1)))
        self._commit(ev, reads, writes)
        self.n_ops += 1

    def dma(self, q, semname, out, in_, reads=(), writes=()):
        s = self.dma_sem(semname)
        waits = self._deps(q, reads, writes)
        if self.count[id(s)] > 0:
            w = {}
            self._need(q, (id(s), self.count[id(s)]), w)
            for sid, val in w.items():
                self.seen[q][sid] = val
                waits.append((self.semobj[sid], val))
        self.count[id(s)] += 16
        ev = (id(s), self.count[id(s)])

        kw = dict(max_dma_last_dim=4096) if q == "pool" else {}

        def fn(e, out=out, in_=in_, kw=kw):
            return e.dma_start(out=out, in_=in_, **kw)
        self.ops[q].append((waits, fn, (s, 16)))
        self._commit(ev, reads, writes)
        self.n_ops += 1

    def barrier(self):
        for e in ENGS:
            waits = []
            for sid, val in self.count.items():
                if val > 0 and self.seen[e].get(sid, 0) < val:
                    if e in self.esem and sid == id(self.esem[e]):
                        continue
                    waits.append((self.semobj[sid], val))
                    self.seen[e][sid] = val
            if waits:
                self.ops[e].append((waits, None, None))

    def emit(self, block):
        prog = self

        def mk(ename):
            oplist = prog.ops[ename]

            def body(eng):
                for waits, fn, inc in oplist:
                    for s, v in waits:
                        eng.wait_ge(s, v)
                    if fn is not None:
                        fn(eng).then_inc(inc[0], inc[1])
            return body
        block.tensor(mk("pe"))
        block.scalar(mk("act"))
        block.vector(mk("dve"))
        block.gpsimd(mk("pool"))
        block.sync(mk("sp"))
        self.ops = {e: [] for e in ENGS}


class B:
    def __init__(self, nc, P):
        self.nc = nc
        self.P = P

    def mm(self, out, lhsT, rhs, start, stop, reads, writes, skip=False):
        self.P.op("pe", lambda e: e.matmul(out, lhsT=lhsT, rhs=rhs, start=start, stop=stop, skip_group_check=skip),
                  reads, writes)

    def tr(self, out, in_, ident, reads, writes):
        self.P.op("pe", lambda e: e.transpose(out=out, in_=in_, identity=ident), reads, writes)

    def act(self, out, in_, func, reads, writes, scale=1.0, bias=0.0):
        self.P.op("act", lambda e: e.activation(out=out, in_=in_, func=func, bias=bias, scale=scale), reads, writes)

    def tt(self, eng, out, in0, in1, op, reads, writes):
        self.P.op(eng, lambda e: e.tensor_tensor(out=out, in0=in0, in1=in1, op=op), reads, writes)

    def ts(self, eng, out, in0, s1, s2, op0, op1, reads, writes):
        if s2 is None:
            self.P.op(eng, lambda e: e.tensor_scalar(out=out, in0=in0, scalar1=s1, scalar2=None, op0=op0), reads, writes)
        else:
            self.P.op(eng, lambda e: e.tensor_scalar(out=out, in0=in0, scalar1=s1, scalar2=s2, op0=op0, op1=op1),
                      reads, writes)

    def stt(self, out, in0, scalar, in1, op0, op1, reads, writes, accum_out=None):
        if accum_out is None:
            self.P.op("dve", lambda e: e.scalar_tensor_tensor(out=out, in0=in0, scalar=scalar, in1=in1, op0=op0, op1=op1),
                      reads, writes)
        else:
            self.P.op("dve", lambda e: e.scalar_tensor_tensor(out=out, in0=in0, scalar=scalar, in1=in1, op0=op0, op1=op1,
                                                             accum_out=accum_out), reads, writes)

    def copy(self, eng, out, in_, reads, writes):
        if eng == "act":
            self.act(out, in_, AF.Copy, reads, writes)
        else:
            self.P.op(eng, lambda e: e.tensor_copy(out=out, in_=in_), reads, writes)

    def memset(self, eng, ap, val, writes):
        self.P.op(eng, lambda e: e.memset(ap, val), (), writes)

    def recip(self, out, in_, reads, writes):
        self.P.op("dve", lambda e: e.reciprocal(out=out, in_=in_), reads, writes)


def build_program(n_bias, a_keys, b_keys, dbg=False, max_phase=99):
    nc = bass.Bass("TRN2", target_bir_lowering=False)

    def din(name, shape, dt=F32):
        return nc.dram_tensor(name, list(shape), dt, kind="ExternalInput").ap()

    xT = din("xT", [D, NL])
    cT = din("cT", [D, NCX])
    cvec = din("cvec", [128, 16])
    ada_w = din("ada_w", [2, D, 6 * D])
    ada_b = din("ada_b", [128, 192])
    g12 = din("g12", [128, 32])
    gfin = din("gfin", [128, 8])
    w_in0 = din("w_in0", [D, 1536])
    w_out0 = din("w_out0", [D, D])
    w_in1 = din("w_in1", [D, 3072])
    w_out1 = din("w_out1", [D, D])
    w1 = din("w1", [2, D, 4 * D])
    w2 = din("w2", [2, 4 * D, D])
    sinkb = din("sinkb", [128, 8])
    lamv = din("lamv", [128, 256])
    subg = din("subg", [128, 128])
    consts = din("consts", [128, 256])
    rope = din("rope", [4, 128, NL])
    biasT = din("biasT", [128, n_bias * 512])
    outT = nc.dram_tensor("outT", [D, NL], F32, kind="ExternalOutput").ap()
    skind = "ExternalOutput" if dbg else "Internal"
    hS = nc.dram_tensor("hS", [D, NT], F32, kind=skind).ap()
    qT_d = nc.dram_tensor("qT_d", [D, NT], BF16, kind=skind).ap()
    kT_d = nc.dram_tensor("kT_d", [D, NT], BF16, kind=skind).ap()
    v_d = nc.dram_tensor("v_d", [NT, 8 * 129], BF16, kind=skind).ap()
    w1b = nc.dram_tensor("w1b", [2, D, 4 * D], BF16).ap()
    w2b = nc.dram_tensor("w2b", [2, 4 * D, D], BF16).ap()
    dbg_out = {}
    if dbg:
        dbg_out["mod"] = nc.dram_tensor("dbg_mod", [128, 192], F32, kind="ExternalOutput").ap()

    def fm(ap):
        return ap.rearrange("(k p) t -> p k t", p=128)

    with ExitStack() as es:
        sems = [es.enter_context(nc.semaphore(f"s{i}")) for i in range(56)]
        P = Prog(nc, sems)
        b = B(nc, P)

        uid = [0]

        def sb(stack, name, shape, dt=F32):
            uid[0] += 1
            return stack.enter_context(nc.sbuf_tensor(f"{name}_{uid[0]}", list(shape), dt))

        def psb(stack, name, shape=(128, 512), dt=F32):
            uid[0] += 1
            return stack.enter_context(nc.psum_tensor(f"{name}_{uid[0]}", list(shape), dt))

        mod = sb(es, "mod", [128, 192])
        gm = sb(es, "gm", [128, 64])
        g12s = sb(es, "g12s", [128, 32])
        gfs = sb(es, "gfs", [128, 8])
        ident = sb(es, "ident", [128, 128], BF16)
        perm = sb(es, "perm", [128, 128], BF16)
        ones = sb(es, "ones", [128, 128], BF16)
        nhalf = sb(es, "nhalf", [128, 1])
        expsink = sb(es, "expsink", [128, 8])
        neglam = sb(es, "neglam", [128, 1])
        subg2 = sb(es, "subg2", [128, 128])
        epsT = sb(es, "epsT", [128, 1])

        mod4 = mod[:, :].rearrange("p (l j s) -> p l j s", l=2, j=48, s=2)
        gm5 = gm[:, :].rearrange("p (l w k s) -> p l w k s", l=2, w=2, k=8, s=2)
        g124 = g12s[:, :].rearrange("p (l w k) -> p l w k", l=2, w=2, k=8)

        def modv(l, idx, k, s):
            return mod4[:, l, idx * 8 + k, s:s + 1]

        def gmv(l, which, k, s):
            return gm5[:, l, which, k, s:s + 1]

        with ExitStack() as ps:
            cv = sb(ps, "cv", [128, 16])
            s_bf = sb(ps, "s_bf", [128, 16], BF16)
            ab = sb(ps, "ab", [128, 192])
            lv = sb(ps, "lv", [128, 256])
            lt = sb(ps, "lt", [128, 128])
            ls = sb(ps, "ls", [128, 4])
            sk = sb(ps, "sk", [128, 8])
            sg = sb(ps, "sg", [128, 128])
            wbuf = [sb(ps, f"wbuf{i}", [128, 8, 3072], BF16) for i in range(2)]
            pm = psb(ps, "pm")
            block = ps.enter_context(nc.Block())

            P.dma("sp", "c0", cv[:, :], cvec, writes=["cv"])
            P.dma("sp", "c1", ab[:, :], ada_b, writes=["ab"])
            P.dma("sp", "c2", g12s[:, :], g12, writes=["g12s"])
            P.dma("sp", "c3", gfs[:, :], gfin, writes=["gfs"])
            P.dma("sp", "c0", sk[:, :], sinkb, writes=["sk"])
            P.dma("sp", "c1", lv[:, :], lamv, writes=["lv"])
            P.dma("sp", "c2", sg[:, :], subg, writes=["sg"])
            P.dma("pool", "c4", ident[:, :], consts[:, 0:128], writes=["ident"])
            P.dma("pool", "c5", perm[:, :], consts[:, 128:256], writes=["perm"])
            b.memset("dve", ones[:, :], 1.0, ["ones"])
            b.memset("dve", nhalf[:, :], -0.5, ["nhalf"])
            b.memset("dve", epsT[:, :], EPS, ["epsT"])
            b.act(s_bf[:, :], cv[:, :], AF.Silu, ["cv"], ["s_bf"])
            b.act(expsink[:, :], sk[:, :], AF.Exp, ["sk"], ["expsink"])
            s3 = s_bf[:, :].rearrange("p (k s) -> p k s", s=2)
            li = 0
            for l in range(2):
                for half in range(2):
                    wb = wbuf[li % 2]
                    wk = f"wbuf{li % 2}"
                    for k in range(KC):
                        P.dma("pool", f"w{k % 16}", wb[:, k, :],
                              ada_w[l, k * 128:(k + 1) * 128, half * 3072:(half + 1) * 3072], writes=[(wk, k)])
                    for jj in range(24):
                        col = (l * 48 + half * 24 + jj) * 2
                        for k in range(KC):
                            b.mm(pm[:, col:col + 2], wb[:, k, jj * 128:(jj + 1) * 128], s3[:, k, :],
                                 k == 0, k == KC - 1, [(wk, k), "s_bf"], ["pm"])
                    li += 1
            b.tt("dve", mod[:, :], pm[:, 0:192], ab[:, :], ALU.add, ["pm", "ab"], ["mod"])
            for l in range(2):
                for w in range(2):
                    for s in range(2):
                        sc = mod4[:, l, (1 + 3 * w) * 8:(2 + 3 * w) * 8, s]
                        b.stt(gm5[:, l, w, :, s], sc, 1.0, g124[:, l, w, :], ALU.add, ALU.mult,
                              ["mod", "g12s"], ["gm"])
            lv3 = lv[:, :].rearrange("p (a d) -> p a d", d=64)
            for i in range(2):
                b.stt(lt[:, 0:64], lv3[:, 2 * i, :], 1.0, lv3[:, 2 * i + 1, :], ALU.mult, ALU.mult,
                      ["lv"], ["lt", "ls"], accum_out=ls[:, i:i + 1])
            b.act(ls[:, 2:4], ls[:, 0:2], AF.Exp, ["ls"], ["ls"])
            b.tt("dve", neglam[:, :], ls[:, 3:4], ls[:, 2:3], ALU.subtract, ["ls"], ["neglam"])
            b.ts("dve", neglam[:, :], neglam[:, :], -LAM_INIT, None, ALU.add, None, ["neglam"], ["neglam"])
            b.ts("dve", subg2[:, :], sg[:, :], 1.0 - LAM_INIT, None, ALU.mult, None, ["sg"], ["subg2"])
            if dbg:
                P.dma("sp", "dbg", dbg_out["mod"], mod[:, :], reads=["mod"], writes=["dbg_mod"])
            P.barrier()
            P.emit(block)

        def normmod(xt, xk, a_out, ak, TS, l, which, s, sqb, ssb, rtmp, rstd):
            for k in range(KC):
                q = sqb[k % 2]
                b.act(q[:, 0:TS], xt[:, k, 0:TS], AF.Square, [xk], [f"sq{k % 2}"])
                b.mm(ssb[:, 0:TS], ones[:, :], q[:, 0:TS], k == 0, k == KC - 1, [f"sq{k % 2}", "ones"], ["ssb"])
            b.act(rtmp[:, 0:TS], ssb[:, 0:TS], AF.Sqrt, ["ssb", "epsT"], ["rtmp"], scale=1.0 / D, bias=epsT[:, 0:1])
            b.recip(rstd[:, 0:TS], rtmp[:, 0:TS], ["rtmp"], ["rstd"])
            for k in range(KC):
                t = sqb[2 + k % 2]
                b.stt(t[:, 0:TS], xt[:, k, 0:TS], gmv(l, which, k, s), rstd[:, 0:TS], ALU.mult, ALU.mult,
                      [xk, "rstd", "gm"], [f"nt{k % 2}"])
                b.act(a_out[:, k, 0:TS], t[:, 0:TS], AF.Identity, [f"nt{k % 2}", "mod"], [(ak, k)],
                      bias=modv(l, 3 * which, k, s))

        def phase_p1(l, W, NC_, fm_chunks, v_col0, nh, dv, src_fn):
            with ExitStack() as ps:
                Wb = sb(ps, "p1W", [128, 8, NC_], BF16)
                xts = [sb(ps, f"p1x{i}", [128, 8, 512]) for i in range(2)]
                a_ts = [sb(ps, f"p1a{i}", [128, 8, 512], BF16) for i in range(2)]
                sqb = [sb(ps, f"p1sq{i}", [128, 512], BF16) for i in range(2)] + \
                      [sb(ps, f"p1nt{i}", [128, 512]) for i in range(2)]
                rtmp = sb(ps, "p1rtmp", [128, 512])
                rstd = sb(ps, "p1rstd", [128, 512])
                nq = sum(1 for c in fm_chunks if c[1] == "q")
                nk = sum(1 for c in fm_chunks if c[1] == "k")
                qst = sb(ps, "p1qst", [128, nq, 512], BF16)
                kst = sb(ps, "p1kst", [128, nk, 512], BF16)
                VW = nh * (dv + 1)
                vst = sb(ps, "p1vst", [128, 4, VW], BF16)
                ntab = 4 if l == 0 else 2
                tabs = [sb(ps, f"p1tab{i}", [128, ntab, 512]) for i in range(2)]
                q_sb = [sb(ps, f"p1qsb{i}", [128, 512], BF16) for i in range(2)]
                t1 = [sb(ps, f"p1t1{i}", [128, 512]) for i in range(2)]
                t2 = [sb(ps, f"p1t2{i}", [128, 512]) for i in range(2)]
                ssb = psb(ps, "p1ss")
                qps = [psb(ps, f"p1qps{i}") for i in range(2)]
                pps = [psb(ps, f"p1pps{i}") for i in range(2)]
                vps = [psb(ps, f"p1vps{i}") for i in range(2)]
                block = ps.enter_context(nc.Block())

                for k in range(KC):
                    P.dma("pool", f"w{k % 16}", Wb[:, k, :], W[k * 128:(k + 1) * 128, :], writes=[("p1W", k)])
                b.memset("dve", vst[:, :, :], 1.0, ["vst"])
                vst4 = vst[:, :, :].rearrange("p b (h e) -> p b h e", e=dv + 1)
                ci = 0
                vi = 0
                def p1_load(ti):
                    t0, TS, s = TILES[ti]
                    P.dma("sp", f"x{ti % 2}", xts[ti % 2][:, :, 0:TS], src_fn(t0, TS, s), writes=[f"p1x{ti % 2}"])
                    if s == 0:
                        P.dma("sp", f"t{ti % 2}", tabs[ti % 2][:, :, 0:TS],
                              rope[0:ntab, :, t0:t0 + TS].rearrange("a p t -> p a t"), writes=[f"p1tab{ti % 2}"])
                p1_load(0)
                for ti, (t0, TS, s) in enumerate(TILES):
                    xt = xts[ti % 2]
                    xk = f"p1x{ti % 2}"
                    tab = tabs[ti % 2]
                    tk = f"p1tab{ti % 2}"
                    if ti + 1 < len(TILES):
                        p1_load(ti + 1)
                    a_t = a_ts[ti % 2]
                    pa = f"p1a{ti % 2}"
                    normmod(xt, xk, a_t, pa, TS, l, 0, s, sqb, ssb, rtmp, rstd)
                    for (col0, kind, dch, rp, qs, ctx_needed) in fm_chunks:
                        if (s == 1 and not ctx_needed) or "fm" in _SKIP:
                            continue
                        if "rope" in _SKIP:
                            rp = False
                        qp = qps[ci % 2]
                        qk = f"p1qps{ci % 2}"
                        for k in range(KC):
                            b.mm(qp[:, 0:TS], Wb[:, k, col0:col0 + 128], a_t[:, k, 0:TS], k == 0, k == KC - 1,
                                 [("p1W", k), (pa, k)], [qk])
                        dst = (qst if kind == "q" else kst)[:, dch, 0:TS]
                        dk = ("p1st", kind, dch)
                        if rp and s == 0:
                            qsb = q_sb[ci % 2]
                            b.copy("act", qsb[:, 0:TS], qp[:, 0:TS], [qk], [f"qsb{ci % 2}"])
                            pp = pps[ci % 2]
                            b.mm(pp[:, 0:TS], perm[:, :], qsb[:, 0:TS], True, True, [f"qsb{ci % 2}", "perm"],
                                 [f"pps{ci % 2}"])
                            to = 2 if qs else 0
                            if "ropeD" not in _SKIP:
                                if "ropeE" not in _SKIP:
                                    b.tt("dve", t1[ci % 2][:, 0:TS], qsb[:, 0:TS], tab[:, to, 0:TS], ALU.mult,
                                         [f"qsb{ci % 2}", tk], [f"t1{ci % 2}"])
                                else:
                                    b.tt("dve", t1[ci % 2][:, 0:TS], qp[:, 0:TS], tab[:, to, 0:TS], ALU.mult, [qk, tk],
                                         [f"t1{ci % 2}"])
                            if "ropeC" not in _SKIP:
                                b.tt("dve", t2[ci % 2][:, 0:TS], pp[:, 0:TS], tab[:, to + 1, 0:TS], ALU.mult,
                                     [f"pps{ci % 2}", tk], [f"t2{ci % 2}"])
                            if "ropeA" in _SKIP:
                                b.copy("act", dst, t1[ci % 2][:, 0:TS], [f"t1{ci % 2}", f"t2{ci % 2}"], [dk])
                            elif "ropeB" in _SKIP:
                                b.tt("dve", dst, t1[ci % 2][:, 0:TS], t2[ci % 2][:, 0:TS], ALU.add,
                                     [f"t1{ci % 2}", f"t2{ci % 2}"], [dk])
                            else:
                                b.tt("pool", dst, t1[ci % 2][:, 0:TS], t2[ci % 2][:, 0:TS], ALU.add,
                                     [f"t1{ci % 2}", f"t2{ci % 2}"], [dk])
                        else:
                            b.act(dst, qp[:, 0:TS], AF.Copy, [qk], [dk], scale=(0.125 if qs else 1.0))
                        ci += 1
                    nb = TS // 128
                    for jb in range(0 if "v" in _SKIP else nb):
                        for hh in range((nh * dv) // 512 if nh * dv >= 512 else 1):
                            wcols = min(512, nh * dv)
                            vp = vps[vi % 2]
                            vk = f"p1vps{vi % 2}"
                            for k in range(KC):
                                b.mm(vp[:, 0:wcols], a_t[:, k, jb * 128:(jb + 1) * 128],
                                     Wb[:, k, v_col0 + hh * 512:v_col0 + hh * 512 + wcols], k == 0, k == KC - 1,
                                     [("p1W", k), (pa, k)], [vk])
                            hpc = wcols // dv
                            b.copy("dve" if vi % 2 else "act", vst4[:, jb, hh * hpc:(hh + 1) * hpc, 0:dv],
                                   vp[:, 0:wcols].rearrange("p (h d) -> p h d", d=dv), [vk], ["vst"])
                            vi += 1
                    nqs = sum(1 for c in fm_chunks if c[1] == "q" and (s == 0 or c[5]))
                    if "st" in _SKIP:
                        continue
                    if nqs:
                        P.dma("sp", "stq", fm(qT_d)[:, 0:nq, t0:t0 + TS], qst[:, :, 0:TS],
                              reads=[("p1st", "q", i) for i in range(nq)], writes=["qT_d"])
                    P.dma("sp", "stk", fm(kT_d)[:, 0:nk, t0:t0 + TS], kst[:, :, 0:TS],
                          reads=[("p1st", "k", i) for i in range(nk)], writes=["kT_d"])
                    P.dma("sp", "stv", v_d[t0:t0 + TS, 0:VW].rearrange("(b p) f -> p b f", p=128),
                          vst[:, 0:nb, :], reads=["vst"], writes=["v_d"])
                P.barrier()
                P.emit(block)

        def outproj_tile(l, Wo, OT_t, TS, s, t0, ybanks, ykeys, ht, hk):
            for dc in range(KC):
                yb = ybanks[dc % len(ybanks)]
                yk = ykeys[dc % len(ybanks)]
                for c in range(KC):
                    b.mm(yb[:, 0:TS], Wo[:, c, dc * 128:(dc + 1) * 128], OT_t[:, c, 0:TS], c == 0, c == KC - 1,
                         ["Wo", "OT_t"], [yk])
                b.stt(ht[:, dc, 0:TS], yb[:, 0:TS], modv(l, 2, dc, s), ht[:, dc, 0:TS], ALU.mult, ALU.add,
                      [yk, hk, (hk, dc), "mod"], [(hk, dc)])
            P.dma("sp", "hst", fm(hS)[:, :, t0:t0 + TS], ht[:, :, 0:TS], reads=[hk] + [(hk, dc) for dc in range(KC)],
                  writes=["hS_dst"])

        def src_x(t0, TS, s):
            return fm(xT)[:, :, t0:t0 + TS] if s == 0 else fm(cT)[:, :, 0:TS]

        def src_h(t0, TS, s):
            return fm(hS)[:, :, t0:t0 + TS]

        def precast(l):
            for k in range(KC):
                P.dma("pool", f"w{k % 16}", w1b[l, k * 128:(k + 1) * 128, :], w1[l, k * 128:(k + 1) * 128, :],
                      writes=[("w1b", l, k)])
            for k in range(32):
                P.dma("pool", f"w{k % 16}", w2b[l, k * 128:(k + 1) * 128, :], w2[l, k * 128:(k + 1) * 128, :],
                      writes=[("w2b", l, k)])

        def phase_l0_attn():
            with ExitStack() as ps:
                KT0 = sb(ps, "KT0", [128, 2, NT], BF16)
                V0 = sb(ps, "V0", [128, 34, 4 * 65], BF16)
                bias = sb(ps, "bias", [128, n_bias, 512], BF16)
                Wo = sb(ps, "Wo", [128, 8, D], BF16)
                Qt = [sb(ps, f"Qt{i}", [128, 8, 512], BF16) for i in range(2)]
                NPT = 3
                pT = [sb(ps, f"pT{i}", [128, 512], BF16) for i in range(NPT)]
                O_sb = [sb(ps, f"O_sb{i}", [128, D], BF16) for i in range(2)]
                OT_t = sb(ps, "OT_t", [128, 8, 512], BF16)
                zt = sb(ps, "zt", [128, 8])
                rz = sb(ps, "rz", [128, 8])
                hts = [sb(ps, f"hres{i}", [128, 8, 512]) for i in range(2)]
                spsb = [psb(ps, f"sps{i}") for i in range(NPT)]
                accb = [psb(ps, f"acc{i}") for i in range(2)]
                tpb = psb(ps, "tp0", [128, 1024], BF16)
                ypb = [psb(ps, f"ypb{i}") for i in range(2)]
                block = ps.enter_context(nc.Block())

                P.dma("sp", "kt", KT0[:, :, :], fm(kT_d)[:, 0:2, :], writes=["KT0"])
                P.dma("sp", "vv", V0[:, :, :], v_d[:, 0:260].rearrange("(b p) f -> p b f", p=128), writes=["V0"])
                nbh = (n_bias + 1) // 2
                P.dma("pool", "w0", bias[:, 0:nbh, :], biasT[:, 0:nbh * 512].rearrange("p (n f) -> p n f", f=512),
                      writes=["bias"])
                P.dma("pool", "w1", bias[:, nbh:n_bias, :],
                      biasT[:, nbh * 512:n_bias * 512].rearrange("p (n f) -> p n f", f=512), writes=["bias"])
                for k in range(KC):
                    P.dma("pool", f"w{k % 16}", Wo[:, k, :], w_out0[k * 128:(k + 1) * 128, :], writes=["Wo"])
                precast(0)
                V04 = V0[:, :, :].rearrange("p b (h e) -> p b h e", e=65)

                steps = []
                for ti, (t0, TS, s) in enumerate(TILES):
                    nqb = TS // 128
                    for qbl in range(nqb):
                        qb = t0 // 128 + qbl
                        for g in range(4):
                            typ, kv = g // 2, g % 2
                            if s == 1:
                                klist = [(32, None), (33, None)]
                            else:
                                klist = (a_keys if typ == 0 else b_keys)[qb] + [(32, None), (33, None)]
                            for idx, (kb, be) in enumerate(klist):
                                steps.append(dict(ti=ti, t0=t0, TS=TS, s=s, qbl=qbl, g=g, typ=typ, kv=kv, idx=idx, kb=kb,
                                                  be=be, last=(idx == len(klist) - 1), first_tile=(qbl == 0 and g == 0 and idx == 0),
                                                  last_q=(g == 3 and idx == len(klist) - 1),
                                                  last_tile=(qbl == nqb - 1 and g == 3 and idx == len(klist) - 1)))
                grp = 0
                qbi = 0
                for st_ in steps:
                    st_["ai"] = grp
                    st_["oi"] = qbi
                    if st_["last"]:
                        grp += 1
                    if st_["last_q"]:
                        qbi += 1

                def load_q(ti):
                    t0, TS, s = TILES[ti]
                    P.dma("sp", f"q{ti % 2}", Qt[ti % 2][:, :, 0:TS], fm(qT_d)[:, :, t0:t0 + TS], writes=[f"Qt{ti % 2}"])

                def load_h(ti):
                    t0, TS, s = TILES[ti]
                    P.dma("sp", f"x{ti % 2}", hts[ti % 2][:, :, 0:TS], src_x(t0, TS, s), writes=[f"hres{ti % 2}"])
                load_q(0)
                load_h(0)

                def emit_S(i):
                    st_ = steps[i]
                    ti, typ, kv, kb, be = st_["ti"], st_["typ"], st_["kv"], st_["kb"], st_["be"]
                    if st_["first_tile"] and ti + 1 < len(TILES):
                        load_q(ti + 1)
                    Q = Qt[ti % 2]
                    qk_ = f"Qt{ti % 2}"
                    hp = kv * 64
                    qoff = st_["qbl"] * 128
                    sp_ = spsb[i % NPT]
                    sk_ = f"sps{i % NPT}"
                    b.mm(sp_[:, :].rearrange("p (c q) -> p c q", c=4), KT0[hp:hp + 64, typ, kb * 128:(kb + 1) * 128],
                         Q[hp:hp + 64, typ * 4:typ * 4 + 4, qoff:qoff + 128], True, be is None, ["KT0", qk_], [sk_])
                    if be is not None:
                        e = be if typ == 0 else be + kv
                        b.mm(sp_[:, :], ident[:, :], bias[:, e, :], False, True, ["bias", "ident"], [sk_])
                    b.act(pT[i % NPT][:, :], sp_[:, :], AF.Exp, [sk_], [f"pT{i % NPT}"])

                tcount = [0]

                def emit_rest(i):
                    st_ = steps[i]
                    ti, typ, kv, kb, s, TS, t0 = st_["ti"], st_["typ"], st_["kv"], st_["kb"], st_["s"], st_["TS"], st_["t0"]
                    ai, oi = st_["ai"], st_["oi"]
                    if st_["first_tile"] and ti + 1 < len(TILES):
                        load_h(ti + 1)
                    acc = accb[ai % 2]
                    acck = f"acc{ai % 2}"
                    p_ = pT[i % NPT]
                    pk_ = f"pT{i % NPT}"
                    for j in range(4):
                        b.mm(acc[:, j * 65:(j + 1) * 65], p_[:, j * 128:(j + 1) * 128], V04[:, kb, typ * 2 + kv, :],
                             st_["idx"] == 0 and j == 0, st_["last"], [pk_, "V0"], [acck], skip=True)
                    if not st_["last"]:
                        return
                    Ob = O_sb[oi % 2]
                    ok_ = f"O_sb{oi % 2}"
                    acc3 = acc[:, 0:260].rearrange("p (j e) -> p j e", e=65)
                    zz = zt[:, (ai % 2) * 4:(ai % 2) * 4 + 4]
                    rr = rz[:, (ai % 2) * 4:(ai % 2) * 4 + 4]
                    zk, rk = f"zt{ai % 2}", f"rz{ai % 2}"
                    if typ == 0:
                        b.tt("dve", zz, acc3[:, :, 64], expsink[:, kv * 4:kv * 4 + 4], ALU.add, [acck, "expsink"], [zk])
                    else:
                        b.copy("dve", zz, acc3[:, :, 64], [acck], [zk])
                    b.recip(rr, zz, [zk], [rk])
                    base = typ * 512 + kv * 256
                    b.tt("dve", Ob[:, base:base + 256].rearrange("p (j d) -> p j d", d=64), acc3[:, :, 0:64],
                         rr.unsqueeze(2).broadcast_to([128, 4, 64]), ALU.mult, [acck, rk], [ok_])
                    if not st_["last_q"]:
                        return
                    qoff = st_["qbl"] * 128
                    for c in range(KC):
                        b.tr(tpb[:, c * 128:(c + 1) * 128], Ob[:, c * 128:(c + 1) * 128], ident[:, :], [ok_, "ident"],
                             ["tp0"])
                    b.copy("dve", OT_t[:, :, qoff:qoff + 128], tpb[:, :].rearrange("p (c q) -> p c q", c=8), ["tp0"],
                           ["OT_t"])
                    if st_["last_tile"]:
                        outproj_tile(0, Wo, OT_t, TS, s, t0, ypb, ["ypb0", "ypb1"], hts[ti % 2], f"hres{ti % 2}")

                DEPTH = NPT - 1
                n = len(steps)
                for i in range(n + DEPTH):
                    if i < n:
                        emit_S(i)
                    if i - DEPTH >= 0:
                        emit_rest(i - DEPTH)
                P.barrier()
                P.emit(block)

        def phase_l1_attn():
            with ExitStack() as ps:
                KT1 = sb(ps, "KT1", [128, 8, NT], BF16)
                V1 = sb(ps, "V1", [128, 34, 8 * 129], BF16)
                Wo = sb(ps, "Wo", [128, 8, D], BF16)
                Qc = [sb(ps, f"Qc{i}", [128, 512], BF16) for i in range(2)]
                pT2 = [sb(ps, f"pT{i}", [128, 1024], BF16) for i in range(2)]
                accS = [sb(ps, f"accS{i}", [128, 8 * 129]) for i in range(2)]
                O_sb = sb(ps, "O_sb", [128, 4, D], BF16)
                OT_t = sb(ps, "OT_t", [128, 8, 512], BF16)
                rz = sb(ps, "rz", [128, 8])
                tmp = [sb(ps, f"tmp{i}", [128, 128]) for i in range(2)]
                ob = [sb(ps, f"ob{i}", [128, 128]) for i in range(2)]
                junk = sb(ps, "junk", [128, 128])
                hres = sb(ps, "hres0", [128, 8, 512])
                sps2 = [psb(ps, f"sps{i}", (128, 1024)) for i in range(2)]
                accb = [psb(ps, f"acc{i}") for i in range(3)]
                tpb = psb(ps, "tp0", [128, 512], BF16)
                block = ps.enter_context(nc.Block())

                for c in range(KC):
                    P.dma("sp", f"kt{c % 2}", KT1[:, c, :], kT_d[c * 128:(c + 1) * 128, :], writes=["KT1"])
                v_r = v_d[:, :].rearrange("(b p) f -> p b f", p=128)
                for i in range(2):
                    P.dma("sp", f"vv{i}", V1[:, i * 17:(i + 1) * 17, :], v_r[:, i * 17:(i + 1) * 17, :], writes=["V1"])
                for k in range(KC):
                    P.dma("pool", f"w{k % 16}", Wo[:, k, :], w_out1[k * 128:(k + 1) * 128, :], writes=["Wo"])
                V14 = V1[:, :, :].rearrange("p b (h e) -> p b h e", e=129)

                def accv(a):
                    return accb[a // 3], f"acc{a // 3}", (a % 3) * 129

                steps = [(ti, h, kb) for ti in range(8) for h in range(8) for kb in range(34)]

                def load_q(n):
                    ti, h = n // 8, n % 8
                    t0 = TILES[ti][0]
                    P.dma("sp", f"q{n % 2}", Qc[n % 2][:, :], qT_d[h * 128:(h + 1) * 128, t0:t0 + 512], writes=[f"Qc{n % 2}"])
                load_q(0)

                def emit_S(i):
                    ti, h, kb = steps[i]
                    n = ti * 8 + h
                    if kb == 0 and n + 1 < 64:
                        load_q(n + 1)
                    Q = Qc[n % 2]
                    qk_ = f"Qc{n % 2}"
                    sp_ = sps2[i % 2]
                    for m in range(2):
                        b.mm(sp_[:, m * 512:(m + 1) * 512], KT1[m * 64:(m + 1) * 64, h, kb * 128:(kb + 1) * 128],
                             Q[m * 64:(m + 1) * 64, :], True, True, ["KT1", qk_], [(f"sps{i % 2}", m)])
                    b.act(pT2[i % 2][:, :], sp_[:, :], AF.Exp, [(f"sps{i % 2}", 0), (f"sps{i % 2}", 1)], [f"pT{i % 2}"],
                          scale=0.125)

                cnt = dict(ni=0, hd=0)

                def transposes(hh):
                    for j in range(4):
                        b.tr(tpb[:, j * 128:(j + 1) * 128], O_sb[:, j, hh * 128:(hh + 1) * 128], ident[:, :],
                             [("O_sb", hh), "ident"], ["tp0"])
                    b.copy("dve", OT_t[:, hh, :], tpb[:, :], ["tp0"], ["OT_t"])

                def emit_rest(i):
                    ti, h, kb = steps[i]
                    t0, TS, s = TILES[ti]
                    p_ = pT2[i % 2]
                    pk_ = f"pT{i % 2}"
                    if h == 7 and kb == 0:
                        P.dma("sp", "x0", hres[:, :, :], src_h(t0, TS, s), writes=["hres0"])
                    for m in range(2):
                        for j in range(4):
                            a = m * 4 + j
                            at, ak_, c0 = accv(a)
                            b.mm(at[:, c0:c0 + 129], p_[:, m * 512 + j * 128:m * 512 + (j + 1) * 128], V14[:, kb, h, :],
                                 kb == 0 and a % 3 == 0, kb == 33, [pk_, "V1"], [ak_], skip=True)
                    if kb == 8 and h > 0:
                        transposes(h - 1)
                    if kb != 33:
                        return
                    hd = cnt["hd"]
                    cnt["hd"] += 1
                    aS = accS[hd % 2]
                    aSk = f"accS{hd % 2}"
                    b.copy("dve", aS[:, 0:387], accb[0][:, 0:387], ["acc0"], [aSk])
                    b.copy("dve", aS[:, 387:774], accb[1][:, 0:387], ["acc1"], [aSk])
                    b.copy("dve", aS[:, 774:1032], accb[2][:, 0:258], ["acc2"], [aSk])
                    for j in range(4):
                        ni = cnt["ni"]
                        cnt["ni"] += 1
                        c1 = j * 129
                        c2 = (4 + j) * 129
                        r_ = rz[:, (ni % 2) * 4:(ni % 2) * 4 + 4]
                        rk_ = f"rz{ni % 2}"
                        tm, tmk = tmp[ni % 2], f"tmp{ni % 2}"
                        o_, obk = ob[ni % 2], f"ob{ni % 2}"
                        b.recip(r_[:, 0:1], aS[:, c1 + 128:c1 + 129], [aSk], [rk_])
                        b.recip(r_[:, 1:2], aS[:, c2 + 128:c2 + 129], [aSk], [rk_])
                        b.ts("dve", r_[:, 1:2], r_[:, 1:2], neglam[:, 0:1], None, ALU.mult, None, [rk_, "neglam"], [rk_])
                        b.ts("dve", tm[:, :], aS[:, c2:c2 + 128], r_[:, 1:2], None, ALU.mult, None, [aSk, rk_], [tmk])
                        b.stt(o_[:, :], aS[:, c1:c1 + 128], r_[:, 0:1], tm[:, :], ALU.mult, ALU.add, [aSk, rk_, tmk], [obk])
                        b.stt(junk[:, :], o_[:, :], 1.0, o_[:, :], ALU.mult, ALU.mult, [obk], ["junk", rk_],
                              accum_out=r_[:, 2:3])
                        b.ts("dve", r_[:, 2:3], r_[:, 2:3], 1.0 / 128, EPS, ALU.mult, ALU.add, [rk_], [rk_])
                        b.tt("pool", r_[:, 3:4], r_[:, 2:3], nhalf[:, 0:1], ALU.pow, [rk_, "nhalf"], [rk_])
                        b.stt(O_sb[:, j, h * 128:(h + 1) * 128], o_[:, :], r_[:, 3:4], subg2[:, :], ALU.mult, ALU.mult,
                              [obk, rk_, "subg2"], [("O_sb", h)])
                    if h != 7:
                        return
                    transposes(7)
                    sp_ = sps2[i % 2]
                    sq_ = sps2[(i + 1) % 2]
                    outproj_tile(1, Wo, OT_t, TS, s, t0, [sp_[:, 0:512], sp_[:, 512:1024], sq_[:, 0:512], sq_[:, 512:1024]],
                                 [(f"sps{i % 2}", 0), (f"sps{i % 2}", 1), (f"sps{(i + 1) % 2}", 0),
                                  (f"sps{(i + 1) % 2}", 1)], hres, "hres0")

                n = len(steps)
                for i in range(n + 1):
                    if i < n:
                        emit_S(i)
                    if i - 1 >= 0:
                        emit_rest(i - 1)
                P.barrier()
                P.emit(block)

        def phase_mlp(l, final):
            with ExitStack() as ps:
                W1 = sb(ps, "W1", [128, 8, 4 * D], BF16)
                W2 = sb(ps, "W2", [128, 32, D], BF16)
                xts = [sb(ps, f"mx{i}", [128, 8, 512]) for i in range(2)]
                m_t = sb(ps, "m_t", [128, 8, 512], BF16)
                h1 = sb(ps, "h1", [128, 16, 512], BF16)
                rr = [sb(ps, f"rr{i}", [128, 512]) for i in range(2)]
                sqb = [sb(ps, f"msq{i}", [128, 512], BF16) for i in range(2)] + \
                      [sb(ps, f"mnt{i}", [128, 512]) for i in range(2)]
                rtmp = sb(ps, "mrtmp", [128, 512])
                rstd = sb(ps, "mrstd", [128, 512])
                ssb = psb(ps, "mss")
                hps = [psb(ps, f"hps{i}") for i in range(2)]
                yps = [psb(ps, f"yps{i}") for i in range(2)]
                block = ps.enter_context(nc.Block())
                for k in range(KC):
                    P.dma("pool", f"m{k % 8}", W1[:, k, :], w1b[l, k * 128:(k + 1) * 128, :], reads=[("w1b", l, k)],
                          writes=[("W1", k)])
                for k4 in range(8):
                    P.dma("pool", f"m{k4 % 8}", W2[:, k4 * 4:(k4 + 1) * 4, :],
                          w2b[l, k4 * 512:(k4 + 1) * 512, :].rearrange("(k p) n -> p k n", p=128),
                          reads=[("w2b", l, k) for k in range(k4 * 4, k4 * 4 + 4)], writes=[("W2", k4 // 4)])
                if l == 0:
                    precast(1)
                tiles = TILES[:8] if final else TILES
                fi = 0
                yi = 0
                def p3_load(ti):
                    t0, TS, s = tiles[ti]
                    P.dma("sp", f"x{ti % 2}", xts[ti % 2][:, :, 0:TS], src_h(t0, TS, s), writes=[f"mx{ti % 2}"])
                p3_load(0)
                for ti, (t0, TS, s) in enumerate(tiles):
                    xt = xts[ti % 2]
                    xk = f"mx{ti % 2}"
                    if ti + 1 < len(tiles):
                        p3_load(ti + 1)
                    normmod(xt, xk, m_t, "m_t", TS, l, 1, s, sqb, ssb, rtmp, rstd)
                    for half in range(2):
                        for fcl in range(16):
                            fc = half * 16 + fcl
                            hp_ = hps[fi % 2]
                            hk_ = f"hps{fi % 2}"
                            for k in range(KC):
                                b.mm(hp_[:, 0:TS], W1[:, k, fc * 128:(fc + 1) * 128], m_t[:, k, 0:TS], k == 0,
                                     k == KC - 1, [("W1", k), ("m_t", k)], [hk_])
                            r_ = rr[fi % 2]
                            b.act(r_[:, 0:TS], hp_[:, 0:TS], AF.Relu, [hk_], [f"rr{fi % 2}"])
                            b.tt("dve", h1[:, fcl, 0:TS], r_[:, 0:TS], r_[:, 0:TS], ALU.mult, [f"rr{fi % 2}"],
                                 [("h1", fcl)])
                            fi += 1
                        for dc in range(KC):
                            yp = yps[yi % 2]
                            yk = f"yps{yi % 2}"
                            for fcl in range(16):
                                b.mm(yp[:, 0:TS], W2[:, half * 16 + fcl, dc * 128:(dc + 1) * 128], h1[:, fcl, 0:TS],
                                     fcl == 0, fcl == 15, [("W2", half), ("h1", fcl)], [yk])
                            b.stt(xt[:, dc, 0:TS], yp[:, 0:TS], modv(l, 5, dc, s), xt[:, dc, 0:TS], ALU.mult, ALU.add,
                                  [yk, xk, "mod"], [xk])
                            yi += 1
                    if not final:
                        P.dma("sp", "hst", fm(hS)[:, :, t0:t0 + TS], xt[:, :, 0:TS], reads=[xk], writes=["hS_dst"])
                    else:
                        for k in range(KC):
                            q = sqb[k % 2]
                            b.act(q[:, 0:TS], xt[:, k, 0:TS], AF.Square, [xk], [f"sq{k % 2}"])
                            b.mm(ssb[:, 0:TS], ones[:, :], q[:, 0:TS], k == 0, k == KC - 1, [f"sq{k % 2}", "ones"],
                                 ["ssb"])
                        b.act(rtmp[:, 0:TS], ssb[:, 0:TS], AF.Sqrt, ["ssb", "epsT"], ["rtmp"], scale=1.0 / D,
                              bias=epsT[:, 0:1])
                        b.recip(rstd[:, 0:TS], rtmp[:, 0:TS], ["rtmp"], ["rstd"])
                        for k in range(KC):
                            b.stt(xt[:, k, 0:TS], xt[:, k, 0:TS], gfs[:, k:k + 1], rstd[:, 0:TS], ALU.mult, ALU.mult,
                                  [xk, "rstd", "gfs"], [xk])
                        P.dma("sp", "hst", fm(outT)[:, :, t0:t0 + TS], xt[:, :, 0:TS], reads=[xk], writes=["outT"])
                P.barrier()
                P.emit(block)

        ch0 = [(c * 128, "q", c, True, True, True) for c in range(4)] + [(512, "k", 0, True, False, True)] + \
              [(640 + c * 128, "q", 4 + c, False, True, True) for c in range(4)] + [(1152, "k", 1, False, False, True)]
        ch1 = [(c * 128, "q", c, True, False, False) for c in range(8)] + \
              [(1024 + c * 128, "k", c, True, False, True) for c in range(8)]
        if max_phase >= 1:
            phase_p1(0, w_in0, 1536, ch0, 1280, 4, 64, src_x)
        if max_phase >= 2:
            phase_l0_attn()
        if max_phase >= 3:
            phase_mlp(0, False)
        if max_phase >= 4:
            phase_p1(1, w_in1, 3072, ch1, 2048, 8, 128, src_h)
        if max_phase >= 5:
            phase_l1_attn()
        if max_phase >= 6:
            phase_mlp(1, True)
        with ExitStack() as ps:
            block = ps.enter_context(nc.Block())
            P.barrier()
            P.emit(block)
    return nc


def _rope_tables_host():
    t = np.arange(NL)
    row = (t // 64).astype(np.float32)
    col = (t % 64).astype(np.float32)
    inv = (10000.0 ** (-np.arange(16, dtype=np.float32) / 16)).astype(np.float32)
    ar = row[:, None] * inv[None, :]
    ac = col[:, None] * inv[None, :]
    ang = np.concatenate([ar, ar, ac, ac], axis=-1).astype(np.float32)
    cos = np.cos(ang).astype(np.float32).T
    sin = np.sin(ang).astype(np.float32).T
    sign = np.where((np.arange(64) % 32) < 16, -1.0, 1.0).astype(np.float32)[:, None]
    sin_s = sin * sign
    cos2 = np.concatenate([cos, cos], 0)
    sin2 = np.concatenate([sin_s, sin_s], 0)
    return np.stack([cos2, sin2, cos2 * 0.125, sin2 * 0.125]).astype(np.float32)


def _consts_host():
    ident = np.eye(128, dtype=np.float32)
    perm = np.zeros((128, 128), np.float32)
    for m in range(128):
        partner = m + 16 if (m % 32) < 16 else m - 16
        perm[partner, m] = 1.0
    return np.concatenate([ident, perm], axis=1)


def _bias_tables(rpb):
    entries = []
    kl = np.arange(128)[:, None]
    ql = np.arange(128)[None, :]
    lower = np.where(kl >= ql, 0.0, NEG).astype(np.float32)
    upper = np.where(kl <= ql, 0.0, NEG).astype(np.float32)
    entries.append(np.repeat(lower[:, None, :], 4, axis=1))
    entries.append(np.repeat(upper[:, None, :], 4, axis=1))
    a_keys = []
    for i in range(32):
        l = []
        if i - 1 >= 0:
            l.append((i - 1, 0))
        l.append((i, None))
        if i + 1 < 32:
            l.append((i + 1, 1))
        a_keys.append(l)
    cache = {}
    b_keys = []
    for i in range(32):
        r = 2 * i + (np.arange(128) // 64)
        cq = np.arange(128) % 64
        rs = np.clip(r - 4, 0, 56)
        cs = np.clip(cq - 8, 0, 48)
        l = []
        for kb in range(32):
            krow = 2 * kb + (np.arange(128) // 64)
            kcol = np.arange(128) % 64
            valid = ((krow[:, None] >= rs[None, :]) & (krow[:, None] < rs[None, :] + 8) &
                     (kcol[:, None] >= cs[None, :]) & (kcol[:, None] < cs[None, :] + 16))
            if not valid.any():
                continue
            roff = np.clip(krow[:, None] - r[None, :] + 7, 0, 14)
            coff = np.clip(kcol[:, None] - cq[None, :], -15, 15) + 15
            key = (valid.tobytes(), np.where(valid, roff, 0).tobytes(), np.where(valid, coff, 0).tobytes())
            if key not in cache:
                cache[key] = len(entries)
                for kv in range(2):
                    g = rpb[kv * 4:(kv + 1) * 4][:, roff, coff]
                    m = np.where(valid[None], g, np.float32(NEG)).astype(np.float32)
                    entries.append(np.transpose(m, (1, 0, 2)))
            l.append((kb, cache[key]))
        b_keys.append(l)
    tab = np.stack(entries, axis=1).reshape(128, -1).astype(np.float32)
    return np.ascontiguousarray(tab), len(entries), a_keys, b_keys


def _bcast(v, n=128):
    return np.ascontiguousarray(np.broadcast_to(np.asarray(v, np.float32).reshape(1, -1), (n, np.asarray(v).size)))


def _key_structure():
    rpb0 = np.zeros((8, 15, 31), np.float32)
    _, n, a_keys, b_keys = _bias_tables(rpb0)
    return n, a_keys, b_keys


_PROG_CACHE = {}


def _prepare_inputs(x, c, ctx, c_ctx, ada_w, ada_b, norm1_g, norm2_g, even_w_in, even_w_out, a_sink, b_rpb,
                    odd_w_in, odd_w_out, lam_q1, lam_k1, lam_q2, lam_k2, subln_g, mlp_w1, mlp_w2, final_g):
    f = np.float32
    tab, n_bias, a_keys, b_keys = _bias_tables(np.asarray(b_rpb[0], f))
    win = np.asarray(even_w_in[0], f)
    aq, ak, av, bq, bk, bv = win[:, 0:512], win[:, 512:640], win[:, 640:768], win[:, 768:1280], win[:, 1280:1408], \
        win[:, 1408:1536]

    def qperm(q):
        cols = []
        for cch in range(4):
            cols.append(q[:, cch * 64:(cch + 1) * 64])
            cols.append(q[:, (4 + cch) * 64:(5 + cch) * 64])
        return np.concatenate(cols, axis=1)
    w_in0 = np.ascontiguousarray(np.concatenate([qperm(aq), ak, qperm(bq), bk, av, bv], axis=1))
    shared = {
        "ada_w": np.ascontiguousarray(ada_w, f),
        "ada_b": np.ascontiguousarray(np.repeat(np.asarray(ada_b, f).reshape(2, 48, 128).transpose(2, 0, 1)[..., None], 2,
                                                axis=-1).reshape(128, 192)),
        "g12": np.ascontiguousarray(np.stack([np.asarray(norm1_g, f), np.asarray(norm2_g, f)], axis=1)
                                    .reshape(2, 2, 8, 128).transpose(3, 0, 1, 2).reshape(128, 32)),
        "gfin": np.ascontiguousarray(np.asarray(final_g, f).reshape(8, 128).T),
        "w_in0": w_in0,
        "w_out0": np.ascontiguousarray(even_w_out[0], f),
        "w_in1": np.ascontiguousarray(odd_w_in[0], f),
        "w_out1": np.ascontiguousarray(odd_w_out[0], f),
        "w1": np.ascontiguousarray(mlp_w1, f),
        "w2": np.ascontiguousarray(mlp_w2, f),
        "sinkb": _bcast(a_sink[0]),
        "lamv": _bcast(np.concatenate([lam_q1[0], lam_k1[0], lam_q2[0], lam_k2[0]])),
        "subg": _bcast(subln_g[0]),
        "consts": _consts_host(),
        "rope": _rope_tables_host(),
        "biasT": tab,
    }
    in_maps = []
    for bb in range(8):
        m = dict(shared)
        m["xT"] = np.ascontiguousarray(np.asarray(x[bb], f).T)
        m["cT"] = np.ascontiguousarray(np.asarray(ctx[bb], f).T)
        cv = np.stack([np.asarray(c[bb], f).reshape(8, 128).T, np.asarray(c_ctx, f).reshape(8, 128).T], axis=-1)
        m["cvec"] = np.ascontiguousarray(cv.reshape(128, 16))
        in_maps.append(m)
    return in_maps, n_bias, a_keys, b_keys


def kernel(**inputs):
    in_maps, n_bias, a_keys, b_keys = _prepare_inputs(**inputs)
    key = ("main", n_bias)
    if key not in _PROG_CACHE:
        _PROG_CACHE[key] = build_program(n_bias, a_keys, b_keys)
    nc = _PROG_CACHE[key]
    res = run_bass_kernel_spmd(nc, in_maps, core_ids=list(range(8)))
    out = np.stack([np.ascontiguousarray(r["outT"].T) for r in res.results], axis=0)
    return out.astype(np.float32)
```

```python
import math
from contextlib import ExitStack
import numpy as np
import concourse.bass as bass
import concourse.mybir as mybir
from concourse.bass_utils import run_bass_kernel_spmd

F32 = mybir.dt.float32
BF16 = mybir.dt.bfloat16
AF = mybir.ActivationFunctionType
ALU = mybir.AluOpType

D = 1024
NL = 4096
NCX = 256
NT = NL + NCX
KC = 8
EPS = 1e-6
NEG = -30000.0
LAM_INIT = 0.8 - 0.6 * math.exp(-0.3 * 1)
TILES = [(t * 512, 512, 0) for t in range(8)] + [(NL, NCX, 1)]
import os as _os
_SKIP = set(_os.environ.get("KSKIP", "").split(","))

ENGS = ("pe", "act", "dve", "pool", "sp")
SAME_ENGINE_SYNC = {"pe": False, "act": True, "dve": True, "pool": True, "sp": False}


class Prog:
    def __init__(self, nc, sems):
        self.nc = nc
        self.esem = {e: sems[i] for i, e in enumerate(("pe", "act", "dve", "pool"))}
        self.dsem_pool = list(sems[4:])
        self.dsem = {}
        self.count = {}
        self.semobj = {}
        for e, s in self.esem.items():
            self.count[id(s)] = 0
            self.semobj[id(s)] = s
        self.ops = {e: [] for e in ENGS}
        self.seen = {e: {} for e in ENGS}
        self.last_w = {}
        self.readers = {}
        self.n_ops = 0

    def dma_sem(self, name):
        if name not in self.dsem:
            s = self.dsem_pool.pop()
            self.dsem[name] = s
            self.count[id(s)] = 0
            self.semobj[id(s)] = s
        return self.dsem[name]

    def _need(self, eng, ev, waits):
        if ev is None:
            return
        sid, val = ev
        if eng in self.esem and sid == id(self.esem[eng]) and not SAME_ENGINE_SYNC[eng]:
            return
        if self.seen[eng].get(sid, 0) >= val:
            return
        if waits.get(sid, 0) < val:
            waits[sid] = val

    def _deps(self, eng, reads, writes):
        waits = {}
        for k in reads:
            self._need(eng, self.last_w.get(k), waits)
        for k in writes:
            self._need(eng, self.last_w.get(k), waits)
            for ev in self.readers.get(k, ()):
                self._need(eng, ev, waits)
        for sid, val in waits.items():
            self.seen[eng][sid] = val
        return [(self.semobj[sid], val) for sid, val in waits.items()]

    def _commit(self, ev, reads, writes):
        for k in writes:
            self.last_w[k] = ev
            self.readers[k] = []
        for k in reads:
            lst = self.readers.setdefault(k, [])
            for i, (sid, val) in enumerate(lst):
                if sid == ev[0]:
                    lst[i] = (sid, max(val, ev[1]))
                    break
            else:
                lst.append(ev)

    def op(self, eng, fn, reads=(), writes=()):
        waits = self._deps(eng, reads, writes)
        s = self.esem[eng]
        self.count[id(s)] += 1
        ev = (id(s), self.count[id(s)])
        self.ops[eng].append((waits, fn, (s, 1)))
        self._commit(ev, reads, writes)
        self.n_ops += 1

    def dma(self, q, semname, out, in_, reads=(), writes=()):
        s = self.dma_sem(semname)
        waits = self._deps(q, reads, writes)
        if self.count[id(s)] > 0:
            w = {}
            self._need(q, (id(s), self.count[id(s)]), w)
            for sid, val in w.items():
                self.seen[q][sid] = val
                waits.append((self.semobj[sid], val))
        self.count[id(s)] += 16
        ev = (id(s), self.count[id(s)])

        kw = dict(max_dma_last_dim=4096) if q == "pool" else {}

        def fn(e, out=out, in_=in_, kw=kw):
            return e.dma_start(out=out, in_=in_, **kw)
        self.ops[q].append((waits, fn, (s, 16)))
        self._commit(ev, reads, writes)
        self.n_ops += 1

    def barrier(self):
        for e in ENGS:
            waits = []
            for sid, val in self.count.items():
                if val > 0 and self.seen[e].get(sid, 0) < val:
                    if e in self.esem and sid == id(self.esem[e]):
                        continue
                    waits.append((self.semobj[sid], val))
                    self.seen[e][sid] = val
            if waits:
                self.ops[e].append((waits, None, None))

    def emit(self, block):
        prog = self

        def mk(ename):
            oplist = prog.ops[ename]

            def body(eng):
                for waits, fn, inc in oplist:
                    for s, v in waits:
                        eng.wait_ge(s, v)
                    if fn is not None:
                        fn(eng).then_inc(inc[0], inc[1])
            return body
        block.tensor(mk("pe"))
        block.scalar(mk("act"))
        block.vector(mk("dve"))
        block.gpsimd(mk("pool"))
        block.sync(mk("sp"))
        self.ops = {e: [] for e in ENGS}


class B:
    def __init__(self, nc, P):
        self.nc = nc
        self.P = P

    def mm(self, out, lhsT, rhs, start, stop, reads, writes, skip=False):
        self.P.op("pe", lambda e: e.matmul(out, lhsT=lhsT, rhs=rhs, start=start, stop=stop, skip_group_check=skip),
                  reads, writes)

    def tr(self, out, in_, ident, reads, writes):
        self.P.op("pe", lambda e: e.transpose(out=out, in_=in_, identity=ident), reads, writes)

    def act(self, out, in_, func, reads, writes, scale=1.0, bias=0.0):
        self.P.op("act", lambda e: e.activation(out=out, in_=in_, func=func, bias=bias, scale=scale), reads, writes)

    def tt(self, eng, out, in0, in1, op, reads, writes):
        self.P.op(eng, lambda e: e.tensor_tensor(out=out, in0=in0, in1=in1, op=op), reads, writes)

    def ts(self, eng, out, in0, s1, s2, op0, op1, reads, writes):
        if s2 is None:
            self.P.op(eng, lambda e: e.tensor_scalar(out=out, in0=in0, scalar1=s1, scalar2=None, op0=op0), reads, writes)
        else:
            self.P.op(eng, lambda e: e.tensor_scalar(out=out, in0=in0, scalar1=s1, scalar2=s2, op0=op0, op1=op1),
                      reads, writes)

    def stt(self, out, in0, scalar, in1, op0, op1, reads, writes, accum_out=None):
        if accum_out is None:
            self.P.op("dve", lambda e: e.scalar_tensor_tensor(out=out, in0=in0, scalar=scalar, in1=in1, op0=op0, op1=op1),
                      reads, writes)
        else:
            self.P.op("dve", lambda e: e.scalar_tensor_tensor(out=out, in0=in0, scalar=scalar, in1=in1, op0=op0, op1=op1,
                                                             accum_out=accum_out), reads, writes)

    def copy(self, eng, out, in_, reads, writes):
        if eng == "act":
            self.act(out, in_, AF.Copy, reads, writes)
        else:
            self.P.op(eng, lambda e: e.tensor_copy(out=out, in_=in_), reads, writes)

    def memset(self, eng, ap, val, writes):
        self.P.op(eng, lambda e: e.memset(ap, val), (), writes)

    def recip(self, out, in_, reads, writes):
        self.P.op("dve", lambda e: e.reciprocal(out=out, in_=in_), reads, writes)


def build_program(n_bias, a_keys, b_keys, dbg=False, max_phase=99):
    nc = bass.Bass("TRN2", target_bir_lowering=False)

    def din(name, shape, dt=F32):
        return nc.dram_tensor(name, list(shape), dt, kind="ExternalInput").ap()

    xT = din("xT", [D, NL])
    cT = din("cT", [D, NCX])
    cvec = din("cvec", [128, 16])
    ada_w = din("ada_w", [2, D, 6 * D])
    ada_b = din("ada_b", [128, 192])
    g12 = din("g12", [128, 32])
    gfin = din("gfin", [128, 8])
    w_in0 = din("w_in0", [D, 1536])
    w_out0 = din("w_out0", [D, D])
    w_in1 = din("w_in1", [D, 3072])
    w_out1 = din("w_out1", [D, D])
    w1 = din("w1", [2, D, 4 * D])
    w2 = din("w2", [2, 4 * D, D])
    sinkb = din("sinkb", [128, 8])
    lamv = din("lamv", [128, 256])
    subg = din("subg", [128, 128])
    consts = din("consts", [128, 256])
    rope = din("rope", [4, 128, NL])
    biasT = din("biasT", [128, n_bias * 512])
    outT = nc.dram_tensor("outT", [D, NL], F32, kind="ExternalOutput").ap()
    skind = "ExternalOutput" if dbg else "Internal"
    hS = nc.dram_tensor("hS", [D, NT], F32, kind=skind).ap()
    qT_d = nc.dram_tensor("qT_d", [D, NT], BF16, kind=skind).ap()
    kT_d = nc.dram_tensor("kT_d", [D, NT], BF16, kind=skind).ap()
    v_d = nc.dram_tensor("v_d", [NT, 8 * 129], BF16, kind=skind).ap()
    w1b = nc.dram_tensor("w1b", [2, D, 4 * D], BF16).ap()
    w2b = nc.dram_tensor("w2b", [2, 4 * D, D], BF16).ap()
    dbg_out = {}
    if dbg:
        dbg_out["mod"] = nc.dram_tensor("dbg_mod", [128, 192], F32, kind="ExternalOutput").ap()

    def fm(ap):
        return ap.rearrange("(k p) t -> p k t", p=128)

    with ExitStack() as es:
        sems = [es.enter_context(nc.semaphore(f"s{i}")) for i in range(56)]
        P = Prog(nc, sems)
        b = B(nc, P)

        uid = [0]

        def sb(stack, name, shape, dt=F32):
            uid[0] += 1
            return stack.enter_context(nc.sbuf_tensor(f"{name}_{uid[0]}", list(shape), dt))

        def psb(stack, name, shape=(128, 512), dt=F32):
            uid[0] += 1
            return stack.enter_context(nc.psum_tensor(f"{name}_{uid[0]}", list(shape), dt))

        mod = sb(es, "mod", [128, 192])
        gm = sb(es, "gm", [128, 64])
        g12s = sb(es, "g12s", [128, 32])
        gfs = sb(es, "gfs", [128, 8])
        ident = sb(es, "ident", [128, 128], BF16)
        perm = sb(es, "perm", [128, 128], BF16)
        ones = sb(es, "ones", [128, 128], BF16)
        nhalf = sb(es, "nhalf", [128, 1])
        expsink = sb(es, "expsink", [128, 8])
        neglam = sb(es, "neglam", [128, 1])
        subg2 = sb(es, "subg2", [128, 128])
        epsT = sb(es, "epsT", [128, 1])

        mod4 = mod[:, :].rearrange("p (l j s) -> p l j s", l=2, j=48, s=2)
        gm5 = gm[:, :].rearrange("p (l w k s) -> p l w k s", l=2, w=2, k=8, s=2)
        g124 = g12s[:, :].rearrange("p (l w k) -> p l w k", l=2, w=2, k=8)

        def modv(l, idx, k, s):
            return mod4[:, l, idx * 8 + k, s:s + 1]

        def gmv(l, which, k, s):
            return gm5[:, l, which, k, s:s + 1]

        with ExitStack() as ps:
            cv = sb(ps, "cv", [128, 16])
            s_bf = sb(ps, "s_bf", [128, 16], BF16)
            ab = sb(ps, "ab", [128, 192])
            lv = sb(ps, "lv", [128, 256])
            lt = sb(ps, "lt", [128, 128])
            ls = sb(ps, "ls", [128, 4])
            sk = sb(ps, "sk", [128, 8])
            sg = sb(ps, "sg", [128, 128])
            wbuf = [sb(ps, f"wbuf{i}", [128, 8, 3072], BF16) for i in range(2)]
            pm = psb(ps, "pm")
            block = ps.enter_context(nc.Block())

            P.dma("sp", "c0", cv[:, :], cvec, writes=["cv"])
            P.dma("sp", "c1", ab[:, :], ada_b, writes=["ab"])
            P.dma("sp", "c2", g12s[:, :], g12, writes=["g12s"])
            P.dma("sp", "c3", gfs[:, :], gfin, writes=["gfs"])
            P.dma("sp", "c0", sk[:, :], sinkb, writes=["sk"])
            P.dma("sp", "c1", lv[:, :], lamv, writes=["lv"])
            P.dma("sp", "c2", sg[:, :], subg, writes=["sg"])
            P.dma("pool", "c4", ident[:, :], consts[:, 0:128], writes=["ident"])
            P.dma("pool", "c5", perm[:, :], consts[:, 128:256], writes=["perm"])
            b.memset("dve", ones[:, :], 1.0, ["ones"])
            b.memset("dve", nhalf[:, :], -0.5, ["nhalf"])
            b.memset("dve", epsT[:, :], EPS, ["epsT"])
            b.act(s_bf[:, :], cv[:, :], AF.Silu, ["cv"], ["s_bf"])
            b.act(expsink[:, :], sk[:, :], AF.Exp, ["sk"], ["expsink"])
            s3 = s_bf[:, :].rearrange("p (k s) -> p k s", s=2)
            li = 0
            for l in range(2):
                for half in range(2):
                    wb = wbuf[li % 2]
                    wk = f"wbuf{li % 2}"
                    for k in range(KC):
                        P.dma("pool", f"w{k % 16}", wb[:, k, :],
                              ada_w[l, k * 128:(k + 1) * 128, half * 3072:(half + 1) * 3072], writes=[(wk, k)])
                    for jj in range(24):
                        col = (l * 48 + half * 24 + jj) * 2
                        for k in range(KC):
                            b.mm(pm[:, col:col + 2], wb[:, k, jj * 128:(jj + 1) * 128], s3[:, k, :],
                                 k == 0, k == KC - 1, [(wk, k), "s_bf"], ["pm"])
                    li += 1
            b.tt("dve", mod[:, :], pm[:, 0:192], ab[:, :], ALU.add, ["pm", "ab"], ["mod"])
            for l in range(2):
                for w in range(2):
                    for s in range(2):
                        sc = mod4[:, l, (1 + 3 * w) * 8:(2 + 3 * w) * 8, s]
                        b.stt(gm5[:, l, w, :, s], sc, 1.0, g124[:, l, w, :], ALU.add, ALU.mult,
                              ["mod", "g12s"], ["gm"])
            lv3 = lv[:, :].rearrange("p (a d) -> p a d", d=64)
            for i in range(2):
                b.stt(lt[:, 0:64], lv3[:, 2 * i, :], 1.0, lv3[:, 2 * i + 1, :], ALU.mult, ALU.mult,
                      ["lv"], ["lt", "ls"], accum_out=ls[:, i:i + 1])
            b.act(ls[:, 2:4], ls[:, 0:2], AF.Exp, ["ls"], ["ls"])
            b.tt("dve", neglam[:, :], ls[:, 3:4], ls[:, 2:3], ALU.subtract, ["ls"], ["neglam"])
            b.ts("dve", neglam[:, :], neglam[:, :], -LAM_INIT, None, ALU.add, None, ["neglam"], ["neglam"])
            b.ts("dve", subg2[:, :], sg[:, :], 1.0 - LAM_INIT, None, ALU.mult, None, ["sg"], ["subg2"])
            if dbg:
                P.dma("sp", "dbg", dbg_out["mod"], mod[:, :], reads=["mod"], writes=["dbg_mod"])
            P.barrier()
            P.emit(block)

        def normmod(xt, xk, a_out, ak, TS, l, which, s, sqb, ssb, rtmp, rstd):
            for k in range(KC):
                q = sqb[k % 2]
                b.act(q[:, 0:TS], xt[:, k, 0:TS], AF.Square, [xk], [f"sq{k % 2}"])
                b.mm(ssb[:, 0:TS], ones[:, :], q[:, 0:TS], k == 0, k == KC - 1, [f"sq{k % 2}", "ones"], ["ssb"])
            b.act(rtmp[:, 0:TS], ssb[:, 0:TS], AF.Sqrt, ["ssb", "epsT"], ["rtmp"], scale=1.0 / D, bias=epsT[:, 0:1])
            b.recip(rstd[:, 0:TS], rtmp[:, 0:TS], ["rtmp"], ["rstd"])
            for k in range(KC):
                t = sqb[2 + k % 2]
                b.stt(t[:, 0:TS], xt[:, k, 0:TS], gmv(l, which, k, s), rstd[:, 0:TS], ALU.mult, ALU.mult,
                      [xk, "rstd", "gm"], [f"nt{k % 2}"])
                b.act(a_out[:, k, 0:TS], t[:, 0:TS], AF.Identity, [f"nt{k % 2}", "mod"], [(ak, k)],
                      bias=modv(l, 3 * which, k, s))

        def phase_p1(l, W, NC_, fm_chunks, v_col0, nh, dv, src_fn):
            with ExitStack() as ps:
                Wb = sb(ps, "p1W", [128, 8, NC_], BF16)
                xts = [sb(ps, f"p1x{i}", [128, 8, 512]) for i in range(2)]
                a_ts = [sb(ps, f"p1a{i}", [128, 8, 512], BF16) for i in range(2)]
                sqb = [sb(ps, f"p1sq{i}", [128, 512], BF16) for i in range(2)] + \
                      [sb(ps, f"p1nt{i}", [128, 512]) for i in range(2)]
                rtmp = sb(ps, "p1rtmp", [128, 512])
                rstd = sb(ps, "p1rstd", [128, 512])
                nq = sum(1 for c in fm_chunks if c[1] == "q")
                nk = sum(1 for c in fm_chunks if c[1] == "k")
                qst = sb(ps, "p1qst", [128, nq, 512], BF16)
                kst = sb(ps, "p1kst", [128, nk, 512], BF16)
                VW = nh * (dv + 1)
                vst = sb(ps, "p1vst", [128, 4, VW], BF16)
                ntab = 4 if l == 0 else 2
                tabs = [sb(ps, f"p1tab{i}", [128, ntab, 512]) for i in range(2)]
                q_sb = [sb(ps, f"p1qsb{i}", [128, 512], BF16) for i in range(2)]
                t1 = [sb(ps, f"p1t1{i}", [128, 512]) for i in range(2)]
                t2 = [sb(ps, f"p1t2{i}", [128, 512]) for i in range(2)]
                ssb = psb(ps, "p1ss")
                qps = [psb(ps, f"p1qps{i}") for i in range(2)]
                pps = [psb(ps, f"p1pps{i}") for i in range(2)]
                vps = [psb(ps, f"p1vps{i}") for i in range(2)]
                block = ps.enter_context(nc.Block())

                for k in range(KC):
                    P.dma("pool", f"w{k % 16}", Wb[:, k, :], W[k * 128:(k + 1) * 128, :], writes=[("p1W", k)])
                b.memset("dve", vst[:, :, :], 1.0, ["vst"])
                vst4 = vst[:, :, :].rearrange("p b (h e) -> p b h e", e=dv + 1)
                ci = 0
                vi = 0
                def p1_load(ti):
                    t0, TS, s = TILES[ti]
                    P.dma("sp", f"x{ti % 2}", xts[ti % 2][:, :, 0:TS], src_fn(t0, TS, s), writes=[f"p1x{ti % 2}"])
                    if s == 0:
                        P.dma("sp", f"t{ti % 2}", tabs[ti % 2][:, :, 0:TS],
                              rope[0:ntab, :, t0:t0 + TS].rearrange("a p t -> p a t"), writes=[f"p1tab{ti % 2}"])
                p1_load(0)
                for ti, (t0, TS, s) in enumerate(TILES):
                    xt = xts[ti % 2]
                    xk = f"p1x{ti % 2}"
                    tab = tabs[ti % 2]
                    tk = f"p1tab{ti % 2}"
                    if ti + 1 < len(TILES):
                        p1_load(ti + 1)
                    a_t = a_ts[ti % 2]
                    pa = f"p1a{ti % 2}"
                    normmod(xt, xk, a_t, pa, TS, l, 0, s, sqb, ssb, rtmp, rstd)
                    for (col0, kind, dch, rp, qs, ctx_needed) in fm_chunks:
                        if (s == 1 and not ctx_needed) or "fm" in _SKIP:
                            continue
                        if "rope" in _SKIP:
                            rp = False
                        qp = qps[ci % 2]
                        qk = f"p1qps{ci % 2}"
                        for k in range(KC):
                            b.mm(qp[:, 0:TS], Wb[:, k, col0:col0 + 128], a_t[:, k, 0:TS], k == 0, k == KC - 1,
                                 [("p1W", k), (pa, k)], [qk])
                        dst = (qst if kind == "q" else kst)[:, dch, 0:TS]
                        dk = ("p1st", kind, dch)
                        if rp and s == 0:
                            qsb = q_sb[ci % 2]
                            b.copy("act", qsb[:, 0:TS], qp[:, 0:TS], [qk], [f"qsb{ci % 2}"])
                            pp = pps[ci % 2]
                            b.mm(pp[:, 0:TS], perm[:, :], qsb[:, 0:TS], True, True, [f"qsb{ci % 2}", "perm"],
                                 [f"pps{ci % 2}"])
                            to = 2 if qs else 0
                            if "ropeD" not in _SKIP:
                                if "ropeE" not in _SKIP:
                                    b.tt("dve", t1[ci % 2][:, 0:TS], qsb[:, 0:TS], tab[:, to, 0:TS], ALU.mult,
                                         [f"qsb{ci % 2}", tk], [f"t1{ci % 2}"])
                                else:
                                    b.tt("dve", t1[ci % 2][:, 0:TS], qp[:, 0:TS], tab[:, to, 0:TS], ALU.mult, [qk, tk],
                                         [f"t1{ci % 2}"])
                            if "ropeC" not in _SKIP:
                                b.tt("dve", t2[ci % 2][:, 0:TS], pp[:, 0:TS], tab[:, to + 1, 0:TS], ALU.mult,
                                     [f"pps{ci % 2}", tk], [f"t2{ci % 2}"])
                            if "ropeA" in _SKIP:
                                b.copy("act", dst, t1[ci % 2][:, 0:TS], [f"t1{ci % 2}", f"t2{ci % 2}"], [dk])
                            elif "ropeB" in _SKIP:
                                b.tt("dve", dst, t1[ci % 2][:, 0:TS], t2[ci % 2][:, 0:TS], ALU.add,
                                     [f"t1{ci % 2}", f"t2{ci % 2}"], [dk])
                            else:
                                b.tt("pool", dst, t1[ci % 2][:, 0:TS], t2[ci % 2][:, 0:TS], ALU.add,
                                     [f"t1{ci % 2}", f"t2{ci % 2}"], [dk])
                        else:
                            b.act(dst, qp[:, 0:TS], AF.Copy, [qk], [dk], scale=(0.125 if qs else 1.0))
                        ci += 1
                    nb = TS // 128
                    for jb in range(0 if "v" in _SKIP else nb):
                        for hh in range((nh * dv) // 512 if nh * dv >= 512 else 1):
                            wcols = min(512, nh * dv)
                            vp = vps[vi % 2]
                            vk = f"p1vps{vi % 2}"
                            for k in range(KC):
                                b.mm(vp[:, 0:wcols], a_t[:, k, jb * 128:(jb + 1) * 128],
                                     Wb[:, k, v_col0 + hh * 512:v_col0 + hh * 512 + wcols], k == 0, k == KC - 1,
                                     [("p1W", k), (pa, k)], [vk])
                            hpc = wcols // dv
                            b.copy("dve" if vi % 2 else "act", vst4[:, jb, hh * hpc:(hh + 1) * hpc, 0:dv],
                                   vp[:, 0:wcols].rearrange("p (h d) -> p h d", d=dv), [vk], ["vst"])
                            vi += 1
                    nqs = sum(1 for c in fm_chunks if c[1] == "q" and (s == 0 or c[5]))
                    if "st" in _SKIP:
                        continue
                    if nqs:
                        P.dma("sp", "stq", fm(qT_d)[:, 0:nq, t0:t0 + TS], qst[:, :, 0:TS],
                              reads=[("p1st", "q", i) for i in range(nq)], writes=["qT_d"])
                    P.dma("sp", "stk", fm(kT_d)[:, 0:nk, t0:t0 + TS], kst[:, :, 0:TS],
                          reads=[("p1st", "k", i) for i in range(nk)], writes=["kT_d"])
                    P.dma("sp", "stv", v_d[t0:t0 + TS, 0:VW].rearrange("(b p) f -> p b f", p=128),
                          vst[:, 0:nb, :], reads=["vst"], writes=["v_d"])
                P.barrier()
                P.emit(block)

        def outproj_tile(l, Wo, OT_t, TS, s, t0, ybanks, ykeys, ht, hk):
            for dc in range(KC):
                yb = ybanks[dc % len(ybanks)]
                yk = ykeys[dc % len(ybanks)]
                for c in range(KC):
                    b.mm(yb[:, 0:TS], Wo[:, c, dc * 128:(dc + 1) * 128], OT_t[:, c, 0:TS], c == 0, c == KC - 1,
                         ["Wo", "OT_t"], [yk])
                b.stt(ht[:, dc, 0:TS], yb[:, 0:TS], modv(l, 2, dc, s), ht[:, dc, 0:TS], ALU.mult, ALU.add,
                      [yk, hk, (hk, dc), "mod"], [(hk, dc)])
            P.dma("sp", "hst", fm(hS)[:, :, t0:t0 + TS], ht[:, :, 0:TS], reads=[hk] + [(hk, dc) for dc in range(KC)],
                  writes=["hS_dst"])

        def src_x(t0, TS, s):
            return fm(xT)[:, :, t0:t0 + TS] if s == 0 else fm(cT)[:, :, 0:TS]

        def src_h(t0, TS, s):
            return fm(hS)[:, :, t0:t0 + TS]

        def precast(l):
            for k in range(KC):
                P.dma("pool", f"w{k % 16}", w1b[l, k * 128:(k + 1) * 128, :], w1[l, k * 128:(k + 1) * 128, :],
                      writes=[("w1b", l, k)])
            for k in range(32):
                P.dma("pool", f"w{k % 16}", w2b[l, k * 128:(k + 1) * 128, :], w2[l, k * 128:(k + 1) * 128, :],
                      writes=[("w2b", l, k)])

        def phase_l0_attn():
            with ExitStack() as ps:
                KT0 = sb(ps, "KT0", [128, 2, NT], BF16)
                V0 = sb(ps, "V0", [128, 34, 4 * 65], BF16)
                bias = sb(ps, "bias", [128, n_bias, 512], BF16)
                Wo = sb(ps, "Wo", [128, 8, D], BF16)
                Qt = [sb(ps, f"Qt{i}", [128, 8, 512], BF16) for i in range(2)]
                NPT = 3
                pT = [sb(ps, f"pT{i}", [128, 512], BF16) for i in range(NPT)]
                O_sb = [sb(ps, f"O_sb{i}", [128, D], BF16) for i in range(2)]
                OT_t = sb(ps, "OT_t", [128, 8, 512], BF16)
                zt = sb(ps, "zt", [128, 8])
                rz = sb(ps, "rz", [128, 8])
                hts = [sb(ps, f"hres{i}", [128, 8, 512]) for i in range(2)]
                spsb = [psb(ps, f"sps{i}") for i in range(NPT)]
                accb = [psb(ps, f"acc{i}") for i in range(2)]
                tpb = psb(ps, "tp0", [128, 1024], BF16)
                ypb = [psb(ps, f"ypb{i}") for i in range(2)]
                block = ps.enter_context(nc.Block())

                P.dma("sp", "kt", KT0[:, :, :], fm(kT_d)[:, 0:2, :], writes=["KT0"])
                P.dma("sp", "vv", V0[:, :, :], v_d[:, 0:260].rearrange("(b p) f -> p b f", p=128), writes=["V0"])
                nbh = (n_bias + 1) // 2
                P.dma("pool", "w0", bias[:, 0:nbh, :], biasT[:, 0:nbh * 512].rearrange("p (n f) -> p n f", f=512),
                      writes=["bias"])
                P.dma("pool", "w1", bias[:, nbh:n_bias, :],
                      biasT[:, nbh * 512:n_bias * 512].rearrange("p (n f) -> p n f", f=512), writes=["bias"])
                for k in range(KC):
                    P.dma("pool", f"w{k % 16}", Wo[:, k, :], w_out0[k * 128:(k + 1) * 128, :], writes=["Wo"])
                precast(0)
                V04 = V0[:, :, :].rearrange("p b (h e) -> p b h e", e=65)

                steps = []
                for ti, (t0, TS, s) in enumerate(TILES):
                    nqb = TS // 128
                    for qbl in range(nqb):
                        qb = t0 // 128 + qbl
                        for g in range(4):
                            typ, kv = g // 2, g % 2
                            if s == 1:
                                klist = [(32, None), (33, None)]
                            else:
                                klist = (a_keys if typ == 0 else b_keys)[qb] + [(32, None), (33, None)]
                            for idx, (kb, be) in enumerate(klist):
                                steps.append(dict(ti=ti, t0=t0, TS=TS, s=s, qbl=qbl, g=g, typ=typ, kv=kv, idx=idx, kb=kb,
                                                  be=be, last=(idx == len(klist) - 1), first_tile=(qbl == 0 and g == 0 and idx == 0),
                                                  last_q=(g == 3 and idx == len(klist) - 1),
                                                  last_tile=(qbl == nqb - 1 and g == 3 and idx == len(klist) - 1)))
                grp = 0
                qbi = 0
                for st_ in steps:
                    st_["ai"] = grp
                    st_["oi"] = qbi
                    if st_["last"]:
                        grp += 1
                    if st_["last_q"]:
                        qbi += 1

                def load_q(ti):
                    t0, TS, s = TILES[ti]
                    P.dma("sp", f"q{ti % 2}", Qt[ti % 2][:, :, 0:TS], fm(qT_d)[:, :, t0:t0 + TS], writes=[f"Qt{ti % 2}"])

                def load_h(ti):
                    t0, TS, s = TILES[ti]
                    P.dma("sp", f"x{ti % 2}", hts[ti % 2][:, :, 0:TS], src_x(t0, TS, s), writes=[f"hres{ti % 2}"])
                load_q(0)
                load_h(0)

                def emit_S(i):
                    st_ = steps[i]
                    ti, typ, kv, kb, be = st_["ti"], st_["typ"], st_["kv"], st_["kb"], st_["be"]
                    if st_["first_tile"] and ti + 1 < len(TILES):
                        load_q(ti + 1)
                    Q = Qt[ti % 2]
                    qk_ = f"Qt{ti % 2}"
                    hp = kv * 64
                    qoff = st_["qbl"] * 128
                    sp_ = spsb[i % NPT]
                    sk_ = f"sps{i % NPT}"
                    b.mm(sp_[:, :].rearrange("p (c q) -> p c q", c=4), KT0[hp:hp + 64, typ, kb * 128:(kb + 1) * 128],
                         Q[hp:hp + 64, typ * 4:typ * 4 + 4, qoff:qoff + 128], True, be is None, ["KT0", qk_], [sk_])
                    if be is not None:
                        e = be if typ == 0 else be + kv
                        b.mm(sp_[:, :], ident[:, :], bias[:, e, :], False, True, ["bias", "ident"], [sk_])
                    b.act(pT[i % NPT][:, :], sp_[:, :], AF.Exp, [sk_], [f"pT{i % NPT}"])

                pending = []

                def emit_rest(i):
                    st_ = steps[i]
                    ti, typ, kv, kb, s, TS, t0 = st_["ti"], st_["typ"], st_["kv"], st_["kb"], st_["s"], st_["TS"], st_["t0"]
                    ai, oi = st_["ai"], st_["oi"]
                    if st_["first_tile"] and ti + 1 < len(TILES):
                        pending.append((i + 12, lambda ti=ti: load_h(ti + 1)))
                    acc = accb[ai % 2]
                    acck = f"acc{ai % 2}"
                    p_ = pT[i % NPT]
                    pk_ = f"pT{i % NPT}"
                    for j in range(4):
                        b.mm(acc[:, j * 65:(j + 1) * 65], p_[:, j * 128:(j + 1) * 128], V04[:, kb, typ * 2 + kv, :],
                             st_["idx"] == 0 and j == 0, st_["last"], [pk_, "V0"], [acck], skip=True)
                    if not st_["last"]:
                        return
                    Ob = O_sb[oi % 2]
                    ok_ = f"O_sb{oi % 2}"
                    acc3 = acc[:, 0:260].rearrange("p (j e) -> p j e", e=65)
                    zz = zt[:, (ai % 2) * 4:(ai % 2) * 4 + 4]
                    rr = rz[:, (ai % 2) * 4:(ai % 2) * 4 + 4]
                    zk, rk = f"zt{ai % 2}", f"rz{ai % 2}"
                    if typ == 0:
                        b.tt("dve", zz, acc3[:, :, 64], expsink[:, kv * 4:kv * 4 + 4], ALU.add, [acck, "expsink"], [zk])
                    else:
                        b.copy("dve", zz, acc3[:, :, 64], [acck], [zk])
                    b.recip(rr, zz, [zk], [rk])
                    base = typ * 512 + kv * 256
                    b.tt("dve", Ob[:, base:base + 256].rearrange("p (j d) -> p j d", d=64), acc3[:, :, 0:64],
                         rr.unsqueeze(2).broadcast_to([128, 4, 64]), ALU.mult, [acck, rk], [ok_])
                    if not st_["last_q"]:
                        return
                    qoff = st_["qbl"] * 128

                    def fin(Ob=Ob, ok_=ok_, qoff=qoff, st_=st_, TS=TS, s=s, t0=t0, ti=ti):
                        for c in range(KC):
                            b.tr(tpb[:, c * 128:(c + 1) * 128], Ob[:, c * 128:(c + 1) * 128], ident[:, :], [ok_, "ident"],
                                 ["tp0"])
                        b.copy("dve", OT_t[:, :, qoff:qoff + 128], tpb[:, :].rearrange("p (c q) -> p c q", c=8), ["tp0"],
                               ["OT_t"])
                        if st_["last_tile"]:
                            outproj_tile(0, Wo, OT_t, TS, s, t0, ypb, ["ypb0", "ypb1"], hts[ti % 2], f"hres{ti % 2}")
                    pending.append((i + 6, fin))

                DEPTH = NPT - 1
                n = len(steps)
                for i in range(n + DEPTH):
                    if i < n:
                        emit_S(i)
                    if i - DEPTH >= 0:
                        emit_rest(i - DEPTH)
                        while pending and pending[0][0] <= i - DEPTH:
                            pending.pop(0)[1]()
                while pending:
                    pending.pop(0)[1]()
                P.barrier()
                P.emit(block)

        def phase_l1_attn():
            with ExitStack() as ps:
                KT1 = sb(ps, "KT1", [128, 8, NT], BF16)
                V1 = sb(ps, "V1", [128, 34, 8 * 129], BF16)
                Wo = sb(ps, "Wo", [128, 8, D], BF16)
                Qc = [sb(ps, f"Qc{i}", [128, 512], BF16) for i in range(2)]
                pT2 = [sb(ps, f"pT{i}", [128, 1024], BF16) for i in range(2)]
                accS = [sb(ps, f"accS{i}", [128, 8 * 129]) for i in range(2)]
                O_sb = sb(ps, "O_sb", [128, 4, D], BF16)
                OT_t = sb(ps, "OT_t", [128, 8, 512], BF16)
                rz = sb(ps, "rz", [128, 32])
                tmp = [sb(ps, f"tmp{i}", [128, 128]) for i in range(2)]
                ob = [sb(ps, f"ob{i}", [128, 128]) for i in range(4)]
                junk = sb(ps, "junk", [128, 128])
                hres = sb(ps, "hres0", [128, 8, 512])
                sps2 = [psb(ps, f"sps{i}", (128, 1024)) for i in range(2)]
                accb = [psb(ps, f"acc{i}") for i in range(3)]
                tpb = psb(ps, "tp0", [128, 512], BF16)
                block = ps.enter_context(nc.Block())

                for c in range(KC):
                    P.dma("sp", f"kt{c % 2}", KT1[:, c, :], kT_d[c * 128:(c + 1) * 128, :], writes=["KT1"])
                v_r = v_d[:, :].rearrange("(b p) f -> p b f", p=128)
                for i in range(2):
                    P.dma("sp", f"vv{i}", V1[:, i * 17:(i + 1) * 17, :], v_r[:, i * 17:(i + 1) * 17, :], writes=["V1"])
                for k in range(KC):
                    P.dma("pool", f"w{k % 16}", Wo[:, k, :], w_out1[k * 128:(k + 1) * 128, :], writes=["Wo"])
                V14 = V1[:, :, :].rearrange("p b (h e) -> p b h e", e=129)

                def accv(a):
                    return accb[a // 3], f"acc{a // 3}", (a % 3) * 129

                steps = [(ti, h, kb) for ti in range(8) for h in range(8) for kb in range(34)]

                def load_q(n):
                    ti, h = n // 8, n % 8
                    t0 = TILES[ti][0]
                    P.dma("sp", f"q{n % 2}", Qc[n % 2][:, :], qT_d[h * 128:(h + 1) * 128, t0:t0 + 512], writes=[f"Qc{n % 2}"])
                load_q(0)

                def emit_S(i):
                    ti, h, kb = steps[i]
                    n = ti * 8 + h
                    if kb == 0 and n + 1 < 64:
                        load_q(n + 1)
                    Q = Qc[n % 2]
                    qk_ = f"Qc{n % 2}"
                    sp_ = sps2[i % 2]
                    for m in range(2):
                        b.mm(sp_[:, m * 512:(m + 1) * 512], KT1[m * 64:(m + 1) * 64, h, kb * 128:(kb + 1) * 128],
                             Q[m * 64:(m + 1) * 64, :], True, True, ["KT1", qk_], [(f"sps{i % 2}", m)])
                    b.act(pT2[i % 2][:, :], sp_[:, :], AF.Exp, [(f"sps{i % 2}", 0), (f"sps{i % 2}", 1)], [f"pT{i % 2}"],
                          scale=0.125)

                cnt = dict(ni=0, hd=0)

                def transposes(hh):
                    for j in range(4):
                        b.tr(tpb[:, j * 128:(j + 1) * 128], O_sb[:, j, hh * 128:(hh + 1) * 128], ident[:, :],
                             [("O_sb", hh), "ident"], ["tp0"])
                    b.copy("dve", OT_t[:, hh, :], tpb[:, :], ["tp0"], ["OT_t"])

                def emit_rest(i):
                    ti, h, kb = steps[i]
                    t0, TS, s = TILES[ti]
                    p_ = pT2[i % 2]
                    pk_ = f"pT{i % 2}"
                    if h == 7 and kb == 0:
                        P.dma("sp", "x0", hres[:, :, :], src_h(t0, TS, s), writes=["hres0"])
                    for m in range(2):
                        for j in range(4):
                            a = m * 4 + j
                            at, ak_, c0 = accv(a)
                            b.mm(at[:, c0:c0 + 129], p_[:, m * 512 + j * 128:m * 512 + (j + 1) * 128], V14[:, kb, h, :],
                                 kb == 0 and a % 3 == 0, kb == 33, [pk_, "V1"], [ak_], skip=True)
                    if kb == 24:
                        if h > 0:
                            transposes(h - 1)
                        elif ti > 0:
                            transposes(7)
                            do_outproj(i, ti - 1)
                    if kb != 33:
                        return
                    hd = cnt["hd"]
                    cnt["hd"] += 1
                    aS = accS[hd % 2]
                    aSk = f"accS{hd % 2}"
                    b.copy("dve", aS[:, 0:387], accb[0][:, 0:387], ["acc0"], [aSk])
                    b.copy("dve", aS[:, 387:774], accb[1][:, 0:387], ["acc1"], [aSk])
                    b.copy("dve", aS[:, 774:1032], accb[2][:, 0:258], ["acc2"], [aSk])
                    aS3 = aS[:, :].rearrange("p (a e) -> p a e", e=129)
                    r_ = rz[:, (hd % 2) * 16:(hd % 2) * 16 + 16]
                    rk_ = f"rz{hd % 2}"
                    b.recip(r_[:, 0:8], aS3[:, :, 128], [aSk], [rk_])
                    b.ts("dve", r_[:, 4:8], r_[:, 4:8], neglam[:, 0:1], None, ALU.mult, None, [rk_, "neglam"], [rk_])
                    for j in range(4):
                        c1 = j * 129
                        c2 = (4 + j) * 129
                        tm, tmk = tmp[j % 2], f"tmp{j % 2}"
                        o_, obk = ob[j], f"ob{j}"
                        b.ts("dve", tm[:, :], aS[:, c2:c2 + 128], r_[:, 4 + j:5 + j], None, ALU.mult, None, [aSk, rk_], [tmk])
                        b.stt(o_[:, :], aS[:, c1:c1 + 128], r_[:, j:j + 1], tm[:, :], ALU.mult, ALU.add, [aSk, rk_, tmk], [obk])
                        b.stt(junk[:, :], o_[:, :], 1.0, o_[:, :], ALU.mult, ALU.mult, [obk], ["junk", (rk_, "ss")],
                              accum_out=r_[:, 8 + j:9 + j])
                    b.ts("dve", r_[:, 8:12], r_[:, 8:12], 1.0 / 128, EPS, ALU.mult, ALU.add, [(rk_, "ss")], [(rk_, "ss")])
                    b.tt("pool", r_[:, 12:16], r_[:, 8:12], nhalf[:, 0:1].broadcast_to([128, 4]), ALU.pow,
                         [(rk_, "ss"), "nhalf"], [(rk_, "rstd")])
                    for j in range(4):
                        o_, obk = ob[j], f"ob{j}"
                        b.stt(O_sb[:, j, h * 128:(h + 1) * 128], o_[:, :], r_[:, 12 + j:13 + j], subg2[:, :], ALU.mult, ALU.mult,
                              [obk, (rk_, "rstd"), "subg2"], [("O_sb", h)])
                    if h == 7 and ti == 7:
                        transposes(7)
                        do_outproj(i, ti)

                def do_outproj(i, tj):
                    t0, TS, s = TILES[tj]
                    sp_ = sps2[i % 2]
                    sq_ = sps2[(i + 1) % 2]
                    outproj_tile(1, Wo, OT_t, TS, s, t0, [sp_[:, 0:512], sp_[:, 512:1024], sq_[:, 0:512], sq_[:, 512:1024]],
                                 [(f"sps{i % 2}", 0), (f"sps{i % 2}", 1), (f"sps{(i + 1) % 2}", 0),
                                  (f"sps{(i + 1) % 2}", 1)], hres, "hres0")

                n = len(steps)
                for i in range(n + 1):
                    if i < n:
                        emit_S(i)
                    if i - 1 >= 0:
                        emit_rest(i - 1)
                P.barrier()
                P.emit(block)

        def phase_mlp(l, final):
            with ExitStack() as ps:
                W1 = sb(ps, "W1", [128, 8, 4 * D], BF16)
                W2 = sb(ps, "W2", [128, 32, D], BF16)
                xts = [sb(ps, f"mx{i}", [128, 8, 512]) for i in range(2)]
                m_t = sb(ps, "m_t", [128, 8, 512], BF16)
                h1 = sb(ps, "h1", [128, 16, 512], BF16)
                rr = [sb(ps, f"rr{i}", [128, 512]) for i in range(2)]
                sqb = [sb(ps, f"msq{i}", [128, 512], BF16) for i in range(2)] + \
                      [sb(ps, f"mnt{i}", [128, 512]) for i in range(2)]
                rtmp = sb(ps, "mrtmp", [128, 512])
                rstd = sb(ps, "mrstd", [128, 512])
                ssb = psb(ps, "mss")
                hps = [psb(ps, f"hps{i}") for i in range(2)]
                yps = [psb(ps, f"yps{i}") for i in range(2)]
                block = ps.enter_context(nc.Block())
                for k in range(KC):
                    P.dma("pool", f"m{k % 8}", W1[:, k, :], w1b[l, k * 128:(k + 1) * 128, :], reads=[("w1b", l, k)],
                          writes=[("W1", k)])
                for k4 in range(8):
                    P.dma("pool", f"m{k4 % 8}", W2[:, k4 * 4:(k4 + 1) * 4, :],
                          w2b[l, k4 * 512:(k4 + 1) * 512, :].rearrange("(k p) n -> p k n", p=128),
                          reads=[("w2b", l, k) for k in range(k4 * 4, k4 * 4 + 4)], writes=[("W2", k4 // 4)])
                if l == 0:
                    precast(1)
                tiles = TILES[:8] if final else TILES
                fi = 0
                yi = 0
                def p3_load(ti):
                    t0, TS, s = tiles[ti]
                    P.dma("sp", f"x{ti % 2}", xts[ti % 2][:, :, 0:TS], src_h(t0, TS, s), writes=[f"mx{ti % 2}"])
                p3_load(0)
                for ti, (t0, TS, s) in enumerate(tiles):
                    xt = xts[ti % 2]
                    xk = f"mx{ti % 2}"
                    if ti + 1 < len(tiles):
                        p3_load(ti + 1)
                    normmod(xt, xk, m_t, "m_t", TS, l, 1, s, sqb, ssb, rtmp, rstd)
                    for half in range(2):
                        for fcl in range(16):
                            fc = half * 16 + fcl
                            hp_ = hps[fi % 2]
                            hk_ = f"hps{fi % 2}"
                            for k in range(KC):
                                b.mm(hp_[:, 0:TS], W1[:, k, fc * 128:(fc + 1) * 128], m_t[:, k, 0:TS], k == 0,
                                     k == KC - 1, [("W1", k), ("m_t", k)], [hk_])
                            r_ = rr[fi % 2]
                            b.act(r_[:, 0:TS], hp_[:, 0:TS], AF.Relu, [hk_], [f"rr{fi % 2}"])
                            b.tt("dve", h1[:, fcl, 0:TS], r_[:, 0:TS], r_[:, 0:TS], ALU.mult, [f"rr{fi % 2}"],
                                 [("h1", fcl)])
                            fi += 1
                        for dc in range(KC):
                            yp = yps[yi % 2]
                            yk = f"yps{yi % 2}"
                            for fcl in range(16):
                                b.mm(yp[:, 0:TS], W2[:, half * 16 + fcl, dc * 128:(dc + 1) * 128], h1[:, fcl, 0:TS],
                                     fcl == 0, fcl == 15, [("W2", half), ("h1", fcl)], [yk])
                            b.stt(xt[:, dc, 0:TS], yp[:, 0:TS], modv(l, 5, dc, s), xt[:, dc, 0:TS], ALU.mult, ALU.add,
                                  [yk, xk, "mod"], [xk])
                            yi += 1
                    if not final:
                        P.dma("sp", "hst", fm(hS)[:, :, t0:t0 + TS], xt[:, :, 0:TS], reads=[xk], writes=["hS_dst"])
                    else:
                        for k in range(KC):
                            q = sqb[k % 2]
                            b.act(q[:, 0:TS], xt[:, k, 0:TS], AF.Square, [xk], [f"sq{k % 2}"])
                            b.mm(ssb[:, 0:TS], ones[:, :], q[:, 0:TS], k == 0, k == KC - 1, [f"sq{k % 2}", "ones"],
                                 ["ssb"])
                        b.act(rtmp[:, 0:TS], ssb[:, 0:TS], AF.Sqrt, ["ssb", "epsT"], ["rtmp"], scale=1.0 / D,
                              bias=epsT[:, 0:1])
                        b.recip(rstd[:, 0:TS], rtmp[:, 0:TS], ["rtmp"], ["rstd"])
                        for k in range(KC):
                            b.stt(xt[:, k, 0:TS], xt[:, k, 0:TS], gfs[:, k:k + 1], rstd[:, 0:TS], ALU.mult, ALU.mult,
                                  [xk, "rstd", "gfs"], [xk])
                        P.dma("sp", "hst", fm(outT)[:, :, t0:t0 + TS], xt[:, :, 0:TS], reads=[xk], writes=["outT"])
                P.barrier()
                P.emit(block)

        ch0 = [(c * 128, "q", c, True, True, True) for c in range(4)] + [(512, "k", 0, True, False, True)] + \
              [(640 + c * 128, "q", 4 + c, False, True, True) for c in range(4)] + [(1152, "k", 1, False, False, True)]
        ch1 = [(c * 128, "q", c, True, False, False) for c in range(8)] + \
              [(1024 + c * 128, "k", c, True, False, True) for c in range(8)]
        if max_phase >= 1:
            phase_p1(0, w_in0, 1536, ch0, 1280, 4, 64, src_x)
        if max_phase >= 2:
            phase_l0_attn()
        if max_phase >= 3:
            phase_mlp(0, False)
        if max_phase >= 4:
            phase_p1(1, w_in1, 3072, ch1, 2048, 8, 128, src_h)
        if max_phase >= 5:
            phase_l1_attn()
        if max_phase >= 6:
            phase_mlp(1, True)
        with ExitStack() as ps:
            block = ps.enter_context(nc.Block())
            P.barrier()
            P.emit(block)
    return nc


def _rope_tables_host():
    t = np.arange(NL)
    row = (t // 64).astype(np.float32)
    col = (t % 64).astype(np.float32)
    inv = (10000.0 ** (-np.arange(16, dtype=np.float32) / 16)).astype(np.float32)
    ar = row[:, None] * inv[None, :]
    ac = col[:, None] * inv[None, :]
    ang = np.concatenate([ar, ar, ac, ac], axis=-1).astype(np.float32)
    cos = np.cos(ang).astype(np.float32).T
    sin = np.sin(ang).astype(np.float32).T
    sign = np.where((np.arange(64) % 32) < 16, -1.0, 1.0).astype(np.float32)[:, None]
    sin_s = sin * sign
    cos2 = np.concatenate([cos, cos], 0)
    sin2 = np.concatenate([sin_s, sin_s], 0)
    return np.stack([cos2, sin2, cos2 * 0.125, sin2 * 0.125]).astype(np.float32)


def _consts_host():
    ident = np.eye(128, dtype=np.float32)
    perm = np.zeros((128, 128), np.float32)
    for m in range(128):
        partner = m + 16 if (m % 32) < 16 else m - 16
        perm[partner, m] = 1.0
    return np.concatenate([ident, perm], axis=1)


def _bias_tables(rpb):
    entries = []
    kl = np.arange(128)[:, None]
    ql = np.arange(128)[None, :]
    lower = np.where(kl >= ql, 0.0, NEG).astype(np.float32)
    upper = np.where(kl <= ql, 0.0, NEG).astype(np.float32)
    entries.append(np.repeat(lower[:, None, :], 4, axis=1))
    entries.append(np.repeat(upper[:, None, :], 4, axis=1))
    a_keys = []
    for i in range(32):
        l = []
        if i - 1 >= 0:
            l.append((i - 1, 0))
        l.append((i, None))
        if i + 1 < 32:
            l.append((i + 1, 1))
        a_keys.append(l)
    cache = {}
    b_keys = []
    for i in range(32):
        r = 2 * i + (np.arange(128) // 64)
        cq = np.arange(128) % 64
        rs = np.clip(r - 4, 0, 56)
        cs = np.clip(cq - 8, 0, 48)
        l = []
        for kb in range(32):
            krow = 2 * kb + (np.arange(128) // 64)
            kcol = np.arange(128) % 64
            valid = ((krow[:, None] >= rs[None, :]) & (krow[:, None] < rs[None, :] + 8) &
                     (kcol[:, None] >= cs[None, :]) & (kcol[:, None] < cs[None, :] + 16))
            if not valid.any():
                continue
            roff = np.clip(krow[:, None] - r[None, :] + 7, 0, 14)
            coff = np.clip(kcol[:, None] - cq[None, :], -15, 15) + 15
            key = (valid.tobytes(), np.where(valid, roff, 0).tobytes(), np.where(valid, coff, 0).tobytes())
            if key not in cache:
                cache[key] = len(entries)
                for kv in range(2):
                    g = rpb[kv * 4:(kv + 1) * 4][:, roff, coff]
                    m = np.where(valid[None], g, np.float32(NEG)).astype(np.float32)
                    entries.append(np.transpose(m, (1, 0, 2)))
            l.append((kb, cache[key]))
        b_keys.append(l)
    tab = np.stack(entries, axis=1).reshape(128, -1).astype(np.float32)
    return np.ascontiguousarray(tab), len(entries), a_keys, b_keys


def _bcast(v, n=128):
    return np.ascontiguousarray(np.broadcast_to(np.asarray(v, np.float32).reshape(1, -1), (n, np.asarray(v).size)))


def _key_structure():
    rpb0 = np.zeros((8, 15, 31), np.float32)
    _, n, a_keys, b_keys = _bias_tables(rpb0)
    return n, a_keys, b_keys


_PROG_CACHE = {}


def _prepare_inputs(x, c, ctx, c_ctx, ada_w, ada_b, norm1_g, norm2_g, even_w_in, even_w_out, a_sink, b_rpb,
                    odd_w_in, odd_w_out, lam_q1, lam_k1, lam_q2, lam_k2, subln_g, mlp_w1, mlp_w2, final_g):
    f = np.float32
    tab, n_bias, a_keys, b_keys = _bias_tables(np.asarray(b_rpb[0], f))
    win = np.asarray(even_w_in[0], f)
    aq, ak, av, bq, bk, bv = win[:, 0:512], win[:, 512:640], win[:, 640:768], win[:, 768:1280], win[:, 1280:1408], \
        win[:, 1408:1536]

    def qperm(q):
        cols = []
        for cch in range(4):
            cols.append(q[:, cch * 64:(cch + 1) * 64])
            cols.append(q[:, (4 + cch) * 64:(5 + cch) * 64])
        return np.concatenate(cols, axis=1)
    w_in0 = np.ascontiguousarray(np.concatenate([qperm(aq), ak, qperm(bq), bk, av, bv], axis=1))
    shared = {
        "ada_w": np.ascontiguousarray(ada_w, f),
        "ada_b": np.ascontiguousarray(np.repeat(np.asarray(ada_b, f).reshape(2, 48, 128).transpose(2, 0, 1)[..., None], 2,
                                                axis=-1).reshape(128, 192)),
        "g12": np.ascontiguousarray(np.stack([np.asarray(norm1_g, f), np.asarray(norm2_g, f)], axis=1)
                                    .reshape(2, 2, 8, 128).transpose(3, 0, 1, 2).reshape(128, 32)),
        "gfin": np.ascontiguousarray(np.asarray(final_g, f).reshape(8, 128).T),
        "w_in0": w_in0,
        "w_out0": np.ascontiguousarray(even_w_out[0], f),
        "w_in1": np.ascontiguousarray(odd_w_in[0], f),
        "w_out1": np.ascontiguousarray(odd_w_out[0], f),
        "w1": np.ascontiguousarray(mlp_w1, f),
        "w2": np.ascontiguousarray(mlp_w2, f),
        "sinkb": _bcast(a_sink[0]),
        "lamv": _bcast(np.concatenate([lam_q1[0], lam_k1[0], lam_q2[0], lam_k2[0]])),
        "subg": _bcast(subln_g[0]),
        "consts": _consts_host(),
        "rope": _rope_tables_host(),
        "biasT": tab,
    }
    in_maps = []
    for bb in range(8):
        m = dict(shared)
        m["xT"] = np.ascontiguousarray(np.asarray(x[bb], f).T)
        m["cT"] = np.ascontiguousarray(np.asarray(ctx[bb], f).T)
        cv = np.stack([np.asarray(c[bb], f).reshape(8, 128).T, np.asarray(c_ctx, f).reshape(8, 128).T], axis=-1)
        m["cvec"] = np.ascontiguousarray(cv.reshape(128, 16))
        in_maps.append(m)
    return in_maps, n_bias, a_keys, b_keys


def kernel(**inputs):
    in_maps, n_bias, a_keys, b_keys = _prepare_inputs(**inputs)
    key = ("main", n_bias)
    if key not in _PROG_CACHE:
        _PROG_CACHE[key] = build_program(n_bias, a_keys, b_keys)
    nc = _PROG_CACHE[key]
    res = run_bass_kernel_spmd(nc, in_maps, core_ids=list(range(8)))
    out = np.stack([np.ascontiguousarray(r["outT"].T) for r in res.results], axis=0)
    return out.astype(np.float32)
```

```python
import math
from contextlib import ExitStack
import numpy as np
import concourse.bass as bass
import concourse.mybir as mybir
from concourse.bass_utils import run_bass_kernel_spmd

F32 = mybir.dt.float32
BF16 = mybir.dt.bfloat16
AF = mybir.ActivationFunctionType
ALU = mybir.AluOpType

D = 1024
NL = 4096
NCX = 256
NT = NL + NCX
KC = 8
EPS = 1e-6
NEG = -30000.0
LAM_INIT = 0.8 - 0.6 * math.exp(-0.3 * 1)
TILES = [(t * 512, 512, 0) for t in range(8)] + [(NL, NCX, 1)]

ENGS = ("pe", "act", "dve", "pool", "sp")
SAME_ENGINE_SYNC = {"pe": False, "act": True, "dve": True, "pool": True, "sp": False}


class Prog:
    def __init__(self, nc, sems):
        self.nc = nc
        self.esem = {e: sems[i] for i, e in enumerate(("pe", "act", "dve", "pool"))}
        self.dsem_pool = list(sems[4:])
        self.dsem = {}
        self.count = {}
        self.semobj = {}
        for e, s in self.esem.items():
            self.count[id(s)] = 0
            self.semobj[id(s)] = s
        self.ops = {e: [] for e in ENGS}
        self.seen = {e: {} for e in ENGS}
        self.last_w = {}
        self.readers = {}
        self.n_ops = 0

    def dma_sem(self, name):
        if name not in self.dsem:
            s = self.dsem_pool.pop()
            self.dsem[name] = s
            self.count[id(s)] = 0
            self.semobj[id(s)] = s
        return self.dsem[name]

    def _need(self, eng, ev, waits):
        if ev is None:
            return
        sid, val = ev
        if eng in self.esem and sid == id(self.esem[eng]) and not SAME_ENGINE_SYNC[eng]:
            return
        if self.seen[eng].get(sid, 0) >= val:
            return
        if waits.get(sid, 0) < val:
            waits[sid] = val

    def _deps(self, eng, reads, writes):
        waits = {}
        for k in reads:
            self._need(eng, self.last_w.get(k), waits)
        for k in writes:
            self._need(eng, self.last_w.get(k), waits)
            for ev in self.readers.get(k, ()):
                self._need(eng, ev, waits)
        for sid, val in waits.items():
            self.seen[eng][sid] = val
        return [(self.semobj[sid], val) for sid, val in waits.items()]

    def _commit(self, ev, reads, writes):
        for k in writes:
            self.last_w[k] = ev
            self.readers[k] = []
        for k in reads:
            lst = self.readers.setdefault(k, [])
            for i, (sid, val) in enumerate(lst):
                if sid == ev[0]:
                    lst[i] = (sid, max(val, ev[1]))
                    break
            else:
                lst.append(ev)

    def op(self, eng, fn, reads=(), writes=()):
        waits = self._deps(eng, reads, writes)
        s = self.esem[eng]
        self.count[id(s)] += 1
        ev = (id(s), self.count[id(s)])
        self.ops[eng].append((waits, fn, (s, 1)))
        self._commit(ev, reads, writes)
        self.n_ops += 1

    def dma(self, q, semname, out, in_, reads=(), writes=()):
        s = self.dma_sem(semname)
        waits = self._deps(q, reads, writes)
        if self.count[id(s)] > 0:
            w = {}
            self._need(q, (id(s), self.count[id(s)]), w)
            for sid, val in w.items():
                self.seen[q][sid] = val
                waits.append((self.semobj[sid], val))
        self.count[id(s)] += 16
        ev = (id(s), self.count[id(s)])

        kw = dict(max_dma_last_dim=4096) if q == "pool" else {}

        def fn(e, out=out, in_=in_, kw=kw):
            return e.dma_start(out=out, in_=in_, **kw)
        self.ops[q].append((waits, fn, (s, 16)))
        self._commit(ev, reads, writes)
        self.n_ops += 1

    def barrier(self):
        for e in ENGS:
            waits = []
            for sid, val in self.count.items():
                if val > 0 and self.seen[e].get(sid, 0) < val:
                    if e in self.esem and sid == id(self.esem[e]):
                        continue
                    waits.append((self.semobj[sid], val))
                    self.seen[e][sid] = val
            if waits:
                self.ops[e].append((waits, None, None))

    def emit(self, block):
        prog = self

        def mk(ename):
            oplist = prog.ops[ename]

            def body(eng):
                for waits, fn, inc in oplist:
                    for s, v in waits:
                        eng.wait_ge(s, v)
                    if fn is not None:
                        fn(eng).then_inc(inc[0], inc[1])
            return body
        block.tensor(mk("pe"))
        block.scalar(mk("act"))
        block.vector(mk("dve"))
        block.gpsimd(mk("pool"))
        block.sync(mk("sp"))
        self.ops = {e: [] for e in ENGS}


class B:
    def __init__(self, nc, P):
        self.nc = nc
        self.P = P

    def mm(self, out, lhsT, rhs, start, stop, reads, writes, skip=False):
        self.P.op("pe", lambda e: e.matmul(out, lhsT=lhsT, rhs=rhs, start=start, stop=stop, skip_group_check=skip),
                  reads, writes)

    def tr(self, out, in_, ident, reads, writes):
        self.P.op("pe", lambda e: e.transpose(out=out, in_=in_, identity=ident), reads, writes)

    def act(self, out, in_, func, reads, writes, scale=1.0, bias=0.0):
        self.P.op("act", lambda e: e.activation(out=out, in_=in_, func=func, bias=bias, scale=scale), reads, writes)

    def tt(self, eng, out, in0, in1, op, reads, writes):
        self.P.op(eng, lambda e: e.tensor_tensor(out=out, in0=in0, in1=in1, op=op), reads, writes)

    def ts(self, eng, out, in0, s1, s2, op0, op1, reads, writes):
        if s2 is None:
            self.P.op(eng, lambda e: e.tensor_scalar(out=out, in0=in0, scalar1=s1, scalar2=None, op0=op0), reads, writes)
        else:
            self.P.op(eng, lambda e: e.tensor_scalar(out=out, in0=in0, scalar1=s1, scalar2=s2, op0=op0, op1=op1),
                      reads, writes)

    def stt(self, out, in0, scalar, in1, op0, op1, reads, writes, accum_out=None):
        if accum_out is None:
            self.P.op("dve", lambda e: e.scalar_tensor_tensor(out=out, in0=in0, scalar=scalar, in1=in1, op0=op0, op1=op1),
                      reads, writes)
        else:
            self.P.op("dve", lambda e: e.scalar_tensor_tensor(out=out, in0=in0, scalar=scalar, in1=in1, op0=op0, op1=op1,
                                                             accum_out=accum_out), reads, writes)

    def copy(self, eng, out, in_, reads, writes):
        if eng == "act":
            self.act(out, in_, AF.Copy, reads, writes)
        else:
            self.P.op(eng, lambda e: e.tensor_copy(out=out, in_=in_), reads, writes)

    def memset(self, eng, ap, val, writes):
        self.P.op(eng, lambda e: e.memset(ap, val), (), writes)

    def recip(self, out, in_, reads, writes):
        self.P.op("dve", lambda e: e.reciprocal(out=out, in_=in_), reads, writes)


def build_program(n_bias, a_keys, b_keys, dbg=False, max_phase=99):
    nc = bass.Bass("TRN2", target_bir_lowering=False)

    def din(name, shape, dt=F32):
        return nc.dram_tensor(name, list(shape), dt, kind="ExternalInput").ap()

    xT = din("xT", [D, NL])
    cT = din("cT", [D, NCX])
    cvec = din("cvec", [128, 16])
    ada_w = din("ada_w", [2, D, 6 * D])
    ada_b = din("ada_b", [128, 192])
    g12 = din("g12", [128, 32])
    gfin = din("gfin", [128, 8])
    w_in0 = din("w_in0", [D, 1536])
    w_out0 = din("w_out0", [D, D])
    w_in1 = din("w_in1", [D, 3072])
    w_out1 = din("w_out1", [D, D])
    w1 = din("w1", [2, D, 4 * D])
    w2 = din("w2", [2, 4 * D, D])
    sinkb = din("sinkb", [128, 8])
    lamv = din("lamv", [128, 256])
    subg = din("subg", [128, 128])
    consts = din("consts", [128, 256])
    rope = din("rope", [4, 128, NL])
    biasT = din("biasT", [128, n_bias * 512])
    outT = nc.dram_tensor("outT", [D, NL], F32, kind="ExternalOutput").ap()
    skind = "ExternalOutput" if dbg else "Internal"
    hS = nc.dram_tensor("hS", [D, NT], F32, kind=skind).ap()
    qT_d = nc.dram_tensor("qT_d", [D, NT], BF16, kind=skind).ap()
    kT_d = nc.dram_tensor("kT_d", [D, NT], BF16, kind=skind).ap()
    v_d = nc.dram_tensor("v_d", [NT, 8 * 129], BF16, kind=skind).ap()
    w1b = nc.dram_tensor("w1b", [2, D, 4 * D], BF16).ap()
    w2b = nc.dram_tensor("w2b", [2, 4 * D, D], BF16).ap()
    dbg_out = {}
    if dbg:
        dbg_out["mod"] = nc.dram_tensor("dbg_mod", [128, 192], F32, kind="ExternalOutput").ap()

    def fm(ap):
        return ap.rearrange("(k p) t -> p k t", p=128)

    with ExitStack() as es:
        sems = [es.enter_context(nc.semaphore(f"s{i}")) for i in range(56)]
        P = Prog(nc, sems)
        b = B(nc, P)

        uid = [0]

        def sb(stack, name, shape, dt=F32):
            uid[0] += 1
            return stack.enter_context(nc.sbuf_tensor(f"{name}_{uid[0]}", list(shape), dt))

        def psb(stack, name, shape=(128, 512), dt=F32):
            uid[0] += 1
            return stack.enter_context(nc.psum_tensor(f"{name}_{uid[0]}", list(shape), dt))

        mod = sb(es, "mod", [128, 192])
        gm = sb(es, "gm", [128, 64])
        g12s = sb(es, "g12s", [128, 32])
        gfs = sb(es, "gfs", [128, 8])
        ident = sb(es, "ident", [128, 128], BF16)
        perm = sb(es, "perm", [128, 128], BF16)
        ones = sb(es, "ones", [128, 128], BF16)
        nhalf = sb(es, "nhalf", [128, 1])
        expsink = sb(es, "expsink", [128, 8])
        neglam = sb(es, "neglam", [128, 1])
        subg2 = sb(es, "subg2", [128, 128])
        epsT = sb(es, "epsT", [128, 1])

        mod4 = mod[:, :].rearrange("p (l j s) -> p l j s", l=2, j=48, s=2)
        gm5 = gm[:, :].rearrange("p (l w k s) -> p l w k s", l=2, w=2, k=8, s=2)
        g124 = g12s[:, :].rearrange("p (l w k) -> p l w k", l=2, w=2, k=8)

        def modv(l, idx, k, s):
            return mod4[:, l, idx * 8 + k, s:s + 1]

        def gmv(l, which, k, s):
            return gm5[:, l, which, k, s:s + 1]

        with ExitStack() as ps:
            cv = sb(ps, "cv", [128, 16])
            s_bf = sb(ps, "s_bf", [128, 16], BF16)
            ab = sb(ps, "ab", [128, 192])
            lv = sb(ps, "lv", [128, 256])
            lt = sb(ps, "lt", [128, 128])
            ls = sb(ps, "ls", [128, 4])
            sk = sb(ps, "sk", [128, 8])
            sg = sb(ps, "sg", [128, 128])
            wbuf = [sb(ps, f"wbuf{i}", [128, 8, 3072], BF16) for i in range(2)]
            pm = psb(ps, "pm")
            block = ps.enter_context(nc.Block())

            P.dma("sp", "c0", cv[:, :], cvec, writes=["cv"])
            P.dma("sp", "c1", ab[:, :], ada_b, writes=["ab"])
            P.dma("sp", "c2", g12s[:, :], g12, writes=["g12s"])
            P.dma("sp", "c3", gfs[:, :], gfin, writes=["gfs"])
            P.dma("sp", "c0", sk[:, :], sinkb, writes=["sk"])
            P.dma("sp", "c1", lv[:, :], lamv, writes=["lv"])
            P.dma("sp", "c2", sg[:, :], subg, writes=["sg"])
            P.dma("pool", "c4", ident[:, :], consts[:, 0:128], writes=["ident"])
            P.dma("pool", "c5", perm[:, :], consts[:, 128:256], writes=["perm"])
            b.memset("dve", ones[:, :], 1.0, ["ones"])
            b.memset("dve", nhalf[:, :], -0.5, ["nhalf"])
            b.memset("dve", epsT[:, :], EPS, ["epsT"])
            b.act(s_bf[:, :], cv[:, :], AF.Silu, ["cv"], ["s_bf"])
            b.act(expsink[:, :], sk[:, :], AF.Exp, ["sk"], ["expsink"])
            s3 = s_bf[:, :].rearrange("p (k s) -> p k s", s=2)
            li = 0
            for l in range(2):
                for half in range(2):
                    wb = wbuf[li % 2]
                    wk = f"wbuf{li % 2}"
                    for k in range(KC):
                        P.dma("pool", f"w{k % 16}", wb[:, k, :],
                              ada_w[l, k * 128:(k + 1) * 128, half * 3072:(half + 1) * 3072], writes=[(wk, k)])
                    for jj in range(24):
                        col = (l * 48 + half * 24 + jj) * 2
                        for k in range(KC):
                            b.mm(pm[:, col:col + 2], wb[:, k, jj * 128:(jj + 1) * 128], s3[:, k, :],
                                 k == 0, k == KC - 1, [(wk, k), "s_bf"], ["pm"])
                    li += 1
            b.tt("dve", mod[:, :], pm[:, 0:192], ab[:, :], ALU.add, ["pm", "ab"], ["mod"])
            for l in range(2):
                for w in range(2):
                    for s in range(2):
                        sc = mod4[:, l, (1 + 3 * w) * 8:(2 + 3 * w) * 8, s]
                        b.stt(gm5[:, l, w, :, s], sc, 1.0, g124[:, l, w, :], ALU.add, ALU.mult,
                              ["mod", "g12s"], ["gm"])
            lv3 = lv[:, :].rearrange("p (a d) -> p a d", d=64)
            for i in range(2):
                b.stt(lt[:, 0:64], lv3[:, 2 * i, :], 1.0, lv3[:, 2 * i + 1, :], ALU.mult, ALU.mult,
                      ["lv"], ["lt", "ls"], accum_out=ls[:, i:i + 1])
            b.act(ls[:, 2:4], ls[:, 0:2], AF.Exp, ["ls"], ["ls"])
            b.tt("dve", neglam[:, :], ls[:, 3:4], ls[:, 2:3], ALU.subtract, ["ls"], ["neglam"])
            b.ts("dve", neglam[:, :], neglam[:, :], -LAM_INIT, None, ALU.add, None, ["neglam"], ["neglam"])
            b.ts("dve", subg2[:, :], sg[:, :], 1.0 - LAM_INIT, None, ALU.mult, None, ["sg"], ["subg2"])
            if dbg:
                P.dma("sp", "dbg", dbg_out["mod"], mod[:, :], reads=["mod"], writes=["dbg_mod"])
            P.barrier()
            P.emit(block)

        def normmod(xt, xk, a_out, ak, TS, l, which, s, sqb, ssb, rtmp, rstd):
            for k in range(KC):
                q = sqb[k % 2]
                b.act(q[:, 0:TS], xt[:, k, 0:TS], AF.Square, [xk], [f"sq{k % 2}"])
                b.mm(ssb[:, 0:TS], ones[:, :], q[:, 0:TS], k == 0, k == KC - 1, [f"sq{k % 2}", "ones"], ["ssb"])
            b.act(rtmp[:, 0:TS], ssb[:, 0:TS], AF.Sqrt, ["ssb", "epsT"], ["rtmp"], scale=1.0 / D, bias=epsT[:, 0:1])
            b.recip(rstd[:, 0:TS], rtmp[:, 0:TS], ["rtmp"], ["rstd"])
            for k in range(KC):
                t = sqb[2 + k % 2]
                b.stt(t[:, 0:TS], xt[:, k, 0:TS], gmv(l, which, k, s), rstd[:, 0:TS], ALU.mult, ALU.mult,
                      [xk, "rstd", "gm"], [f"nt{k % 2}"])
                b.act(a_out[:, k, 0:TS], t[:, 0:TS], AF.Identity, [f"nt{k % 2}", "mod"], [(ak, k)],
                      bias=modv(l, 3 * which, k, s))

        def phase_p1(l, W, NC_, fm_chunks, v_col0, nh, dv, src_fn):
            with ExitStack() as ps:
                Wb = sb(ps, "p1W", [128, 8, NC_], BF16)
                xts = [sb(ps, f"p1x{i}", [128, 8, 512]) for i in range(2)]
                a_ts = [sb(ps, f"p1a{i}", [128, 8, 512], BF16) for i in range(2)]
                sqb = [sb(ps, f"p1sq{i}", [128, 512], BF16) for i in range(2)] + \
                      [sb(ps, f"p1nt{i}", [128, 512]) for i in range(2)]
                rtmp = sb(ps, "p1rtmp", [128, 512])
                rstd = sb(ps, "p1rstd", [128, 512])
                nq = sum(1 for c in fm_chunks if c[1] == "q")
                nk = sum(1 for c in fm_chunks if c[1] == "k")
                qst = sb(ps, "p1qst", [128, nq, 512], BF16)
                kst = sb(ps, "p1kst", [128, nk, 512], BF16)
                VW = nh * (dv + 1)
                vst = sb(ps, "p1vst", [128, 4, VW], BF16)
                ntab = 4 if l == 0 else 2
                tabs = [sb(ps, f"p1tab{i}", [128, ntab, 512]) for i in range(2)]
                q_sb = [sb(ps, f"p1qsb{i}", [128, 512], BF16) for i in range(2)]
                t1 = [sb(ps, f"p1t1{i}", [128, 512]) for i in range(2)]
                t2 = [sb(ps, f"p1t2{i}", [128, 512]) for i in range(2)]
                ssb = psb(ps, "p1ss")
                qps = [psb(ps, f"p1qps{i}") for i in range(2)]
                pps = [psb(ps, f"p1pps{i}") for i in range(2)]
                vps = [psb(ps, f"p1vps{i}") for i in range(2)]
                block = ps.enter_context(nc.Block())

                for k in range(KC):
                    P.dma("pool", f"w{k % 16}", Wb[:, k, :], W[k * 128:(k + 1) * 128, :], writes=[("p1W", k)])
                b.memset("dve", vst[:, :, :], 1.0, ["vst"])
                vst4 = vst[:, :, :].rearrange("p b (h e) -> p b h e", e=dv + 1)
                ci = 0
                vi = 0
                def p1_load(ti):
                    t0, TS, s = TILES[ti]
                    P.dma("sp", f"x{ti % 2}", xts[ti % 2][:, :, 0:TS], src_fn(t0, TS, s), writes=[f"p1x{ti % 2}"])
                    if s == 0:
                        P.dma("sp", f"t{ti % 2}", tabs[ti % 2][:, :, 0:TS],
                              rope[0:ntab, :, t0:t0 + TS].rearrange("a p t -> p a t"), writes=[f"p1tab{ti % 2}"])
                p1_load(0)
                for ti, (t0, TS, s) in enumerate(TILES):
                    xt = xts[ti % 2]
                    xk = f"p1x{ti % 2}"
                    tab = tabs[ti % 2]
                    tk = f"p1tab{ti % 2}"
                    if ti + 1 < len(TILES):
                        p1_load(ti + 1)
                    a_t = a_ts[ti % 2]
                    pa = f"p1a{ti % 2}"
                    normmod(xt, xk, a_t, pa, TS, l, 0, s, sqb, ssb, rtmp, rstd)
                    tail = None
                    for (col0, kind, dch, rp, qs, ctx_needed) in fm_chunks:
                        if s == 1 and not ctx_needed:
                            continue
                        qp = qps[ci % 2]
                        qk = f"p1qps{ci % 2}"
                        for k in range(KC):
                            b.mm(qp[:, 0:TS], Wb[:, k, col0:col0 + 128], a_t[:, k, 0:TS], k == 0, k == KC - 1,
                                 [("p1W", k), (pa, k)], [qk])
                        dst = (qst if kind == "q" else kst)[:, dch, 0:TS]
                        dk = ("p1st", kind, dch)
                        if rp and s == 0:
                            qsb = q_sb[ci % 2]
                            b.copy("act", qsb[:, 0:TS], qp[:, 0:TS], [qk], [f"qsb{ci % 2}"])

                            def mk_tail(c=ci, qsb=qsb, dst=dst, dk=dk, to=(2 if qs else 0)):
                                def f():
                                    pp = pps[c % 2]
                                    b.mm(pp[:, 0:TS], perm[:, :], qsb[:, 0:TS], True, True, [f"qsb{c % 2}", "perm"],
                                         [f"pps{c % 2}"])
                                    b.tt("dve", t1[c % 2][:, 0:TS], qsb[:, 0:TS], tab[:, to, 0:TS], ALU.mult,
                                         [f"qsb{c % 2}", tk], [f"t1{c % 2}"])
                                    b.tt("dve", t2[c % 2][:, 0:TS], pp[:, 0:TS], tab[:, to + 1, 0:TS], ALU.mult,
                                         [f"pps{c % 2}", tk], [f"t2{c % 2}"])
                                    b.tt("pool", dst, t1[c % 2][:, 0:TS], t2[c % 2][:, 0:TS], ALU.add,
                                         [f"t1{c % 2}", f"t2{c % 2}"], [dk])
                                return f
                            if tail is not None:
                                tail()
                            tail = mk_tail()
                        else:
                            b.act(dst, qp[:, 0:TS], AF.Copy, [qk], [dk], scale=(0.125 if qs else 1.0))
                            if tail is not None:
                                tail()
                                tail = None
                        ci += 1
                    if tail is not None:
                        tail()
                    nb = TS // 128
                    for jb in range(nb):
                        for hh in range((nh * dv) // 512 if nh * dv >= 512 else 1):
                            wcols = min(512, nh * dv)
                            vp = vps[vi % 2]
                            vk = f"p1vps{vi % 2}"
                            for k in range(KC):
                                b.mm(vp[:, 0:wcols], a_t[:, k, jb * 128:(jb + 1) * 128],
                                     Wb[:, k, v_col0 + hh * 512:v_col0 + hh * 512 + wcols], k == 0, k == KC - 1,
                                     [("p1W", k), (pa, k)], [vk])
                            hpc = wcols // dv
                            b.copy("dve" if vi % 2 else "act", vst4[:, jb, hh * hpc:(hh + 1) * hpc, 0:dv],
                                   vp[:, 0:wcols].rearrange("p (h d) -> p h d", d=dv), [vk], ["vst"])
                            vi += 1
                    nqs = sum(1 for c in fm_chunks if c[1] == "q" and (s == 0 or c[5]))
                    if nqs:
                        P.dma("sp", "stq", fm(qT_d)[:, 0:nq, t0:t0 + TS], qst[:, :, 0:TS],
                              reads=[("p1st", "q", i) for i in range(nq)], writes=["qT_d"])
                    P.dma("sp", "stk", fm(kT_d)[:, 0:nk, t0:t0 + TS], kst[:, :, 0:TS],
                          reads=[("p1st", "k", i) for i in range(nk)], writes=["kT_d"])
                    P.dma("sp", "stv", v_d[t0:t0 + TS, 0:VW].rearrange("(b p) f -> p b f", p=128),
                          vst[:, 0:nb, :], reads=["vst"], writes=["v_d"])
                P.barrier()
                P.emit(block)

        def outproj_tile(l, Wo, OT_t, TS, s, t0, ybanks, ykeys, ht, hk):
            for dc in range(KC):
                yb = ybanks[dc % len(ybanks)]
                yk = ykeys[dc % len(ybanks)]
                for c in range(KC):
                    b.mm(yb[:, 0:TS], Wo[:, c, dc * 128:(dc + 1) * 128], OT_t[:, c, 0:TS], c == 0, c == KC - 1,
                         ["Wo", "OT_t"], [yk])
                b.stt(ht[:, dc, 0:TS], yb[:, 0:TS], modv(l, 2, dc, s), ht[:, dc, 0:TS], ALU.mult, ALU.add,
                      [yk, hk, (hk, dc), "mod"], [(hk, dc)])
            P.dma("sp", "hst", fm(hS)[:, :, t0:t0 + TS], ht[:, :, 0:TS], reads=[hk] + [(hk, dc) for dc in range(KC)],
                  writes=["hS_dst"])

        def src_x(t0, TS, s):
            return fm(xT)[:, :, t0:t0 + TS] if s == 0 else fm(cT)[:, :, 0:TS]

        def src_h(t0, TS, s):
            return fm(hS)[:, :, t0:t0 + TS]

        def precast(l):
            for k in range(KC):
                P.dma("pool", f"w{k % 16}", w1b[l, k * 128:(k + 1) * 128, :], w1[l, k * 128:(k + 1) * 128, :],
                      writes=[("w1b", l, k)])
            for k in range(32):
                P.dma("pool", f"w{k % 16}", w2b[l, k * 128:(k + 1) * 128, :], w2[l, k * 128:(k + 1) * 128, :],
                      writes=[("w2b", l, k)])

        def phase_l0_attn():
            with ExitStack() as ps:
                KT0 = sb(ps, "KT0", [128, 2, NT], BF16)
                V0 = sb(ps, "V0", [128, 34, 4 * 65], BF16)
                bias = sb(ps, "bias", [128, n_bias, 512], BF16)
                Wo = sb(ps, "Wo", [128, 8, D], BF16)
                Qt = [sb(ps, f"Qt{i}", [128, 8, 512], BF16) for i in range(2)]
                NPT = 3
                pT = [sb(ps, f"pT{i}", [128, 512], BF16) for i in range(NPT)]
                O_sb = [sb(ps, f"O_sb{i}", [128, D], BF16) for i in range(2)]
                OT_t = sb(ps, "OT_t", [128, 8, 512], BF16)
                zt = sb(ps, "zt", [128, 8])
                rz = sb(ps, "rz", [128, 8])
                hts = [sb(ps, f"hres{i}", [128, 8, 512]) for i in range(2)]
                spsb = [psb(ps, f"sps{i}") for i in range(NPT)]
                accb = [psb(ps, f"acc{i}") for i in range(2)]
                tpb = psb(ps, "tp0", [128, 1024], BF16)
                ypb = [psb(ps, f"ypb{i}") for i in range(2)]
                block = ps.enter_context(nc.Block())

                P.dma("sp", "kt", KT0[:, :, :], fm(kT_d)[:, 0:2, :], writes=["KT0"])
                P.dma("sp", "vv", V0[:, :, :], v_d[:, 0:260].rearrange("(b p) f -> p b f", p=128), writes=["V0"])
                nbh = (n_bias + 1) // 2
                P.dma("pool", "w0", bias[:, 0:nbh, :], biasT[:, 0:nbh * 512].rearrange("p (n f) -> p n f", f=512),
                      writes=["bias"])
                P.dma("pool", "w1", bias[:, nbh:n_bias, :],
                      biasT[:, nbh * 512:n_bias * 512].rearrange("p (n f) -> p n f", f=512), writes=["bias"])
                for k in range(KC):
                    P.dma("pool", f"w{k % 16}", Wo[:, k, :], w_out0[k * 128:(k + 1) * 128, :], writes=["Wo"])
                precast(0)
                V04 = V0[:, :, :].rearrange("p b (h e) -> p b h e", e=65)

                steps = []
                for ti, (t0, TS, s) in enumerate(TILES):
                    nqb = TS // 128
                    for qbl in range(nqb):
                        qb = t0 // 128 + qbl
                        for g in range(4):
                            typ, kv = g // 2, g % 2
                            if s == 1:
                                klist = [(32, None), (33, None)]
                            else:
                                klist = (a_keys if typ == 0 else b_keys)[qb] + [(32, None), (33, None)]
                            for idx, (kb, be) in enumerate(klist):
                                steps.append(dict(ti=ti, t0=t0, TS=TS, s=s, qbl=qbl, g=g, typ=typ, kv=kv, idx=idx, kb=kb,
                                                  be=be, last=(idx == len(klist) - 1), first_tile=(qbl == 0 and g == 0 and idx == 0),
                                                  last_q=(g == 3 and idx == len(klist) - 1),
                                                  last_tile=(qbl == nqb - 1 and g == 3 and idx == len(klist) - 1)))
                grp = 0
                qbi = 0
                for st_ in steps:
                    st_["ai"] = grp
                    st_["oi"] = qbi
                    if st_["last"]:
                        grp += 1
                    if st_["last_q"]:
                        qbi += 1

                def load_q(ti):
                    t0, TS, s = TILES[ti]
                    P.dma("sp", f"q{ti % 2}", Qt[ti % 2][:, :, 0:TS], fm(qT_d)[:, :, t0:t0 + TS], writes=[f"Qt{ti % 2}"])

                def load_h(ti):
                    t0, TS, s = TILES[ti]
                    P.dma("sp", f"x{ti % 2}", hts[ti % 2][:, :, 0:TS], src_x(t0, TS, s), writes=[f"hres{ti % 2}"])
                load_q(0)
                load_h(0)

                def emit_S(i):
                    st_ = steps[i]
                    ti, typ, kv, kb, be = st_["ti"], st_["typ"], st_["kv"], st_["kb"], st_["be"]
                    if st_["first_tile"] and ti + 1 < len(TILES):
                        load_q(ti + 1)
                    Q = Qt[ti % 2]
                    qk_ = f"Qt{ti % 2}"
                    hp = kv * 64
                    qoff = st_["qbl"] * 128
                    sp_ = spsb[i % NPT]
                    sk_ = f"sps{i % NPT}"
                    b.mm(sp_[:, :].rearrange("p (c q) -> p c q", c=4), KT0[hp:hp + 64, typ, kb * 128:(kb + 1) * 128],
                         Q[hp:hp + 64, typ * 4:typ * 4 + 4, qoff:qoff + 128], True, be is None, ["KT0", qk_], [sk_])
                    if be is not None:
                        e = be if typ == 0 else be + kv
                        b.mm(sp_[:, :], ident[:, :], bias[:, e, :], False, True, ["bias", "ident"], [sk_])
                    b.act(pT[i % NPT][:, :], sp_[:, :], AF.Exp, [sk_], [f"pT{i % NPT}"])

                pending = []

                def emit_rest(i):
                    st_ = steps[i]
                    ti, typ, kv, kb, s, TS, t0 = st_["ti"], st_["typ"], st_["kv"], st_["kb"], st_["s"], st_["TS"], st_["t0"]
                    ai, oi = st_["ai"], st_["oi"]
                    if st_["first_tile"] and ti + 1 < len(TILES):
                        pending.append((i + 12, lambda ti=ti: load_h(ti + 1)))
                    acc = accb[ai % 2]
                    acck = f"acc{ai % 2}"
                    p_ = pT[i % NPT]
                    pk_ = f"pT{i % NPT}"
                    for j in range(4):
                        b.mm(acc[:, j * 65:(j + 1) * 65], p_[:, j * 128:(j + 1) * 128], V04[:, kb, typ * 2 + kv, :],
                             st_["idx"] == 0 and j == 0, st_["last"], [pk_, "V0"], [acck], skip=True)
                    if not st_["last"]:
                        return
                    Ob = O_sb[oi % 2]
                    ok_ = f"O_sb{oi % 2}"
                    acc3 = acc[:, 0:260].rearrange("p (j e) -> p j e", e=65)
                    zz = zt[:, (ai % 2) * 4:(ai % 2) * 4 + 4]
                    rr = rz[:, (ai % 2) * 4:(ai % 2) * 4 + 4]
                    zk, rk = f"zt{ai % 2}", f"rz{ai % 2}"
                    if typ == 0:
                        b.tt("dve", zz, acc3[:, :, 64], expsink[:, kv * 4:kv * 4 + 4], ALU.add, [acck, "expsink"], [zk])
                    else:
                        b.copy("dve", zz, acc3[:, :, 64], [acck], [zk])
                    b.recip(rr, zz, [zk], [rk])
                    base = typ * 512 + kv * 256
                    b.tt("dve", Ob[:, base:base + 256].rearrange("p (j d) -> p j d", d=64), acc3[:, :, 0:64],
                         rr.unsqueeze(2).broadcast_to([128, 4, 64]), ALU.mult, [acck, rk], [ok_])
                    if not st_["last_q"]:
                        return
                    qoff = st_["qbl"] * 128

                    def fin(Ob=Ob, ok_=ok_, qoff=qoff, st_=st_, TS=TS, s=s, t0=t0, ti=ti):
                        for c in range(KC):
                            b.tr(tpb[:, c * 128:(c + 1) * 128], Ob[:, c * 128:(c + 1) * 128], ident[:, :], [ok_, "ident"],
                                 ["tp0"])
                        b.copy("dve", OT_t[:, :, qoff:qoff + 128], tpb[:, :].rearrange("p (c q) -> p c q", c=8), ["tp0"],
                               ["OT_t"])
                        if st_["last_tile"]:
                            outproj_tile(0, Wo, OT_t, TS, s, t0, ypb, ["ypb0", "ypb1"], hts[ti % 2], f"hres{ti % 2}")
                    pending.append((i + 6, fin))

                DEPTH = NPT - 1
                n = len(steps)
                for i in range(n + DEPTH):
                    if i < n:
                        emit_S(i)
                    if i - DEPTH >= 0:
                        emit_rest(i - DEPTH)
                        while pending and pending[0][0] <= i - DEPTH:
                            pending.pop(0)[1]()
                while pending:
                    pending.pop(0)[1]()
                P.barrier()
                P.emit(block)

        def phase_l1_attn():
            with ExitStack() as ps:
                KT1 = sb(ps, "KT1", [128, 8, NT], BF16)
                V1 = sb(ps, "V1", [128, 34, 8 * 129], BF16)
                Wo = sb(ps, "Wo", [128, 8, D], BF16)
                Qc = [sb(ps, f"Qc{i}", [128, 512], BF16) for i in range(2)]
                pT2 = [sb(ps, f"pT{i}", [128, 1024], BF16) for i in range(2)]
                accS = [sb(ps, f"accS{i}", [128, 8 * 129]) for i in range(2)]
                O_sb = sb(ps, "O_sb", [128, 4, D], BF16)
                OT_t = sb(ps, "OT_t", [128, 8, 512], BF16)
                rz = sb(ps, "rz", [128, 32])
                tmp = [sb(ps, f"tmp{i}", [128, 128]) for i in range(2)]
                ob = [sb(ps, f"ob{i}", [128, 128]) for i in range(4)]
                junk = sb(ps, "junk", [128, 128])
                hres = sb(ps, "hres0", [128, 8, 512])
                sps2 = [psb(ps, f"sps{i}", (128, 1024)) for i in range(2)]
                accb = [psb(ps, f"acc{i}") for i in range(3)]
                tpb = psb(ps, "tp0", [128, 512], BF16)
                block = ps.enter_context(nc.Block())

                for c in range(KC):
                    P.dma("sp", f"kt{c % 2}", KT1[:, c, :], kT_d[c * 128:(c + 1) * 128, :], writes=["KT1"])
                v_r = v_d[:, :].rearrange("(b p) f -> p b f", p=128)
                for i in range(2):
                    P.dma("sp", f"vv{i}", V1[:, i * 17:(i + 1) * 17, :], v_r[:, i * 17:(i + 1) * 17, :], writes=["V1"])
                for k in range(KC):
                    P.dma("pool", f"w{k % 16}", Wo[:, k, :], w_out1[k * 128:(k + 1) * 128, :], writes=["Wo"])
                V14 = V1[:, :, :].rearrange("p b (h e) -> p b h e", e=129)

                def accv(a):
                    return accb[a // 3], f"acc{a // 3}", (a % 3) * 129

                steps = [(ti, h, kb) for ti in range(8) for h in range(8) for kb in range(34)]

                def load_q(n):
                    ti, h = n // 8, n % 8
                    t0 = TILES[ti][0]
                    P.dma("sp", f"q{n % 2}", Qc[n % 2][:, :], qT_d[h * 128:(h + 1) * 128, t0:t0 + 512], writes=[f"Qc{n % 2}"])
                load_q(0)

                def emit_S(i):
                    ti, h, kb = steps[i]
                    n = ti * 8 + h
                    if kb == 0 and n + 1 < 64:
                        load_q(n + 1)
                    Q = Qc[n % 2]
                    qk_ = f"Qc{n % 2}"
                    sp_ = sps2[i % 2]
                    for m in range(2):
                        b.mm(sp_[:, m * 512:(m + 1) * 512], KT1[m * 64:(m + 1) * 64, h, kb * 128:(kb + 1) * 128],
                             Q[m * 64:(m + 1) * 64, :], True, True, ["KT1", qk_], [(f"sps{i % 2}", m)])
                    b.act(pT2[i % 2][:, :], sp_[:, :], AF.Exp, [(f"sps{i % 2}", 0), (f"sps{i % 2}", 1)], [f"pT{i % 2}"],
                          scale=0.125)

                cnt = dict(ni=0, hd=0)

                def transposes(hh):
                    for j in range(4):
                        b.tr(tpb[:, j * 128:(j + 1) * 128], O_sb[:, j, hh * 128:(hh + 1) * 128], ident[:, :],
                             [("O_sb", hh), "ident"], ["tp0"])
                    b.copy("dve", OT_t[:, hh, :], tpb[:, :], ["tp0"], ["OT_t"])

                def emit_rest(i):
                    ti, h, kb = steps[i]
                    t0, TS, s = TILES[ti]
                    p_ = pT2[i % 2]
                    pk_ = f"pT{i % 2}"
                    if h == 7 and kb == 0:
                        P.dma("sp", "x0", hres[:, :, :], src_h(t0, TS, s), writes=["hres0"])
                    for m in range(2):
                        for j in range(4):
                            a = m * 4 + j
                            at, ak_, c0 = accv(a)
                            b.mm(at[:, c0:c0 + 129], p_[:, m * 512 + j * 128:m * 512 + (j + 1) * 128], V14[:, kb, h, :],
                                 kb == 0 and a % 3 == 0, kb == 33, [pk_, "V1"], [ak_], skip=True)
                    if kb == 24:
                        if h > 0:
                            transposes(h - 1)
                        elif ti > 0:
                            transposes(7)
                            do_outproj(i, ti - 1)
                    if kb != 33:
                        return
                    hd = cnt["hd"]
                    cnt["hd"] += 1
                    aS = accS[hd % 2]
                    aSk = f"accS{hd % 2}"
                    b.copy("dve", aS[:, 0:387], accb[0][:, 0:387], ["acc0"], [aSk])
                    b.copy("dve", aS[:, 387:774], accb[1][:, 0:387], ["acc1"], [aSk])
                    b.copy("dve", aS[:, 774:1032], accb[2][:, 0:258], ["acc2"], [aSk])
                    aS3 = aS[:, :].rearrange("p (a e) -> p a e", e=129)
                    r_ = rz[:, (hd % 2) * 16:(hd % 2) * 16 + 16]
                    rk_ = f"rz{hd % 2}"
                    b.recip(r_[:, 0:8], aS3[:, :, 128], [aSk], [rk_])
                    b.ts("dve", r_[:, 4:8], r_[:, 4:8], neglam[:, 0:1], None, ALU.mult, None, [rk_, "neglam"], [rk_])
                    for j in range(4):
                        c1 = j * 129
                        c2 = (4 + j) * 129
                        tm, tmk = tmp[j % 2], f"tmp{j % 2}"
                        o_, obk = ob[j], f"ob{j}"
                        b.ts("dve", tm[:, :], aS[:, c2:c2 + 128], r_[:, 4 + j:5 + j], None, ALU.mult, None, [aSk, rk_], [tmk])
                        b.stt(o_[:, :], aS[:, c1:c1 + 128], r_[:, j:j + 1], tm[:, :], ALU.mult, ALU.add, [aSk, rk_, tmk], [obk])
                        b.stt(junk[:, :], o_[:, :], 1.0, o_[:, :], ALU.mult, ALU.mult, [obk], ["junk", (rk_, "ss")],
                              accum_out=r_[:, 8 + j:9 + j])
                    b.ts("dve", r_[:, 8:12], r_[:, 8:12], 1.0 / 128, EPS, ALU.mult, ALU.add, [(rk_, "ss")], [(rk_, "ss")])
                    b.tt("pool", r_[:, 12:16], r_[:, 8:12], nhalf[:, 0:1].broadcast_to([128, 4]), ALU.pow,
                         [(rk_, "ss"), "nhalf"], [(rk_, "rstd")])
                    for j in range(4):
                        o_, obk = ob[j], f"ob{j}"
                        b.stt(O_sb[:, j, h * 128:(h + 1) * 128], o_[:, :], r_[:, 12 + j:13 + j], subg2[:, :], ALU.mult, ALU.mult,
                              [obk, (rk_, "rstd"), "subg2"], [("O_sb", h)])
                    if h == 7 and ti == 7:
                        transposes(7)
                        do_outproj(i, ti)

                def do_outproj(i, tj):
                    t0, TS, s = TILES[tj]
                    sp_ = sps2[i % 2]
                    sq_ = sps2[(i + 1) % 2]
                    outproj_tile(1, Wo, OT_t, TS, s, t0, [sp_[:, 0:512], sp_[:, 512:1024], sq_[:, 0:512], sq_[:, 512:1024]],
                                 [(f"sps{i % 2}", 0), (f"sps{i % 2}", 1), (f"sps{(i + 1) % 2}", 0),
                                  (f"sps{(i + 1) % 2}", 1)], hres, "hres0")

                n = len(steps)
                for i in range(n + 1):
                    if i < n:
                        emit_S(i)
                    if i - 1 >= 0:
                        emit_rest(i - 1)
                P.barrier()
                P.emit(block)

        def phase_mlp(l, final):
            with ExitStack() as ps:
                W1 = sb(ps, "W1", [128, 8, 4 * D], BF16)
                W2 = sb(ps, "W2", [128, 32, D], BF16)
                xts = [sb(ps, f"mx{i}", [128, 8, 512]) for i in range(2)]
                m_t = sb(ps, "m_t", [128, 8, 512], BF16)
                h1 = sb(ps, "h1", [128, 16, 512], BF16)
                rr = [sb(ps, f"rr{i}", [128, 512]) for i in range(2)]
                sqb = [sb(ps, f"msq{i}", [128, 512], BF16) for i in range(2)] + \
                      [sb(ps, f"mnt{i}", [128, 512]) for i in range(2)]
                rtmp = sb(ps, "mrtmp", [128, 512])
                rstd = sb(ps, "mrstd", [128, 512])
                ssb = psb(ps, "mss")
                hps = [psb(ps, f"hps{i}") for i in range(2)]
                yps = [psb(ps, f"yps{i}") for i in range(3)]
                block = ps.enter_context(nc.Block())
                for k in range(KC):
                    P.dma("pool", f"m{k % 8}", W1[:, k, :], w1b[l, k * 128:(k + 1) * 128, :], reads=[("w1b", l, k)],
                          writes=[("W1", k)])
                for k4 in range(8):
                    P.dma("pool", f"m{k4 % 8}", W2[:, k4 * 4:(k4 + 1) * 4, :],
                          w2b[l, k4 * 512:(k4 + 1) * 512, :].rearrange("(k p) n -> p k n", p=128),
                          reads=[("w2b", l, k) for k in range(k4 * 4, k4 * 4 + 4)], writes=[("W2", k4 // 4)])
                if l == 0:
                    precast(1)
                tiles = TILES[:8] if final else TILES
                fi = 0
                yi = 0
                def p3_load(ti):
                    t0, TS, s = tiles[ti]
                    P.dma("sp", f"x{ti % 2}", xts[ti % 2][:, :, 0:TS], src_h(t0, TS, s), writes=[f"mx{ti % 2}"])
                p3_load(0)
                for ti, (t0, TS, s) in enumerate(tiles):
                    xt = xts[ti % 2]
                    xk = f"mx{ti % 2}"
                    if ti + 1 < len(tiles):
                        p3_load(ti + 1)
                    normmod(xt, xk, m_t, "m_t", TS, l, 1, s, sqb, ssb, rtmp, rstd)
                    for half in range(2):
                        for fcl in range(16):
                            fc = half * 16 + fcl
                            hp_ = hps[fi % 2]
                            hk_ = f"hps{fi % 2}"
                            for k in range(KC):
                                b.mm(hp_[:, 0:TS], W1[:, k, fc * 128:(fc + 1) * 128], m_t[:, k, 0:TS], k == 0,
                                     k == KC - 1, [("W1", k), ("m_t", k)], [hk_])
                            r_ = rr[fi % 2]
                            b.act(r_[:, 0:TS], hp_[:, 0:TS], AF.Relu, [hk_], [f"rr{fi % 2}"])
                            b.tt("dve", h1[:, fcl, 0:TS], r_[:, 0:TS], r_[:, 0:TS], ALU.mult, [f"rr{fi % 2}"],
                                 [("h1", fcl)])
                            fi += 1
                        for dc in range(KC):
                            yp = yps[yi % 3]
                            yk = f"yps{yi % 3}"
                            for fcl in range(16):
                                b.mm(yp[:, 0:TS], W2[:, half * 16 + fcl, dc * 128:(dc + 1) * 128], h1[:, fcl, 0:TS],
                                     fcl == 0, fcl == 15, [("W2", half), ("h1", fcl)], [yk])
                            b.stt(xt[:, dc, 0:TS], yp[:, 0:TS], modv(l, 5, dc, s), xt[:, dc, 0:TS], ALU.mult, ALU.add,
                                  [yk, xk, "mod"], [xk])
                            yi += 1
                    if not final:
                        P.dma("sp", "hst", fm(hS)[:, :, t0:t0 + TS], xt[:, :, 0:TS], reads=[xk], writes=["hS_dst"])
                    else:
                        for k in range(KC):
                            q = sqb[k % 2]
                            b.act(q[:, 0:TS], xt[:, k, 0:TS], AF.Square, [xk], [f"sq{k % 2}"])
                            b.mm(ssb[:, 0:TS], ones[:, :], q[:, 0:TS], k == 0, k == KC - 1, [f"sq{k % 2}", "ones"],
                                 ["ssb"])
                        b.act(rtmp[:, 0:TS], ssb[:, 0:TS], AF.Sqrt, ["ssb", "epsT"], ["rtmp"], scale=1.0 / D,
                              bias=epsT[:, 0:1])
                        b.recip(rstd[:, 0:TS], rtmp[:, 0:TS], ["rtmp"], ["rstd"])
                        for k in range(KC):
                            b.stt(xt[:, k, 0:TS], xt[:, k, 0:TS], gfs[:, k:k + 1], rstd[:, 0:TS], ALU.mult, ALU.mult,
                                  [xk, "rstd", "gfs"], [xk])
                        P.dma("sp", "hst", fm(outT)[:, :, t0:t0 + TS], xt[:, :, 0:TS], reads=[xk], writes=["outT"])
                P.barrier()
                P.emit(block)

        ch0 = [(c * 128, "q", c, True, True, True) for c in range(4)] + [(512, "k", 0, True, False, True)] + \
              [(640 + c * 128, "q", 4 + c, False, True, True) for c in range(4)] + [(1152, "k", 1, False, False, True)]
        ch1 = [(c * 128, "q", c, True, False, False) for c in range(8)] + \
              [(1024 + c * 128, "k", c, True, False, True) for c in range(8)]
        if max_phase >= 1:
            phase_p1(0, w_in0, 1536, ch0, 1280, 4, 64, src_x)
        if max_phase >= 2:
            phase_l0_attn()
        if max_phase >= 3:
            phase_mlp(0, False)
        if max_phase >= 4:
            phase_p1(1, w_in1, 3072, ch1, 2048, 8, 128, src_h)
        if max_phase >= 5:
            phase_l1_attn()
        if max_phase >= 6:
            phase_mlp(1, True)
        with ExitStack() as ps:
            block = ps.enter_context(nc.Block())
            P.barrier()
            P.emit(block)
    return nc


def _rope_tables_host():
    t = np.arange(NL)
    row = (t // 64).astype(np.float32)
    col = (t % 64).astype(np.float32)
    inv = (10000.0 ** (-np.arange(16, dtype=np.float32) / 16)).astype(np.float32)
    ar = row[:, None] * inv[None, :]
    ac = col[:, None] * inv[None, :]
    ang = np.concatenate([ar, ar, ac, ac], axis=-1).astype(np.float32)
    cos = np.cos(ang).astype(np.float32).T
    sin = np.sin(ang).astype(np.float32).T
    sign = np.where((np.arange(64) % 32) < 16, -1.0, 1.0).astype(np.float32)[:, None]
    sin_s = sin * sign
    cos2 = np.concatenate([cos, cos], 0)
    sin2 = np.concatenate([sin_s, sin_s], 0)
    return np.stack([cos2, sin2, cos2 * 0.125, sin2 * 0.125]).astype(np.float32)


def _consts_host():
    ident = np.eye(128, dtype=np.float32)
    perm = np.zeros((128, 128), np.float32)
    for m in range(128):
        partner = m + 16 if (m % 32) < 16 else m - 16
        perm[partner, m] = 1.0
    return np.concatenate([ident, perm], axis=1)


def _bias_tables(rpb):
    entries = []
    kl = np.arange(128)[:, None]
    ql = np.arange(128)[None, :]
    lower = np.where(kl >= ql, 0.0, NEG).astype(np.float32)
    upper = np.where(kl <= ql, 0.0, NEG).astype(np.float32)
    entries.append(np.repeat(lower[:, None, :], 4, axis=1))
    entries.append(np.repeat(upper[:, None, :], 4, axis=1))
    a_keys = []
    for i in range(32):
        l = []
        if i - 1 >= 0:
            l.append((i - 1, 0))
        l.append((i, None))
        if i + 1 < 32:
            l.append((i + 1, 1))
        a_keys.append(l)
    cache = {}
    b_keys = []
    for i in range(32):
        r = 2 * i + (np.arange(128) // 64)
        cq = np.arange(128) % 64
        rs = np.clip(r - 4, 0, 56)
        cs = np.clip(cq - 8, 0, 48)
        l = []
        for kb in range(32):
            krow = 2 * kb + (np.arange(128) // 64)
            kcol = np.arange(128) % 64
            valid = ((krow[:, None] >= rs[None, :]) & (krow[:, None] < rs[None, :] + 8) &
                     (kcol[:, None] >= cs[None, :]) & (kcol[:, None] < cs[None, :] + 16))
            if not valid.any():
                continue
            roff = np.clip(krow[:, None] - r[None, :] + 7, 0, 14)
            coff = np.clip(kcol[:, None] - cq[None, :], -15, 15) + 15
            key = (valid.tobytes(), np.where(valid, roff, 0).tobytes(), np.where(valid, coff, 0).tobytes())
            if key not in cache:
                cache[key] = len(entries)
                for kv in range(2):
                    g = rpb[kv * 4:(kv + 1) * 4][:, roff, coff]
                    m = np.where(valid[None], g, np.float32(NEG)).astype(np.float32)
                    entries.append(np.transpose(m, (1, 0, 2)))
            l.append((kb, cache[key]))
        b_keys.append(l)
    tab = np.stack(entries, axis=1).reshape(128, -1).astype(np.float32)
    return np.ascontiguousarray(tab), len(entries), a_keys, b_keys


def _bcast(v, n=128):
    return np.ascontiguousarray(np.broadcast_to(np.asarray(v, np.float32).reshape(1, -1), (n, np.asarray(v).size)))


def _key_structure():
    rpb0 = np.zeros((8, 15, 31), np.float32)
    _, n, a_keys, b_keys = _bias_tables(rpb0)
    return n, a_keys, b_keys


_PROG_CACHE = {}


def _prepare_inputs(x, c, ctx, c_ctx, ada_w, ada_b, norm1_g, norm2_g, even_w_in, even_w_out, a_sink, b_rpb,
                    odd_w_in, odd_w_out, lam_q1, lam_k1, lam_q2, lam_k2, subln_g, mlp_w1, mlp_w2, final_g):
    f = np.float32
    tab, n_bias, a_keys, b_keys = _bias_tables(np.asarray(b_rpb[0], f))
    win = np.asarray(even_w_in[0], f)
    aq, ak, av, bq, bk, bv = win[:, 0:512], win[:, 512:640], win[:, 640:768], win[:, 768:1280], win[:, 1280:1408], \
        win[:, 1408:1536]

    def qperm(q):
        cols = []
        for cch in range(4):
            cols.append(q[:, cch * 64:(cch + 1) * 64])
            cols.append(q[:, (4 + cch) * 64:(5 + cch) * 64])
        return np.concatenate(cols, axis=1)
    w_in0 = np.ascontiguousarray(np.concatenate([qperm(aq), ak, qperm(bq), bk, av, bv], axis=1))
    shared = {
        "ada_w": np.ascontiguousarray(ada_w, f),
        "ada_b": np.ascontiguousarray(np.repeat(np.asarray(ada_b, f).reshape(2, 48, 128).transpose(2, 0, 1)[..., None], 2,
                                                axis=-1).reshape(128, 192)),
        "g12": np.ascontiguousarray(np.stack([np.asarray(norm1_g, f), np.asarray(norm2_g, f)], axis=1)
                                    .reshape(2, 2, 8, 128).transpose(3, 0, 1, 2).reshape(128, 32)),
        "gfin": np.ascontiguousarray(np.asarray(final_g, f).reshape(8, 128).T),
        "w_in0": w_in0,
        "w_out0": np.ascontiguousarray(even_w_out[0], f),
        "w_in1": np.ascontiguousarray(odd_w_in[0], f),
        "w_out1": np.ascontiguousarray(odd_w_out[0], f),
        "w1": np.ascontiguousarray(mlp_w1, f),
        "w2": np.ascontiguousarray(mlp_w2, f),
        "sinkb": _bcast(a_sink[0]),
        "lamv": _bcast(np.concatenate([lam_q1[0], lam_k1[0], lam_q2[0], lam_k2[0]])),
        "subg": _bcast(subln_g[0]),
        "consts": _consts_host(),
        "rope": _rope_tables_host(),
        "biasT": tab,
    }
    in_maps = []
    for bb in range(8):
        m = dict(shared)
        m["xT"] = np.ascontiguousarray(np.asarray(x[bb], f).T)
        m["cT"] = np.ascontiguousarray(np.asarray(ctx[bb], f).T)
        cv = np.stack([np.asarray(c[bb], f).reshape(8, 128).T, np.asarray(c_ctx, f).reshape(8, 128).T], axis=-1)
        m["cvec"] = np.ascontiguousarray(cv.reshape(128, 16))
        in_maps.append(m)
    return in_maps, n_bias, a_keys, b_keys


def kernel(**inputs):
    in_maps, n_bias, a_keys, b_keys = _prepare_inputs(**inputs)
    key = ("main", n_bias)
    if key not in _PROG_CACHE:
        _PROG_CACHE[key] = build_program(n_bias, a_keys, b_keys)
    nc = _PROG_CACHE[key]
    res = run_bass_kernel_spmd(nc, in_maps, core_ids=list(range(8)))
    out = np.stack([np.ascontiguousarray(r["outT"].T) for r in res.results], axis=0)
    return out.astype(np.float32)
```

```python
import math
from contextlib import ExitStack
import numpy as np
import concourse.bass as bass
import concourse.mybir as mybir
from concourse.bass_utils import run_bass_kernel_spmd

F32 = mybir.dt.float32
BF16 = mybir.dt.bfloat16
AF = mybir.ActivationFunctionType
ALU = mybir.AluOpType

D = 1024
NL = 4096
NCX = 256
NT = NL + NCX
KC = 8
EPS = 1e-6
NEG = -30000.0
LAM_INIT = 0.8 - 0.6 * math.exp(-0.3 * 1)
TILES = [(t * 512, 512, 0) for t in range(8)] + [(NL, NCX, 1)]

ENGS = ("pe", "act", "dve", "pool", "sp")
SAME_ENGINE_SYNC = {"pe": False, "act": True, "dve": True, "pool": True, "sp": False}


class Prog:
    def __init__(self, nc, sems):
        self.nc = nc
        self.esem = {e: sems[i] for i, e in enumerate(("pe", "act", "dve", "pool"))}
        self.dsem_pool = list(sems[4:])
        self.dsem = {}
        self.count = {}
        self.semobj = {}
        for e, s in self.esem.items():
            self.count[id(s)] = 0
            self.semobj[id(s)] = s
        self.ops = {e: [] for e in ENGS}
        self.seen = {e: {} for e in ENGS}
        self.last_w = {}
        self.readers = {}
        self.n_ops = 0

    def dma_sem(self, name):
        if name not in self.dsem:
            s = self.dsem_pool.pop()
            self.dsem[name] = s
            self.count[id(s)] = 0
            self.semobj[id(s)] = s
        return self.dsem[name]

    def _need(self, eng, ev, waits):
        if ev is None:
            return
        sid, val = ev
        if eng in self.esem and sid == id(self.esem[eng]) and not SAME_ENGINE_SYNC[eng]:
            return
        if self.seen[eng].get(sid, 0) >= val:
            return
        if waits.get(sid, 0) < val:
            waits[sid] = val

    def _deps(self, eng, reads, writes):
        waits = {}
        for k in reads:
            self._need(eng, self.last_w.get(k), waits)
        for k in writes:
            self._need(eng, self.last_w.get(k), waits)
            for ev in self.readers.get(k, ()):
                self._need(eng, ev, waits)
        for sid, val in waits.items():
            self.seen[eng][sid] = val
        return [(self.semobj[sid], val) for sid, val in waits.items()]

    def _commit(self, ev, reads, writes):
        for k in writes:
            self.last_w[k] = ev
            self.readers[k] = []
        for k in reads:
            lst = self.readers.setdefault(k, [])
            for i, (sid, val) in enumerate(lst):
                if sid == ev[0]:
                    lst[i] = (sid, max(val, ev[1]))
                    break
            else:
                lst.append(ev)

    def op(self, eng, fn, reads=(), writes=()):
        waits = self._deps(eng, reads, writes)
        s = self.esem[eng]
        self.count[id(s)] += 1
        ev = (id(s), self.count[id(s)])
        self.ops[eng].append((waits, fn, (s, 1)))
        self._commit(ev, reads, writes)
        self.n_ops += 1

    def dma(self, q, semname, out, in_, reads=(), writes=()):
        s = self.dma_sem(semname)
        waits = self._deps(q, reads, writes)
        if self.count[id(s)] > 0:
            w = {}
            self._need(q, (id(s), self.count[id(s)]), w)
            for sid, val in w.items():
                self.seen[q][sid] = val
                waits.append((self.semobj[sid], val))
        self.count[id(s)] += 16
        ev = (id(s), self.count[id(s)])

        kw = dict(max_dma_last_dim=4096) if q == "pool" else {}

        def fn(e, out=out, in_=in_, kw=kw):
            return e.dma_start(out=out, in_=in_, **kw)
        self.ops[q].append((waits, fn, (s, 16)))
        self._commit(ev, reads, writes)
        self.n_ops += 1

    def barrier(self):
        for e in ENGS:
            waits = []
            for sid, val in self.count.items():
                if val > 0 and self.seen[e].get(sid, 0) < val:
                    if e in self.esem and sid == id(self.esem[e]):
                        continue
                    waits.append((self.semobj[sid], val))
                    self.seen[e][sid] = val
            if waits:
                self.ops[e].append((waits, None, None))

    def emit(self, block):
        prog = self

        def mk(ename):
            oplist = prog.ops[ename]

            def body(eng):
                for waits, fn, inc in oplist:
                    for s, v in waits:
                        eng.wait_ge(s, v)
                    if fn is not None:
                        fn(eng).then_inc(inc[0], inc[1])
            return body
        block.tensor(mk("pe"))
        block.scalar(mk("act"))
        block.vector(mk("dve"))
        block.gpsimd(mk("pool"))
        block.sync(mk("sp"))
        self.ops = {e: [] for e in ENGS}


class B:
    def __init__(self, nc, P):
        self.nc = nc
        self.P = P

    def mm(self, out, lhsT, rhs, start, stop, reads, writes, skip=False):
        self.P.op("pe", lambda e: e.matmul(out, lhsT=lhsT, rhs=rhs, start=start, stop=stop, skip_group_check=skip),
                  reads, writes)

    def tr(self, out, in_, ident, reads, writes):
        self.P.op("pe", lambda e: e.transpose(out=out, in_=in_, identity=ident), reads, writes)

    def act(self, out, in_, func, reads, writes, scale=1.0, bias=0.0):
        self.P.op("act", lambda e: e.activation(out=out, in_=in_, func=func, bias=bias, scale=scale), reads, writes)

    def tt(self, eng, out, in0, in1, op, reads, writes):
        self.P.op(eng, lambda e: e.tensor_tensor(out=out, in0=in0, in1=in1, op=op), reads, writes)

    def ts(self, eng, out, in0, s1, s2, op0, op1, reads, writes):
        if s2 is None:
            self.P.op(eng, lambda e: e.tensor_scalar(out=out, in0=in0, scalar1=s1, scalar2=None, op0=op0), reads, writes)
        else:
            self.P.op(eng, lambda e: e.tensor_scalar(out=out, in0=in0, scalar1=s1, scalar2=s2, op0=op0, op1=op1),
                      reads, writes)

    def stt(self, out, in0, scalar, in1, op0, op1, reads, writes, accum_out=None):
        if accum_out is None:
            self.P.op("dve", lambda e: e.scalar_tensor_tensor(out=out, in0=in0, scalar=scalar, in1=in1, op0=op0, op1=op1),
                      reads, writes)
        else:
            self.P.op("dve", lambda e: e.scalar_tensor_tensor(out=out, in0=in0, scalar=scalar, in1=in1, op0=op0, op1=op1,
                                                             accum_out=accum_out), reads, writes)

    def copy(self, eng, out, in_, reads, writes):
        if eng == "act":
            self.act(out, in_, AF.Copy, reads, writes)
        else:
            self.P.op(eng, lambda e: e.tensor_copy(out=out, in_=in_), reads, writes)

    def memset(self, eng, ap, val, writes):
        self.P.op(eng, lambda e: e.memset(ap, val), (), writes)

    def recip(self, out, in_, reads, writes):
        self.P.op("dve", lambda e: e.reciprocal(out=out, in_=in_), reads, writes)


def build_program(n_bias, a_keys, b_keys, dbg=False, max_phase=99):
    nc = bass.Bass("TRN2", target_bir_lowering=False)

    def din(name, shape, dt=F32):
        return nc.dram_tensor(name, list(shape), dt, kind="ExternalInput").ap()

    xT = din("xT", [D, NL])
    cT = din("cT", [D, NCX])
    cvec = din("cvec", [128, 16])
    ada_w = din("ada_w", [2, D, 6 * D])
    ada_b = din("ada_b", [128, 192])
    g12 = din("g12", [128, 32])
    gfin = din("gfin", [128, 8])
    w_in0 = din("w_in0", [D, 1536])
    w_out0 = din("w_out0", [D, D])
    w_in1 = din("w_in1", [D, 3072])
    w_out1 = din("w_out1", [D, D])
    w1 = din("w1", [2, D, 4 * D])
    w2 = din("w2", [2, 4 * D, D])
    sinkb = din("sinkb", [128, 8])
    lamv = din("lamv", [128, 256])
    subg = din("subg", [128, 128])
    consts = din("consts", [128, 256])
    rope = din("rope", [4, 128, NL])
    biasT = din("biasT", [128, n_bias * 512])
    outT = nc.dram_tensor("outT", [D, NL], F32, kind="ExternalOutput").ap()
    skind = "ExternalOutput" if dbg else "Internal"
    hS = nc.dram_tensor("hS", [D, NT], F32, kind=skind).ap()
    qT_d = nc.dram_tensor("qT_d", [D, NT], BF16, kind=skind).ap()
    kT_d = nc.dram_tensor("kT_d", [D, NT], BF16, kind=skind).ap()
    v_d = nc.dram_tensor("v_d", [NT, 8 * 129], BF16, kind=skind).ap()
    w1b = nc.dram_tensor("w1b", [2, D, 4 * D], BF16).ap()
    w2b = nc.dram_tensor("w2b", [2, 4 * D, D], BF16).ap()
    dbg_out = {}
    if dbg:
        dbg_out["mod"] = nc.dram_tensor("dbg_mod", [128, 192], F32, kind="ExternalOutput").ap()

    def fm(ap):
        return ap.rearrange("(k p) t -> p k t", p=128)

    with ExitStack() as es:
        sems = [es.enter_context(nc.semaphore(f"s{i}")) for i in range(56)]
        P = Prog(nc, sems)
        b = B(nc, P)

        uid = [0]

        def sb(stack, name, shape, dt=F32):
            uid[0] += 1
            return stack.enter_context(nc.sbuf_tensor(f"{name}_{uid[0]}", list(shape), dt))

        def psb(stack, name, shape=(128, 512), dt=F32):
            uid[0] += 1
            return stack.enter_context(nc.psum_tensor(f"{name}_{uid[0]}", list(shape), dt))

        mod = sb(es, "mod", [128, 192])
        gm = sb(es, "gm", [128, 64])
        g12s = sb(es, "g12s", [128, 32])
        gfs = sb(es, "gfs", [128, 8])
        ident = sb(es, "ident", [128, 128], BF16)
        perm = sb(es, "perm", [128, 128], BF16)
        ones = sb(es, "ones", [128, 128], BF16)
        nhalf = sb(es, "nhalf", [128, 1])
        expsink = sb(es, "expsink", [128, 8])
        neglam = sb(es, "neglam", [128, 1])
        subg2 = sb(es, "subg2", [128, 128])
        epsT = sb(es, "epsT", [128, 1])

        mod4 = mod[:, :].rearrange("p (l j s) -> p l j s", l=2, j=48, s=2)
        gm5 = gm[:, :].rearrange("p (l w k s) -> p l w k s", l=2, w=2, k=8, s=2)
        g124 = g12s[:, :].rearrange("p (l w k) -> p l w k", l=2, w=2, k=8)

        def modv(l, idx, k, s):
            return mod4[:, l, idx * 8 + k, s:s + 1]

        def gmv(l, which, k, s):
            return gm5[:, l, which, k, s:s + 1]

        with ExitStack() as ps:
            cv = sb(ps, "cv", [128, 16])
            s_bf = sb(ps, "s_bf", [128, 16], BF16)
            ab = sb(ps, "ab", [128, 192])
            lv = sb(ps, "lv", [128, 256])
            lt = sb(ps, "lt", [128, 128])
            ls = sb(ps, "ls", [128, 4])
            sk = sb(ps, "sk", [128, 8])
            sg = sb(ps, "sg", [128, 128])
            wbuf = [sb(ps, f"wbuf{i}", [128, 8, 3072], BF16) for i in range(2)]
            pm = psb(ps, "pm")
            block = ps.enter_context(nc.Block())

            P.dma("sp", "c0", cv[:, :], cvec, writes=["cv"])
            P.dma("sp", "c1", ab[:, :], ada_b, writes=["ab"])
            P.dma("sp", "c2", g12s[:, :], g12, writes=["g12s"])
            P.dma("sp", "c3", gfs[:, :], gfin, writes=["gfs"])
            P.dma("sp", "c0", sk[:, :], sinkb, writes=["sk"])
            P.dma("sp", "c1", lv[:, :], lamv, writes=["lv"])
            P.dma("sp", "c2", sg[:, :], subg, writes=["sg"])
            P.dma("pool", "c4", ident[:, :], consts[:, 0:128], writes=["ident"])
            P.dma("pool", "c5", perm[:, :], consts[:, 128:256], writes=["perm"])
            b.memset("dve", ones[:, :], 1.0, ["ones"])
            b.memset("dve", nhalf[:, :], -0.5, ["nhalf"])
            b.memset("dve", epsT[:, :], EPS, ["epsT"])
            b.act(s_bf[:, :], cv[:, :], AF.Silu, ["cv"], ["s_bf"])
            b.act(expsink[:, :], sk[:, :], AF.Exp, ["sk"], ["expsink"])
            s3 = s_bf[:, :].rearrange("p (k s) -> p k s", s=2)
            li = 0
            for l in range(2):
                for half in range(2):
                    wb = wbuf[li % 2]
                    wk = f"wbuf{li % 2}"
                    for k in range(KC):
                        P.dma("pool", f"w{k % 16}", wb[:, k, :],
                              ada_w[l, k * 128:(k + 1) * 128, half * 3072:(half + 1) * 3072], writes=[(wk, k)])
                    for jj in range(24):
                        col = (l * 48 + half * 24 + jj) * 2
                        for k in range(KC):
                            b.mm(pm[:, col:col + 2], wb[:, k, jj * 128:(jj + 1) * 128], s3[:, k, :],
                                 k == 0, k == KC - 1, [(wk, k), "s_bf"], ["pm"])
                    li += 1
            b.tt("dve", mod[:, :], pm[:, 0:192], ab[:, :], ALU.add, ["pm", "ab"], ["mod"])
            for l in range(2):
                for w in range(2):
                    for s in range(2):
                        sc = mod4[:, l, (1 + 3 * w) * 8:(2 + 3 * w) * 8, s]
                        b.stt(gm5[:, l, w, :, s], sc, 1.0, g124[:, l, w, :], ALU.add, ALU.mult,
                              ["mod", "g12s"], ["gm"])
            lv3 = lv[:, :].rearrange("p (a d) -> p a d", d=64)
            for i in range(2):
                b.stt(lt[:, 0:64], lv3[:, 2 * i, :], 1.0, lv3[:, 2 * i + 1, :], ALU.mult, ALU.mult,
                      ["lv"], ["lt", "ls"], accum_out=ls[:, i:i + 1])
            b.act(ls[:, 2:4], ls[:, 0:2], AF.Exp, ["ls"], ["ls"])
            b.tt("dve", neglam[:, :], ls[:, 3:4], ls[:, 2:3], ALU.subtract, ["ls"], ["neglam"])
            b.ts("dve", neglam[:, :], neglam[:, :], -LAM_INIT, None, ALU.add, None, ["neglam"], ["neglam"])
            b.ts("dve", subg2[:, :], sg[:, :], 1.0 - LAM_INIT, None, ALU.mult, None, ["sg"], ["subg2"])
            if dbg:
                P.dma("sp", "dbg", dbg_out["mod"], mod[:, :], reads=["mod"], writes=["dbg_mod"])
            P.barrier()
            P.emit(block)

        def normmod_parts(xt, xk, a_out, ak, TS, l, which, s, sqb, ssb, rtmp, rstd):
            def p1():
                for k in range(KC):
                    q = sqb[k % 2]
                    b.act(q[:, 0:TS], xt[:, k, 0:TS], AF.Square, [xk], [f"sq{k % 2}"])
                    b.mm(ssb[:, 0:TS], ones[:, :], q[:, 0:TS], k == 0, k == KC - 1, [f"sq{k % 2}", "ones"], ["ssb"])

            def p2():
                b.act(rtmp[:, 0:TS], ssb[:, 0:TS], AF.Sqrt, ["ssb", "epsT"], ["rtmp"], scale=1.0 / D, bias=epsT[:, 0:1])
                b.recip(rstd[:, 0:TS], rtmp[:, 0:TS], ["rtmp"], ["rstd"])

            def p34(k0):
                def f():
                    for k in range(k0, k0 + 4):
                        t = sqb[2 + k % 2]
                        b.stt(t[:, 0:TS], xt[:, k, 0:TS], gmv(l, which, k, s), rstd[:, 0:TS], ALU.mult, ALU.mult,
                              [xk, "rstd", "gm"], [f"nt{k % 2}"])
                        b.act(a_out[:, k, 0:TS], t[:, 0:TS], AF.Identity, [f"nt{k % 2}", "mod"], [(ak, k)],
                              bias=modv(l, 3 * which, k, s))
                return f
            return [p1, p2, p34(0), p34(4)]

        def normmod(*args):
            for f in normmod_parts(*args):
                f()

        def phase_p1(l, W, NC_, fm_chunks, v_col0, nh, dv, src_fn):
            with ExitStack() as ps:
                Wb = sb(ps, "p1W", [128, 8, NC_], BF16)
                xts = [sb(ps, f"p1x{i}", [128, 8, 512]) for i in range(2)]
                a_ts = [sb(ps, f"p1a{i}", [128, 8, 512], BF16) for i in range(2)]
                sqb = [sb(ps, f"p1sq{i}", [128, 512], BF16) for i in range(2)] + \
                      [sb(ps, f"p1nt{i}", [128, 512]) for i in range(2)]
                rtmp = sb(ps, "p1rtmp", [128, 512])
                rstd = sb(ps, "p1rstd", [128, 512])
                nq = sum(1 for c in fm_chunks if c[1] == "q")
                nk = sum(1 for c in fm_chunks if c[1] == "k")
                qst = sb(ps, "p1qst", [128, nq, 512], BF16)
                kst = sb(ps, "p1kst", [128, nk, 512], BF16)
                VW = nh * (dv + 1)
                vst = sb(ps, "p1vst", [128, 4, VW], BF16)
                ntab = 4 if l == 0 else 2
                tabs = [sb(ps, f"p1tab{i}", [128, ntab, 512]) for i in range(2)]
                q_sb = [sb(ps, f"p1qsb{i}", [128, 512], BF16) for i in range(2)]
                t1 = [sb(ps, f"p1t1{i}", [128, 512]) for i in range(2)]
                t2 = [sb(ps, f"p1t2{i}", [128, 512]) for i in range(2)]
                ssb = psb(ps, "p1ss")
                qps = [psb(ps, f"p1qps{i}") for i in range(2)]
                pps = [psb(ps, f"p1pps{i}") for i in range(2)]
                vps = [psb(ps, f"p1vps{i}") for i in range(2)]
                block = ps.enter_context(nc.Block())

                for k in range(KC):
                    P.dma("pool", f"w{k % 16}", Wb[:, k, :], W[k * 128:(k + 1) * 128, :], writes=[("p1W", k)])
                b.memset("dve", vst[:, :, :], 1.0, ["vst"])
                vst4 = vst[:, :, :].rearrange("p b (h e) -> p b h e", e=dv + 1)
                ci = 0
                vi = 0
                def p1_load(ti):
                    t0, TS, s = TILES[ti]
                    P.dma("sp", f"x{ti % 2}", xts[ti % 2][:, :, 0:TS], src_fn(t0, TS, s), writes=[f"p1x{ti % 2}"])
                    if s == 0:
                        P.dma("sp", f"t{ti % 2}", tabs[ti % 2][:, :, 0:TS],
                              rope[0:ntab, :, t0:t0 + TS].rearrange("a p t -> p a t"), writes=[f"p1tab{ti % 2}"])
                p1_load(0)
                for ti, (t0, TS, s) in enumerate(TILES):
                    xt = xts[ti % 2]
                    xk = f"p1x{ti % 2}"
                    tab = tabs[ti % 2]
                    tk = f"p1tab{ti % 2}"
                    if ti + 1 < len(TILES):
                        p1_load(ti + 1)
                    a_t = a_ts[ti % 2]
                    pa = f"p1a{ti % 2}"
                    if ti == 0:
                        normmod(xt, xk, a_t, pa, TS, l, 0, s, sqb, ssb, rtmp, rstd)
                    nparts = []
                    if ti + 1 < len(TILES):
                        t0n, TSn, sn = TILES[ti + 1]
                        nparts = normmod_parts(xts[(ti + 1) % 2], f"p1x{(ti + 1) % 2}", a_ts[(ti + 1) % 2],
                                               f"p1a{(ti + 1) % 2}", TSn, l, 0, sn, sqb, ssb, rtmp, rstd)
                    tail = None
                    for (col0, kind, dch, rp, qs, ctx_needed) in fm_chunks:
                        if s == 1 and not ctx_needed:
                            continue
                        qp = qps[ci % 2]
                        qk = f"p1qps{ci % 2}"
                        for k in range(KC):
                            b.mm(qp[:, 0:TS], Wb[:, k, col0:col0 + 128], a_t[:, k, 0:TS], k == 0, k == KC - 1,
                                 [("p1W", k), (pa, k)], [qk])
                        dst = (qst if kind == "q" else kst)[:, dch, 0:TS]
                        dk = ("p1st", kind, dch)
                        if rp and s == 0:
                            qsb = q_sb[ci % 2]
                            b.copy("act", qsb[:, 0:TS], qp[:, 0:TS], [qk], [f"qsb{ci % 2}"])

                            def mk_tail(c=ci, qsb=qsb, dst=dst, dk=dk, to=(2 if qs else 0)):
                                def f():
                                    pp = pps[c % 2]
                                    b.mm(pp[:, 0:TS], perm[:, :], qsb[:, 0:TS], True, True, [f"qsb{c % 2}", "perm"],
                                         [f"pps{c % 2}"])
                                    b.tt("dve", t1[c % 2][:, 0:TS], qsb[:, 0:TS], tab[:, to, 0:TS], ALU.mult,
                                         [f"qsb{c % 2}", tk], [f"t1{c % 2}"])
                                    b.tt("dve", t2[c % 2][:, 0:TS], pp[:, 0:TS], tab[:, to + 1, 0:TS], ALU.mult,
                                         [f"pps{c % 2}", tk], [f"t2{c % 2}"])
                                    b.tt("pool", dst, t1[c % 2][:, 0:TS], t2[c % 2][:, 0:TS], ALU.add,
                                         [f"t1{c % 2}", f"t2{c % 2}"], [dk])
                                return f
                            if tail is not None:
                                tail()
                            tail = mk_tail()
                        else:
                            b.act(dst, qp[:, 0:TS], AF.Copy, [qk], [dk], scale=(0.125 if qs else 1.0))
                            if tail is not None:
                                tail()
                                tail = None
                        ci += 1
                    if tail is not None:
                        tail()
                    nb = TS // 128
                    for jb in range(nb):
                        for hh in range((nh * dv) // 512 if nh * dv >= 512 else 1):
                            wcols = min(512, nh * dv)
                            vp = vps[vi % 2]
                            vk = f"p1vps{vi % 2}"
                            for k in range(KC):
                                b.mm(vp[:, 0:wcols], a_t[:, k, jb * 128:(jb + 1) * 128],
                                     Wb[:, k, v_col0 + hh * 512:v_col0 + hh * 512 + wcols], k == 0, k == KC - 1,
                                     [("p1W", k), (pa, k)], [vk])
                            hpc = wcols // dv
                            b.copy("dve" if vi % 2 else "act", vst4[:, jb, hh * hpc:(hh + 1) * hpc, 0:dv],
                                   vp[:, 0:wcols].rearrange("p (h d) -> p h d", d=dv), [vk], ["vst"])
                            vi += 1
                        if nparts:
                            nparts.pop(0)()
                    while nparts:
                        nparts.pop(0)()
                    nqs = sum(1 for c in fm_chunks if c[1] == "q" and (s == 0 or c[5]))
                    if nqs:
                        P.dma("sp", "stq", fm(qT_d)[:, 0:nq, t0:t0 + TS], qst[:, :, 0:TS],
                              reads=[("p1st", "q", i) for i in range(nq)], writes=["qT_d"])
                    P.dma("sp", "stk", fm(kT_d)[:, 0:nk, t0:t0 + TS], kst[:, :, 0:TS],
                          reads=[("p1st", "k", i) for i in range(nk)], writes=["kT_d"])
                    P.dma("sp", "stv", v_d[t0:t0 + TS, 0:VW].rearrange("(b p) f -> p b f", p=128),
                          vst[:, 0:nb, :], reads=["vst"], writes=["v_d"])
                P.barrier()
                P.emit(block)

        def outproj_tile(l, Wo, OT_t, TS, s, t0, ybanks, ykeys, ht, hk):
            for dc in range(KC):
                yb = ybanks[dc % len(ybanks)]
                yk = ykeys[dc % len(ybanks)]
                for c in range(KC):
                    b.mm(yb[:, 0:TS], Wo[:, c, dc * 128:(dc + 1) * 128], OT_t[:, c, 0:TS], c == 0, c == KC - 1,
                         ["Wo", "OT_t"], [yk])
                b.stt(ht[:, dc, 0:TS], yb[:, 0:TS], modv(l, 2, dc, s), ht[:, dc, 0:TS], ALU.mult, ALU.add,
                      [yk, hk, (hk, dc), "mod"], [(hk, dc)])
            P.dma("sp", "hst", fm(hS)[:, :, t0:t0 + TS], ht[:, :, 0:TS], reads=[hk] + [(hk, dc) for dc in range(KC)],
                  writes=["hS_dst"])

        def src_x(t0, TS, s):
            return fm(xT)[:, :, t0:t0 + TS] if s == 0 else fm(cT)[:, :, 0:TS]

        def src_h(t0, TS, s):
            return fm(hS)[:, :, t0:t0 + TS]

        def precast(l):
            for k in range(KC):
                P.dma("pool", f"w{k % 16}", w1b[l, k * 128:(k + 1) * 128, :], w1[l, k * 128:(k + 1) * 128, :],
                      writes=[("w1b", l, k)])
            for k in range(32):
                P.dma("pool", f"w{k % 16}", w2b[l, k * 128:(k + 1) * 128, :], w2[l, k * 128:(k + 1) * 128, :],
                      writes=[("w2b", l, k)])

        def phase_l0_attn():
            with ExitStack() as ps:
                KT0 = sb(ps, "KT0", [128, 2, NT], BF16)
                V0 = sb(ps, "V0", [128, 34, 4 * 65], BF16)
                bias = sb(ps, "bias", [128, n_bias, 512], BF16)
                Wo = sb(ps, "Wo", [128, 8, D], BF16)
                Qt = [sb(ps, f"Qt{i}", [128, 8, 512], BF16) for i in range(2)]
                NPT = 3
                pT = [sb(ps, f"pT{i}", [128, 512], BF16) for i in range(NPT)]
                O_sb = [sb(ps, f"O_sb{i}", [128, D], BF16) for i in range(2)]
                OT_t = sb(ps, "OT_t", [128, 8, 512], BF16)
                zt = sb(ps, "zt", [128, 8])
                rz = sb(ps, "rz", [128, 8])
                hts = [sb(ps, f"hres{i}", [128, 8, 512]) for i in range(2)]
                spsb = [psb(ps, f"sps{i}") for i in range(NPT)]
                accb = [psb(ps, f"acc{i}") for i in range(2)]
                tpb = psb(ps, "tp0", [128, 1024], BF16)
                ypb = [psb(ps, f"ypb{i}") for i in range(2)]
                block = ps.enter_context(nc.Block())

                P.dma("sp", "kt", KT0[:, :, :], fm(kT_d)[:, 0:2, :], writes=["KT0"])
                P.dma("sp", "vv", V0[:, :, :], v_d[:, 0:260].rearrange("(b p) f -> p b f", p=128), writes=["V0"])
                nbh = (n_bias + 1) // 2
                P.dma("pool", "w0", bias[:, 0:nbh, :], biasT[:, 0:nbh * 512].rearrange("p (n f) -> p n f", f=512),
                      writes=["bias"])
                P.dma("pool", "w1", bias[:, nbh:n_bias, :],
                      biasT[:, nbh * 512:n_bias * 512].rearrange("p (n f) -> p n f", f=512), writes=["bias"])
                for k in range(KC):
                    P.dma("pool", f"w{k % 16}", Wo[:, k, :], w_out0[k * 128:(k + 1) * 128, :], writes=["Wo"])
                precast(0)
                V04 = V0[:, :, :].rearrange("p b (h e) -> p b h e", e=65)

                steps = []
                for ti, (t0, TS, s) in enumerate(TILES):
                    nqb = TS // 128
                    for qbl in range(nqb):
                        qb = t0 // 128 + qbl
                        for g in range(4):
                            typ, kv = g // 2, g % 2
                            if s == 1:
                                klist = [(32, None), (33, None)]
                            else:
                                klist = (a_keys if typ == 0 else b_keys)[qb] + [(32, None), (33, None)]
                            for idx, (kb, be) in enumerate(klist):
                                steps.append(dict(ti=ti, t0=t0, TS=TS, s=s, qbl=qbl, g=g, typ=typ, kv=kv, idx=idx, kb=kb,
                                                  be=be, last=(idx == len(klist) - 1), first_tile=(qbl == 0 and g == 0 and idx == 0),
                                                  last_q=(g == 3 and idx == len(klist) - 1),
                                                  last_tile=(qbl == nqb - 1 and g == 3 and idx == len(klist) - 1)))
                grp = 0
                qbi = 0
                for st_ in steps:
                    st_["ai"] = grp
                    st_["oi"] = qbi
                    if st_["last"]:
                        grp += 1
                    if st_["last_q"]:
                        qbi += 1

                def load_q(ti):
                    t0, TS, s = TILES[ti]
                    P.dma("sp", f"q{ti % 2}", Qt[ti % 2][:, :, 0:TS], fm(qT_d)[:, :, t0:t0 + TS], writes=[f"Qt{ti % 2}"])

                def load_h(ti):
                    t0, TS, s = TILES[ti]
                    P.dma("sp", f"x{ti % 2}", hts[ti % 2][:, :, 0:TS], src_x(t0, TS, s), writes=[f"hres{ti % 2}"])
                load_q(0)
                load_h(0)

                def emit_S(i):
                    st_ = steps[i]
                    ti, typ, kv, kb, be = st_["ti"], st_["typ"], st_["kv"], st_["kb"], st_["be"]
                    if st_["first_tile"] and ti + 1 < len(TILES):
                        load_q(ti + 1)
                    Q = Qt[ti % 2]
                    qk_ = f"Qt{ti % 2}"
                    hp = kv * 64
                    qoff = st_["qbl"] * 128
                    sp_ = spsb[i % NPT]
                    sk_ = f"sps{i % NPT}"
                    b.mm(sp_[:, :].rearrange("p (c q) -> p c q", c=4), KT0[hp:hp + 64, typ, kb * 128:(kb + 1) * 128],
                         Q[hp:hp + 64, typ * 4:typ * 4 + 4, qoff:qoff + 128], True, be is None, ["KT0", qk_], [sk_])
                    if be is not None:
                        e = be if typ == 0 else be + kv
                        b.mm(sp_[:, :], ident[:, :], bias[:, e, :], False, True, ["bias", "ident"], [sk_])
                    b.act(pT[i % NPT][:, :], sp_[:, :], AF.Exp, [sk_], [f"pT{i % NPT}"])

                pending = []

                def emit_rest(i):
                    st_ = steps[i]
                    ti, typ, kv, kb, s, TS, t0 = st_["ti"], st_["typ"], st_["kv"], st_["kb"], st_["s"], st_["TS"], st_["t0"]
                    ai, oi = st_["ai"], st_["oi"]
                    if st_["first_tile"] and ti + 1 < len(TILES):
                        pending.append((i + 12, lambda ti=ti: load_h(ti + 1)))
                    acc = accb[ai % 2]
                    acck = f"acc{ai % 2}"
                    p_ = pT[i % NPT]
                    pk_ = f"pT{i % NPT}"
                    for j in range(4):
                        b.mm(acc[:, j * 65:(j + 1) * 65], p_[:, j * 128:(j + 1) * 128], V04[:, kb, typ * 2 + kv, :],
                             st_["idx"] == 0 and j == 0, st_["last"], [pk_, "V0"], [acck], skip=True)
                    if not st_["last"]:
                        return
                    Ob = O_sb[oi % 2]
                    ok_ = f"O_sb{oi % 2}"
                    acc3 = acc[:, 0:260].rearrange("p (j e) -> p j e", e=65)
                    zz = zt[:, (ai % 2) * 4:(ai % 2) * 4 + 4]
                    rr = rz[:, (ai % 2) * 4:(ai % 2) * 4 + 4]
                    zk, rk = f"zt{ai % 2}", f"rz{ai % 2}"
                    if typ == 0:
                        b.tt("dve", zz, acc3[:, :, 64], expsink[:, kv * 4:kv * 4 + 4], ALU.add, [acck, "expsink"], [zk])
                    else:
                        b.copy("dve", zz, acc3[:, :, 64], [acck], [zk])
                    b.recip(rr, zz, [zk], [rk])
                    base = typ * 512 + kv * 256
                    b.tt("dve", Ob[:, base:base + 256].rearrange("p (j d) -> p j d", d=64), acc3[:, :, 0:64],
                         rr.unsqueeze(2).broadcast_to([128, 4, 64]), ALU.mult, [acck, rk], [ok_])
                    if not st_["last_q"]:
                        return
                    qoff = st_["qbl"] * 128

                    def fin(Ob=Ob, ok_=ok_, qoff=qoff, st_=st_, TS=TS, s=s, t0=t0, ti=ti):
                        for c in range(KC):
                            b.tr(tpb[:, c * 128:(c + 1) * 128], Ob[:, c * 128:(c + 1) * 128], ident[:, :], [ok_, "ident"],
                                 ["tp0"])
                        b.copy("dve", OT_t[:, :, qoff:qoff + 128], tpb[:, :].rearrange("p (c q) -> p c q", c=8), ["tp0"],
                               ["OT_t"])
                        if st_["last_tile"]:
                            outproj_tile(0, Wo, OT_t, TS, s, t0, ypb, ["ypb0", "ypb1"], hts[ti % 2], f"hres{ti % 2}")
                    pending.append((i + 6, fin))

                DEPTH = NPT - 1
                n = len(steps)
                for i in range(n + DEPTH):
                    if i < n:
                        emit_S(i)
                    if i - DEPTH >= 0:
                        emit_rest(i - DEPTH)
                        while pending and pending[0][0] <= i - DEPTH:
                            pending.pop(0)[1]()
                while pending:
                    pending.pop(0)[1]()
                P.barrier()
                P.emit(block)

        def phase_l1_attn():
            with ExitStack() as ps:
                KT1 = sb(ps, "KT1", [128, 8, NT], BF16)
                V1 = sb(ps, "V1", [128, 34, 8 * 129], BF16)
                Wo = sb(ps, "Wo", [128, 8, D], BF16)
                Qc = [sb(ps, f"Qc{i}", [128, 512], BF16) for i in range(2)]
                pT2 = [sb(ps, f"pT{i}", [128, 1024], BF16) for i in range(2)]
                accS = [sb(ps, f"accS{i}", [128, 8 * 129]) for i in range(2)]
                O_sb = sb(ps, "O_sb", [128, 4, D], BF16)
                OT_t = sb(ps, "OT_t", [128, 8, 512], BF16)
                rz = sb(ps, "rz", [128, 32])
                tmp = [sb(ps, f"tmp{i}", [128, 128]) for i in range(2)]
                ob = [sb(ps, f"ob{i}", [128, 128]) for i in range(4)]
                junk = sb(ps, "junk", [128, 128])
                hres = sb(ps, "hres0", [128, 8, 512])
                sps2 = [psb(ps, f"sps{i}", (128, 1024)) for i in range(2)]
                accb = [psb(ps, f"acc{i}") for i in range(3)]
                tpb = psb(ps, "tp0", [128, 512], BF16)
                block = ps.enter_context(nc.Block())

                for c in range(KC):
                    P.dma("sp", f"kt{c % 2}", KT1[:, c, :], kT_d[c * 128:(c + 1) * 128, :], writes=["KT1"])
                v_r = v_d[:, :].rearrange("(b p) f -> p b f", p=128)
                for i in range(2):
                    P.dma("sp", f"vv{i}", V1[:, i * 17:(i + 1) * 17, :], v_r[:, i * 17:(i + 1) * 17, :], writes=["V1"])
                for k in range(KC):
                    P.dma("pool", f"w{k % 16}", Wo[:, k, :], w_out1[k * 128:(k + 1) * 128, :], writes=["Wo"])
                V14 = V1[:, :, :].rearrange("p b (h e) -> p b h e", e=129)

                def accv(a):
                    return accb[a // 3], f"acc{a // 3}", (a % 3) * 129

                steps = [(ti, h, kb) for ti in range(8) for h in range(8) for kb in range(34)]

                def load_q(n):
                    ti, h = n // 8, n % 8
                    t0 = TILES[ti][0]
                    P.dma("sp", f"q{n % 2}", Qc[n % 2][:, :], qT_d[h * 128:(h + 1) * 128, t0:t0 + 512], writes=[f"Qc{n % 2}"])
                load_q(0)

                def emit_S(i):
                    ti, h, kb = steps[i]
                    n = ti * 8 + h
                    if kb == 0 and n + 1 < 64:
                        load_q(n + 1)
                    Q = Qc[n % 2]
                    qk_ = f"Qc{n % 2}"
                    sp_ = sps2[i % 2]
                    for m in range(2):
                        b.mm(sp_[:, m * 512:(m + 1) * 512], KT1[m * 64:(m + 1) * 64, h, kb * 128:(kb + 1) * 128],
                             Q[m * 64:(m + 1) * 64, :], True, True, ["KT1", qk_], [(f"sps{i % 2}", m)])
                    b.act(pT2[i % 2][:, :], sp_[:, :], AF.Exp, [(f"sps{i % 2}", 0), (f"sps{i % 2}", 1)], [f"pT{i % 2}"],
                          scale=0.125)

                cnt = dict(ni=0, hd=0)

                def transposes(hh):
                    for j in range(4):
                        b.tr(tpb[:, j * 128:(j + 1) * 128], O_sb[:, j, hh * 128:(hh + 1) * 128], ident[:, :],
                             [("O_sb", hh), "ident"], ["tp0"])
                    b.copy("dve", OT_t[:, hh, :], tpb[:, :], ["tp0"], ["OT_t"])

                def emit_rest(i):
                    ti, h, kb = steps[i]
                    t0, TS, s = TILES[ti]
                    p_ = pT2[i % 2]
                    pk_ = f"pT{i % 2}"
                    if h == 7 and kb == 0:
                        P.dma("sp", "x0", hres[:, :, :], src_h(t0, TS, s), writes=["hres0"])
                    for m in range(2):
                        for j in range(4):
                            a = m * 4 + j
                            at, ak_, c0 = accv(a)
                            b.mm(at[:, c0:c0 + 129], p_[:, m * 512 + j * 128:m * 512 + (j + 1) * 128], V14[:, kb, h, :],
                                 kb == 0 and a % 3 == 0, kb == 33, [pk_, "V1"], [ak_], skip=True)
                    if kb == 24:
                        if h > 0:
                            transposes(h - 1)
                        elif ti > 0:
                            transposes(7)
                            do_outproj(i, ti - 1)
                    if kb != 33:
                        return
                    hd = cnt["hd"]
                    cnt["hd"] += 1
                    aS = accS[hd % 2]
                    aSk = f"accS{hd % 2}"
                    b.copy("dve", aS[:, 0:387], accb[0][:, 0:387], ["acc0"], [aSk])
                    b.copy("dve", aS[:, 387:774], accb[1][:, 0:387], ["acc1"], [aSk])
                    b.copy("dve", aS[:, 774:1032], accb[2][:, 0:258], ["acc2"], [aSk])
                    aS3 = aS[:, :].rearrange("p (a e) -> p a e", e=129)
                    r_ = rz[:, (hd % 2) * 16:(hd % 2) * 16 + 16]
                    rk_ = f"rz{hd % 2}"
                    b.recip(r_[:, 0:8], aS3[:, :, 128], [aSk], [rk_])
                    b.ts("dve", r_[:, 4:8], r_[:, 4:8], neglam[:, 0:1], None, ALU.mult, None, [rk_, "neglam"], [rk_])
                    for j in range(4):
                        c1 = j * 129
                        c2 = (4 + j) * 129
                        tm, tmk = tmp[j % 2], f"tmp{j % 2}"
                        o_, obk = ob[j], f"ob{j}"
                        b.ts("dve", tm[:, :], aS[:, c2:c2 + 128], r_[:, 4 + j:5 + j], None, ALU.mult, None, [aSk, rk_], [tmk])
                        b.stt(o_[:, :], aS[:, c1:c1 + 128], r_[:, j:j + 1], tm[:, :], ALU.mult, ALU.add, [aSk, rk_, tmk], [obk])
                        b.stt(junk[:, :], o_[:, :], 1.0, o_[:, :], ALU.mult, ALU.mult, [obk], ["junk", (rk_, "ss")],
                              accum_out=r_[:, 8 + j:9 + j])
                    b.ts("dve", r_[:, 8:12], r_[:, 8:12], 1.0 / 128, EPS, ALU.mult, ALU.add, [(rk_, "ss")], [(rk_, "ss")])
                    b.tt("pool", r_[:, 12:16], r_[:, 8:12], nhalf[:, 0:1].broadcast_to([128, 4]), ALU.pow,
                         [(rk_, "ss"), "nhalf"], [(rk_, "rstd")])
                    for j in range(4):
                        o_, obk = ob[j], f"ob{j}"
                        b.stt(O_sb[:, j, h * 128:(h + 1) * 128], o_[:, :], r_[:, 12 + j:13 + j], subg2[:, :], ALU.mult, ALU.mult,
                              [obk, (rk_, "rstd"), "subg2"], [("O_sb", h)])
                    if h == 7 and ti == 7:
                        transposes(7)
                        do_outproj(i, ti)

                def do_outproj(i, tj):
                    t0, TS, s = TILES[tj]
                    sp_ = sps2[i % 2]
                    sq_ = sps2[(i + 1) % 2]
                    outproj_tile(1, Wo, OT_t, TS, s, t0, [sp_[:, 0:512], sp_[:, 512:1024], sq_[:, 0:512], sq_[:, 512:1024]],
                                 [(f"sps{i % 2}", 0), (f"sps{i % 2}", 1), (f"sps{(i + 1) % 2}", 0),
                                  (f"sps{(i + 1) % 2}", 1)], hres, "hres0")

                n = len(steps)
                for i in range(n + 1):
                    if i < n:
                        emit_S(i)
                    if i - 1 >= 0:
                        emit_rest(i - 1)
                P.barrier()
                P.emit(block)

        def phase_mlp(l, final):
            with ExitStack() as ps:
                W1 = sb(ps, "W1", [128, 8, 4 * D], BF16)
                W2 = sb(ps, "W2", [128, 32, D], BF16)
                xts = [sb(ps, f"mx{i}", [128, 8, 512]) for i in range(2)]
                m_t = sb(ps, "m_t", [128, 8, 512], BF16)
                h1 = sb(ps, "h1", [128, 16, 512], BF16)
                rr = [sb(ps, f"rr{i}", [128, 512]) for i in range(2)]
                sqb = [sb(ps, f"msq{i}", [128, 512], BF16) for i in range(2)] + \
                      [sb(ps, f"mnt{i}", [128, 512]) for i in range(2)]
                rtmp = sb(ps, "mrtmp", [128, 512])
                rstd = sb(ps, "mrstd", [128, 512])
                ssb = psb(ps, "mss")
                hps = [psb(ps, f"hps{i}") for i in range(2)]
                yps = [psb(ps, f"yps{i}") for i in range(3)]
                block = ps.enter_context(nc.Block())
                for k in range(KC):
                    P.dma("pool", f"m{k % 8}", W1[:, k, :], w1b[l, k * 128:(k + 1) * 128, :], reads=[("w1b", l, k)],
                          writes=[("W1", k)])
                for k4 in range(8):
                    P.dma("pool", f"m{k4 % 8}", W2[:, k4 * 4:(k4 + 1) * 4, :],
                          w2b[l, k4 * 512:(k4 + 1) * 512, :].rearrange("(k p) n -> p k n", p=128),
                          reads=[("w2b", l, k) for k in range(k4 * 4, k4 * 4 + 4)], writes=[("W2", k4 // 4)])
                if l == 0:
                    precast(1)
                tiles = TILES[:8] if final else TILES
                fi = 0
                yi = 0
                def p3_load(ti):
                    t0, TS, s = tiles[ti]
                    P.dma("sp", f"x{ti % 2}", xts[ti % 2][:, :, 0:TS], src_h(t0, TS, s), writes=[f"mx{ti % 2}"])
                p3_load(0)
                for ti, (t0, TS, s) in enumerate(tiles):
                    xt = xts[ti % 2]
                    xk = f"mx{ti % 2}"
                    if ti + 1 < len(tiles):
                        p3_load(ti + 1)
                    normmod(xt, xk, m_t, "m_t", TS, l, 1, s, sqb, ssb, rtmp, rstd)
                    for half in range(2):
                        for fcl in range(16):
                            fc = half * 16 + fcl
                            hp_ = hps[fi % 2]
                            hk_ = f"hps{fi % 2}"
                            for k in range(KC):
                                b.mm(hp_[:, 0:TS], W1[:, k, fc * 128:(fc + 1) * 128], m_t[:, k, 0:TS], k == 0,
                                     k == KC - 1, [("W1", k), ("m_t", k)], [hk_])
                            r_ = rr[fi % 2]
                            b.act(r_[:, 0:TS], hp_[:, 0:TS], AF.Relu, [hk_], [f"rr{fi % 2}"])
                            b.tt("dve", h1[:, fcl, 0:TS], r_[:, 0:TS], r_[:, 0:TS], ALU.mult, [f"rr{fi % 2}"],
                                 [("h1", fcl)])
                            fi += 1
                        for dc in range(KC):
                            yp = yps[yi % 3]
                            yk = f"yps{yi % 3}"
                            for fcl in range(16):
                                b.mm(yp[:, 0:TS], W2[:, half * 16 + fcl, dc * 128:(dc + 1) * 128], h1[:, fcl, 0:TS],
                                     fcl == 0, fcl == 15, [("W2", half), ("h1", fcl)], [yk])
                            b.stt(xt[:, dc, 0:TS], yp[:, 0:TS], modv(l, 5, dc, s), xt[:, dc, 0:TS], ALU.mult, ALU.add,
                                  [yk, xk, "mod"], [xk])
                            yi += 1
                    if not final:
                        P.dma("sp", "hst", fm(hS)[:, :, t0:t0 + TS], xt[:, :, 0:TS], reads=[xk], writes=["hS_dst"])
                    else:
                        for k in range(KC):
                            q = sqb[k % 2]
                            b.act(q[:, 0:TS], xt[:, k, 0:TS], AF.Square, [xk], [f"sq{k % 2}"])
                            b.mm(ssb[:, 0:TS], ones[:, :], q[:, 0:TS], k == 0, k == KC - 1, [f"sq{k % 2}", "ones"],
                                 ["ssb"])
                        b.act(rtmp[:, 0:TS], ssb[:, 0:TS], AF.Sqrt, ["ssb", "epsT"], ["rtmp"], scale=1.0 / D,
                              bias=epsT[:, 0:1])
                        b.recip(rstd[:, 0:TS], rtmp[:, 0:TS], ["rtmp"], ["rstd"])
                        for k in range(KC):
                            b.stt(xt[:, k, 0:TS], xt[:, k, 0:TS], gfs[:, k:k + 1], rstd[:, 0:TS], ALU.mult, ALU.mult,
                                  [xk, "rstd", "gfs"], [xk])
                        P.dma("sp", "hst", fm(outT)[:, :, t0:t0 + TS], xt[:, :, 0:TS], reads=[xk], writes=["outT"])
                P.barrier()
                P.emit(block)

        ch0 = [(c * 128, "q", c, True, True, True) for c in range(4)] + [(512, "k", 0, True, False, True)] + \
              [(640 + c * 128, "q", 4 + c, False, True, True) for c in range(4)] + [(1152, "k", 1, False, False, True)]
        ch1 = [(c * 128, "q", c, True, False, False) for c in range(8)] + \
              [(1024 + c * 128, "k", c, True, False, True) for c in range(8)]
        if max_phase >= 1:
            phase_p1(0, w_in0, 1536, ch0, 1280, 4, 64, src_x)
        if max_phase >= 2:
            phase_l0_attn()
        if max_phase >= 3:
            phase_mlp(0, False)
        if max_phase >= 4:
            phase_p1(1, w_in1, 3072, ch1, 2048, 8, 128, src_h)
        if max_phase >= 5:
            phase_l1_attn()
        if max_phase >= 6:
            phase_mlp(1, True)
        with ExitStack() as ps:
            block = ps.enter_context(nc.Block())
            P.barrier()
            P.emit(block)
    return nc


def _rope_tables_host():
    t = np.arange(NL)
    row = (t // 64).astype(np.float32)
    col = (t % 64).astype(np.float32)
    inv = (10000.0 ** (-np.arange(16, dtype=np.float32) / 16)).astype(np.float32)
    ar = row[:, None] * inv[None, :]
    ac = col[:, None] * inv[None, :]
    ang = np.concatenate([ar, ar, ac, ac], axis=-1).astype(np.float32)
    cos = np.cos(ang).astype(np.float32).T
    sin = np.sin(ang).astype(np.float32).T
    sign = np.where((np.arange(64) % 32) < 16, -1.0, 1.0).astype(np.float32)[:, None]
    sin_s = sin * sign
    cos2 = np.concatenate([cos, cos], 0)
    sin2 = np.concatenate([sin_s, sin_s], 0)
    return np.stack([cos2, sin2, cos2 * 0.125, sin2 * 0.125]).astype(np.float32)


def _consts_host():
    ident = np.eye(128, dtype=np.float32)
    perm = np.zeros((128, 128), np.float32)
    for m in range(128):
        partner = m + 16 if (m % 32) < 16 else m - 16
        perm[partner, m] = 1.0
    return np.concatenate([ident, perm], axis=1)


def _bias_tables(rpb):
    entries = []
    kl = np.arange(128)[:, None]
    ql = np.arange(128)[None, :]
    lower = np.where(kl >= ql, 0.0, NEG).astype(np.float32)
    upper = np.where(kl <= ql, 0.0, NEG).astype(np.float32)
    entries.append(np.repeat(lower[:, None, :], 4, axis=1))
    entries.append(np.repeat(upper[:, None, :], 4, axis=1))
    a_keys = []
    for i in range(32):
        l = []
        if i - 1 >= 0:
            l.append((i - 1, 0))
        l.append((i, None))
        if i + 1 < 32:
            l.append((i + 1, 1))
        a_keys.append(l)
    cache = {}
    b_keys = []
    for i in range(32):
        r = 2 * i + (np.arange(128) // 64)
        cq = np.arange(128) % 64
        rs = np.clip(r - 4, 0, 56)
        cs = np.clip(cq - 8, 0, 48)
        l = []
        for kb in range(32):
            krow = 2 * kb + (np.arange(128) // 64)
            kcol = np.arange(128) % 64
            valid = ((krow[:, None] >= rs[None, :]) & (krow[:, None] < rs[None, :] + 8) &
                     (kcol[:, None] >= cs[None, :]) & (kcol[:, None] < cs[None, :] + 16))
            if not valid.any():
                continue
            roff = np.clip(krow[:, None] - r[None, :] + 7, 0, 14)
            coff = np.clip(kcol[:, None] - cq[None, :], -15, 15) + 15
            key = (valid.tobytes(), np.where(valid, roff, 0).tobytes(), np.where(valid, coff, 0).tobytes())
            if key not in cache:
                cache[key] = len(entries)
                for kv in range(2):
                    g = rpb[kv * 4:(kv + 1) * 4][:, roff, coff]
                    m = np.where(valid[None], g, np.float32(NEG)).astype(np.float32)
                    entries.append(np.transpose(m, (1, 0, 2)))
            l.append((kb, cache[key]))
        b_keys.append(l)
    tab = np.stack(entries, axis=1).reshape(128, -1).astype(np.float32)
    return np.ascontiguousarray(tab), len(entries), a_keys, b_keys


def _bcast(v, n=128):
    return np.ascontiguousarray(np.broadcast_to(np.asarray(v, np.float32).reshape(1, -1), (n, np.asarray(v).size)))


def _key_structure():
    rpb0 = np.zeros((8, 15, 31), np.float32)
    _, n, a_keys, b_keys = _bias_tables(rpb0)
    return n, a_keys, b_keys


_PROG_CACHE = {}


def _prepare_inputs(x, c, ctx, c_ctx, ada_w, ada_b, norm1_g, norm2_g, even_w_in, even_w_out, a_sink, b_rpb,
                    odd_w_in, odd_w_out, lam_q1, lam_k1, lam_q2, lam_k2, subln_g, mlp_w1, mlp_w2, final_g):
    f = np.float32
    tab, n_bias, a_keys, b_keys = _bias_tables(np.asarray(b_rpb[0], f))
    win = np.asarray(even_w_in[0], f)
    aq, ak, av, bq, bk, bv = win[:, 0:512], win[:, 512:640], win[:, 640:768], win[:, 768:1280], win[:, 1280:1408], \
        win[:, 1408:1536]

    def qperm(q):
        cols = []
        for cch in range(4):
            cols.append(q[:, cch * 64:(cch + 1) * 64])
            cols.append(q[:, (4 + cch) * 64:(5 + cch) * 64])
        return np.concatenate(cols, axis=1)
    w_in0 = np.ascontiguousarray(np.concatenate([qperm(aq), ak, qperm(bq), bk, av, bv], axis=1))
    shared = {
        "ada_w": np.ascontiguousarray(ada_w, f),
        "ada_b": np.ascontiguousarray(np.repeat(np.asarray(ada_b, f).reshape(2, 48, 128).transpose(2, 0, 1)[..., None], 2,
                                                axis=-1).reshape(128, 192)),
        "g12": np.ascontiguousarray(np.stack([np.asarray(norm1_g, f), np.asarray(norm2_g, f)], axis=1)
                                    .reshape(2, 2, 8, 128).transpose(3, 0, 1, 2).reshape(128, 32)),
        "gfin": np.ascontiguousarray(np.asarray(final_g, f).reshape(8, 128).T),
        "w_in0": w_in0,
        "w_out0": np.ascontiguousarray(even_w_out[0], f),
        "w_in1": np.ascontiguousarray(odd_w_in[0], f),
        "w_out1": np.ascontiguousarray(odd_w_out[0], f),
        "w1": np.ascontiguousarray(mlp_w1, f),
        "w2": np.ascontiguousarray(mlp_w2, f),
        "sinkb": _bcast(a_sink[0]),
        "lamv": _bcast(np.concatenate([lam_q1[0], lam_k1[0], lam_q2[0], lam_k2[0]])),
        "subg": _bcast(subln_g[0]),
        "consts": _consts_host(),
        "rope": _rope_tables_host(),
        "biasT": tab,
    }
    in_maps = []
    for bb in range(8):
        m = dict(shared)
        m["xT"] = np.ascontiguousarray(np.asarray(x[bb], f).T)
        m["cT"] = np.ascontiguousarray(np.asarray(ctx[bb], f).T)
        cv = np.stack([np.asarray(c[bb], f).reshape(8, 128).T, np.asarray(c_ctx, f).reshape(8, 128).T], axis=-1)
        m["cvec"] = np.ascontiguousarray(cv.reshape(128, 16))
        in_maps.append(m)
    return in_maps, n_bias, a_keys, b_keys


def kernel(**inputs):
    in_maps, n_bias, a_keys, b_keys = _prepare_inputs(**inputs)
    key = ("main", n_bias)
    if key not in _PROG_CACHE:
        _PROG_CACHE[key] = build_program(n_bias, a_keys, b_keys)
    nc = _PROG_CACHE[key]
    res = run_bass_kernel_spmd(nc, in_maps, core_ids=list(range(8)))
    out = np.stack([np.ascontiguousarray(r["outT"].T) for r in res.results], axis=0)
    return out.astype(np.float32)
```
